# Optimizing a Trainium2 kernel written in Bass

```python
import jax, jax.numpy as jnp
from jax import lax
import numpy as np

D_MODEL = 2048
BATCH = 4
SEQ = 2048
DEPTH = 1

N_META = 16
CHUNK = 64
MIX_WIDTH = D_MODEL
GLA_WIDTH = MIX_WIDTH // 2
POOL_WIDTH = MIX_WIDTH - GLA_WIDTH
GLA_HEADS = 4
GLA_DV = GLA_WIDTH // GLA_HEADS
GLA_DK = GLA_DV // 2
GLA_KW = GLA_HEADS * GLA_DK
GATE_RANK = 16
GATE_TAU = 16.0
POOL_WINDOWS = (2, 4, 8, 16)
POOL_GROUPS = len(POOL_WINDOWS)
POOL_GC = POOL_WIDTH // POOL_GROUPS
D_FF = 4 * D_MODEL
EPS = 1e-6
SPLIT_POINTS = (
    GLA_KW,
    2 * GLA_KW,
    2 * GLA_KW + GLA_WIDTH,
    2 * GLA_KW + 2 * GLA_WIDTH,
    2 * GLA_KW + 2 * GLA_WIDTH + GATE_RANK,
)
D_IN = 2 * GLA_KW + 2 * GLA_WIDTH + GATE_RANK + POOL_WIDTH

kernel_name = "hybrid_gla_multiscale_pool_meta"


def rmsnorm(x, w):
    xf = x.astype(jnp.float32)
    y = xf * lax.rsqrt(jnp.mean(xf * xf, axis=-1, keepdims=True) + EPS)
    return (y * w.astype(jnp.float32)).astype(x.dtype)


def gla_chunked(q, k, v, logg):
    B, T, H, DK = q.shape
    DV = v.shape[-1]
    N = T // CHUNK

    def chunks(a):
        return a.reshape(B, N, CHUNK, H, a.shape[-1]).transpose(0, 3, 1, 2, 4)

    q, k, v, logg = chunks(q), chunks(k), chunks(v), chunks(logg)
    G = jnp.cumsum(logg, axis=3)
    G_last = G[:, :, :, -1:, :]
    q_dec = q * jnp.exp(G)
    k_inv = k * jnp.exp(-G)
    causal = jnp.tril(jnp.ones((CHUNK, CHUNK), dtype=bool))
    scores = jnp.einsum('bhncd,bhnsd->bhncs', q_dec, k_inv)
    scores = jnp.where(causal, scores, 0.0)
    o_intra = jnp.einsum('bhncs,bhnsv->bhncv', scores, v)
    k_to_end = k * jnp.exp(G_last - G)
    dS = jnp.einsum('bhncd,bhncv->bhndv', k_to_end, v)
    decay = jnp.exp(G_last[:, :, :, 0, :])

    def step(S, inp):
        dec, ds = inp
        return dec[..., None] * S + ds, S

    S0 = jnp.zeros((B, H, DK, DV), jnp.float32)
    _, S_prev = lax.scan(step, S0, (decay.transpose(2, 0, 1, 3), dS.transpose(2, 0, 1, 3, 4)))
    S_prev = S_prev.transpose(1, 2, 0, 3, 4)
    o_inter = jnp.einsum('bhncd,bhndv->bhncv', q_dec, S_prev)
    o = o_intra + o_inter
    return o.transpose(0, 2, 3, 1, 4).reshape(B, T, H, DV)


def multiscale_pool(pu, pool_w, pool_scale):
    B, L, _ = pu.shape
    xg = pu.astype(jnp.float32).reshape(B, L, POOL_GROUPS, POOL_GC)
    cs = jnp.pad(jnp.cumsum(xg, axis=1), ((0, 0), (1, 0), (0, 0), (0, 0)))
    t = jnp.arange(L)
    win = jnp.array(POOL_WINDOWS, dtype=jnp.int32)
    lo = jnp.maximum(t[:, None] + 1 - win[None, :], 0)
    g_idx = jnp.arange(POOL_GROUPS)[None, :]
    window_sum = cs[:, 1:] - cs[:, lo, g_idx]
    count = (t[:, None] + 1 - lo).astype(jnp.float32)[None, :, :, None]
    y = window_sum / count - xg
    y = jnp.einsum('blgc,gcd->blgd', y, pool_w.astype(jnp.float32))
    return y.reshape(B, L, POOL_WIDTH) * pool_scale.astype(jnp.float32)


def hybrid_layer(h, norm1_w, w_in, gate_w2, gate_b, gla_norm_w, pool_w, pool_scale,
                 w_out, norm2_w, mlp_w1, mlp_w2):
    B, L, _ = h.shape
    u = rmsnorm(h, norm1_w)
    proj = u @ w_in
    q, k, v, r, glr, pu = jnp.split(proj, SPLIT_POINTS, axis=-1)

    g_raw = (glr @ gate_w2 + gate_b).astype(jnp.float32)
    logg = jax.nn.log_sigmoid(g_raw) / GATE_TAU
    pad = (-N_META) % CHUNK

    def heads(a, d):
        a = a.astype(jnp.float32).reshape(B, L, GLA_HEADS, d)
        return jnp.pad(a, ((0, 0), (pad, 0), (0, 0), (0, 0)))

    o = gla_chunked(heads(q, GLA_DK) * (GLA_DK ** -0.5), heads(k, GLA_DK),
                    heads(v, GLA_DV), heads(logg, GLA_DK))[:, pad:]
    o = rmsnorm(o, gla_norm_w)
    gate_out = jax.nn.silu(r.astype(jnp.float32)).reshape(B, L, GLA_HEADS, GLA_DV)
    o_gla = (o * gate_out).reshape(B, L, GLA_WIDTH)

    o_pool = multiscale_pool(pu, pool_w, pool_scale)

    mixed = jnp.concatenate([o_gla, o_pool], axis=-1).astype(h.dtype)
    h = h + mixed @ w_out

    z = rmsnorm(h, norm2_w) @ mlp_w1
    h = h + jnp.square(jax.nn.relu(z)) @ mlp_w2
    return h


def setup_inputs(seed: int = 0) -> dict:
    key = jax.random.key(seed)
    ks = jax.random.split(key, 16)
    f32 = jnp.float32
    nrm = lambda k, shape, s: jax.random.normal(k, shape, f32) * s
    return {
        "x": nrm(ks[0], (BATCH, SEQ, D_MODEL), 1.0),
        "meta_tokens": nrm(ks[1], (N_META, D_MODEL), 1.0),
        "norm1_w": 1.0 + nrm(ks[2], (DEPTH, D_MODEL), 0.02),
        "w_in": nrm(ks[3], (DEPTH, D_MODEL, D_IN), D_MODEL ** -0.5),
        "gate_w2": nrm(ks[4], (DEPTH, GATE_RANK, GLA_KW), GATE_RANK ** -0.5),
        "gate_b": nrm(ks[5], (DEPTH, GLA_KW), 0.1),
        "gla_norm_w": 1.0 + nrm(ks[6], (DEPTH, GLA_DV), 0.02),
        "pool_w": nrm(ks[7], (DEPTH, POOL_GROUPS, POOL_GC, POOL_GC), POOL_GC ** -0.5),
        "pool_scale": 1.0 + nrm(ks[8], (DEPTH, POOL_WIDTH), 0.1),
        "w_out": nrm(ks[9], (DEPTH, MIX_WIDTH, D_MODEL), MIX_WIDTH ** -0.5),
        "norm2_w": 1.0 + nrm(ks[10], (DEPTH, D_MODEL), 0.02),
        "mlp_w1": nrm(ks[11], (DEPTH, D_MODEL, D_FF), D_MODEL ** -0.5),
        "mlp_w2": nrm(ks[12], (DEPTH, D_FF, D_MODEL), D_FF ** -0.5),
        "final_norm_w": 1.0 + nrm(ks[13], (D_MODEL,), 0.02),
    }


def reference(x, meta_tokens, norm1_w, w_in, gate_w2, gate_b, gla_norm_w, pool_w,
              pool_scale, w_out, norm2_w, mlp_w1, mlp_w2, final_norm_w):
    B = x.shape[0]
    meta = jnp.broadcast_to(meta_tokens[None].astype(x.dtype), (B, N_META, D_MODEL))
    h = jnp.concatenate([meta, x], axis=1)
    for i in range(DEPTH):
        h = hybrid_layer(h, norm1_w[i], w_in[i], gate_w2[i], gate_b[i], gla_norm_w[i],
                         pool_w[i], pool_scale[i], w_out[i], norm2_w[i], mlp_w1[i], mlp_w2[i])
    h = rmsnorm(h, final_norm_w)
    return h[:, N_META:]
```

```python
import numpy as np
import ml_dtypes
import concourse.bass as bass
import concourse.mybir as mybir
from concourse.bass_utils import run_bass_kernel_spmd
from contextlib import ExitStack

F32 = mybir.dt.float32
BF16 = mybir.dt.bfloat16
AF = mybir.ActivationFunctionType
ALU = mybir.AluOpType

ENGS = ("pe", "act", "dve", "pool", "sp")

D = 2048
NTM = 8
NTA = NTM + 1
TM, TA = NTM * 128, NTA * 128
DIN = 4112
DFF = 8192
EPS = 1e-6
POOL_W = (2, 4, 8, 16)


class Sched:
    def __init__(self, nc, es):
        self.nc, self.es = nc, es
        self.prog = {e: [] for e in ENGS}
        self.cnt = {}
        self.semh = {}
        self.waited = {}
        self.lastw = {}
        self.readers = {}

    def _sem(self, key):
        if key not in self.semh:
            self.semh[key] = self.es.enter_context(self.nc.semaphore("s_" + key))
            self.cnt[key] = 0
        return self.semh[key]

    def _wait(self, eng, tok):
        if tok is None:
            return
        key, val = tok
        if key == "E_pe" and eng == "pe":
            return
        if self.waited.get((eng, key), 0) >= val:
            return
        self.waited[(eng, key)] = val
        self.prog[eng].append(("wait", key, val))

    def _deps(self, eng, reads, writes):
        for k in reads:
            self._wait(eng, self.lastw.get(k))
        for k in writes:
            self._wait(eng, self.lastw.get(k))
            for sk, v in self.readers.get(k, {}).items():
                self._wait(eng, (sk, v))

    def _commit(self, tok, reads, writes):
        for k in reads:
            r = self.readers.setdefault(k, {})
            r[tok[0]] = max(r.get(tok[0], 0), tok[1])
        for k in writes:
            self.lastw[k] = tok
            self.readers[k] = {}

    def op(self, eng, fn, reads=(), writes=()):
        self._deps(eng, reads, writes)
        key = "E_" + eng
        self._sem(key)
        self.cnt[key] += 1
        tok = (key, self.cnt[key])
        self.prog[eng].append(("op", fn, key, 1))
        self._commit(tok, reads, writes)
        return tok

    def dma(self, eng, fn, semkey, reads=(), writes=()):
        self._deps(eng, reads, writes)
        self._sem(semkey)
        self.cnt[semkey] += 16
        tok = (semkey, self.cnt[semkey])
        self.prog[eng].append(("op", fn, semkey, 16))
        self._commit(tok, reads, writes)
        return tok

    def fence(self, old_keys, new_keys):
        acc = {}
        for k in old_keys:
            t = self.lastw.get(k)
            if t is not None:
                acc[t[0]] = max(acc.get(t[0], 0), t[1])
            for sk, v in self.readers.get(k, {}).items():
                acc[sk] = max(acc.get(sk, 0), v)
        for k in new_keys:
            r = self.readers.setdefault(k, {})
            for sk, v in acc.items():
                r[sk] = max(r.get(sk, 0), v)

    def wait(self, eng, tok):
        self._wait(eng, tok)

    def emit(self):
        nc = self.nc
        with nc.Block() as block:
            def replay(name, e):
                for it in self.prog[name]:
                    if it[0] == "wait":
                        e.wait_ge(self.semh[it[1]], it[2])
                    else:
                        ins = it[1](e)
                        ins.then_inc(self.semh[it[2]], it[3])

            @block.tensor
            def _(e):
                replay("pe", e)

            @block.scalar
            def _(e):
                replay("act", e)

            @block.vector
            def _(e):
                replay("dve", e)

            @block.gpsimd
            def _(e):
                replay("pool", e)

            @block.sync
            def _(e):
                replay("sp", e)


A_OFF = 0
B_OFF = 32768
C_OFF = B_OFF + 70912
D_OFF = C_OFF + 32768
E_OFF = D_OFF + 65536
E_SIZE = 10240
ARENA_BYTES = E_OFF + E_SIZE


class _Stop(Exception):
    pass


def build_program(stop=None, dumps=()):
    nc = bass.Bass("TRN2", target_bir_lowering=False)

    def din(name, shape, dt=F32):
        return nc.dram_tensor(name, list(shape), dt, kind="ExternalInput").ap()

    xin = din("xin", [TA, D])
    flags_d = din("flags", [128, 2])
    w_in = din("w_in", [D, DIN])
    gw2a_d = din("gw2a", [17, 512])
    gnw_d = din("gnw", [128, 2])
    psc_d = din("psc", [128, 8])
    pool_w = din("pool_w", [4, 256, 256])
    w_out = din("w_out", [D, D])
    nw1 = din("nw1", [D])
    nw2 = din("nw2", [D])
    nwf = din("nwf", [D])
    w1 = din("w1", [D, DFF])
    w2 = din("w2", [DFF, D])
    c_ident = din("c_ident", [128, 128], BF16)
    c_mask4 = din("c_mask4", [128, 512], BF16)
    out = nc.dram_tensor("out", [TM, D], F32, kind="ExternalOutput").ap()

    with ExitStack() as es:
        S = Sched(nc, es)

        def checkpoint(name, bufs):
            for bname, (ap, keys, shape, dt) in bufs.items():
                if bname in dumps:
                    d_ap = nc.dram_tensor("dbg_" + bname, list(shape), dt, kind="ExternalOutput").ap()
                    S.dma("sp", lambda e, d_ap=d_ap, ap=ap: e.dma_start(out=d_ap, in_=ap), "dbg_" + bname, reads=keys)
            if stop == name:
                raise _Stop()

        AR = es.enter_context(nc.sbuf_tensor("arena", [128, ARENA_BYTES // 4], F32))
        PS = es.enter_context(nc.psum_tensor("ps", [128, 4096], F32))

        def V(off, shape, dt=F32):
            n = int(np.prod(shape[1:]))
            isz = 4 if dt == F32 else 2
            assert off % 4 == 0 and (n * isz) % 4 == 0
            ap = AR[:, off // 4:(off + n * isz) // 4]
            if dt != F32:
                ap = ap.bitcast(dt)
            if len(shape) == 3:
                ap = ap.rearrange("p (a b) -> p a b", b=shape[2])
            elif len(shape) == 4:
                ap = ap.rearrange("p (a b c) -> p a b c", b=shape[2], c=shape[3])
            if shape[0] != 128:
                ap = ap[0:shape[0]]
            return ap

        def bank(b, n=512):
            return PS[:, b * 512:b * 512 + n]

        def bankbf(b):
            return PS[:, b * 512:(b + 1) * 512].bitcast(BF16)

        uT = V(A_OFF, [128, 16, TM], BF16)
        qT = V(B_OFF, [128, 4, TM], BF16)
        kT = V(B_OFF + 8192, [128, 4, TA], BF16)
        vv = V(B_OFF + 17408, [128, NTA, 1024], BF16)
        glrT = V(B_OFF + 35840, [17, TA], BF16)
        glrT_full = V(B_OFF + 35840, [128, TA], BF16)
        siluT = V(B_OFF + 38144, [128, 8, TM], BF16)
        yT = V(B_OFF + 54528, [128, 8, TM], BF16)
        us2 = [V(B_OFF + o_, [128, D], BF16) for o_ in (0, 4096, 16384)]
        nwb2 = V(B_OFF + 8192, [128, D], F32)
        aT = [V(B_OFF + 16384 + i * 8192, [128, 4, TM], BF16) for i in range(2)]
        w2r = [V(B_OFF + 32768 + i * 16384, [128, 4, D], BF16) for i in range(2)]
        rtmp = [V(B_OFF + 65536 + i * 2048, [128, 512], F32) for i in range(2)]
        ring = [V(C_OFF + i * 16384, [128, 16, 512], BF16) for i in range(2)]
        ostage = [V(C_OFF + i * 8192, [128, D], F32) for i in range(2)]
        NXS = 3
        xs = [V(D_OFF + 4096 + i * 8192, [128, D], F32) for i in range(NXS)]
        us = [V(D_OFF + 28672 + i * 4096, [128, D], BF16) for i in range(2)]
        nwb1 = V(D_OFF + 36864, [128, D], F32)
        put = [V(D_OFF + i * 4160, [128, 1040], F32) for i in range(3)]
        hh = V(D_OFF, [128, NTM, D], F32)

        def ga(par):
            o = D_OFF + 12480
            return dict(e=V(o + par * 2048, [128, 512]), L=V(o + 4096 + par * 2048, [128, 512]),
                        eN=V(o + 8192 + par * 2048, [128, 512]), eG=V(o + 12288, [128, 512]),
                        ki32=V(o + 14336, [128, 512]))
        Sf = V(D_OFF + 28864, [128, 1024])
        ki_t = V(D_OFF + 32960, [128, 512], BF16)
        keTt = V(D_OFF + 33984, [128, 512], BF16)
        qd_all = V(D_OFF + 35008, [128, NTM, 512], BF16)
        sc_all = V(D_OFF + 43200, [128, NTM, 512], BF16)
        ke_all = V(D_OFF + 51392, [128, NTA, 512], BF16)
        dec_all = V(D_OFF + 60608, [128, NTA + 1, 4], F32)
        poolw = V(D_OFF + 60768, [128, 4, 2, 256], BF16)
        wglr = V(D_OFF + 64864, [128, 16, 16], BF16)

        def sbt(par):
            o = D_OFF + par * 14336
            return dict(sq=V(o, [128, 1024]), rstd=V(o + 4096, [128, 512]), A=V(o + 6144, [128, 1024]),
                        osb=V(o + 10240, [128, 1024]))

        eo = E_OFF
        ident = V(eo, [128, 128], BF16); eo += 256
        mask4 = V(eo, [128, 512], BF16); eo += 1024
        ones = V(eo, [128, 128]); eo += 512
        gw2a = V(eo, [17, 512], BF16); eo += 1024
        gnw = V(eo, [128, 2]); eo += 8
        psc = V(eo, [128, 8]); eo += 32
        pscw = V(eo, [128, 8]); eo += 32
        flags = V(eo, [128, 2]); eo += 8
        st = V(eo, [128, 64]); eo += 256
        rs = V(eo, [128, 64]); eo += 256
        Sb = V(eo, [128, 1024], BF16); eo += 2048
        uTtail = V(eo, [128, 16, 16], BF16); eo += 512
        uTh = V(eo, [128, 16, 128], BF16); eo += 4096
        assert eo <= E_OFF + E_SIZE, eo - E_OFF
        assert D_OFF + 64864 + 512 <= E_OFF

        s_bounce = nc.dram_tensor("s_bounce", [128, 1024], F32)
        s_gath = nc.dram_tensor("s_gath", [256, 1024], F32)

        w_in_v = w_in.rearrange("(kc p) n -> p kc n", p=128)
        w_out_v = w_out.rearrange("(kc p) n -> p kc n", p=128)
        w1_v = w1.rearrange("(kc p) n -> p kc n", p=128)
        w2_v = w2.rearrange("(kc p) n -> p kc n", p=128)

        def sp_load(dst, src, key):
            return S.dma("sp", lambda e: e.dma_start(out=dst, in_=src), "c_" + str(key), writes=[key])

        def copy_op(i, dst, src, reads, writes):
            if i % 2 == 0:
                return S.op("act", lambda e: e.copy(out=dst, in_=src), reads=reads, writes=writes)
            return S.op("dve", lambda e: e.tensor_copy(out=dst, in_=src), reads=reads, writes=writes)

        def mm_group(mms, reads, writes):
            def fn(e):
                last = None
                for (o, l, r, a, b) in mms:
                    last = e.matmul(o, l, r, start=a, stop=b)
                return last
            return S.op("pe", fn, reads=reads, writes=writes)

        def tr_group(trs, reads, writes):
            def fn(e):
                last = None
                for (o, i) in trs:
                    last = e.transpose(out=o, in_=i, identity=ident)
                return last
            return S.op("pe", fn, reads=reads + ["ident"], writes=writes)

        try:
            sp_load(nwb1, nw1.partition_broadcast(128), "nwb1")
            S.op("dve", lambda e: e.memset(ones, 1.0), writes=["ones"])
            S.op("dve", lambda e: e.memset(st, 0.0), writes=["st"])
            ring_n = [0]

            def ring_load(src_ap, ncols=512):
                if ring_n[0] == 0:
                    S.wait("pool", x_toks[3])
                s = ring_n[0] % 2
                ring_n[0] += 1
                dst = ring[s] if ncols == 512 else ring[s][:, :, 0:ncols]
                S.dma("pool", lambda e: e.dma_start(out=dst, in_=src_ap), f"ring{s}", writes=[("ring", s)])
                return s

            stat_col = [0]
            x_toks = {}

            def norm_a(src, src_keys, nwb, nwb_key, stg, stg_key, apply=True):
                c = stat_col[0]
                stat_col[0] += 1
                S.op("act", lambda e: e.activation(out=stg, in_=src, func=AF.Square, scale=float(D ** -0.5),
                                                   accum_out=st[:, c:c + 1]),
                     reads=src_keys + ["st"], writes=[stg_key, ("st", c)])
                S.op("act", lambda e: e.activation(out=rs[:, c:c + 1], in_=st[:, c:c + 1], func=AF.Ln, bias=EPS),
                     reads=[("st", c)], writes=[("rs", c)])
                S.op("act", lambda e: e.activation(out=rs[:, c:c + 1], in_=rs[:, c:c + 1], func=AF.Exp, scale=-0.5),
                     reads=[("rs", c)], writes=[("rs", c)])
                if apply:
                    S.op("dve", lambda e: e.scalar_tensor_tensor(out=stg, in0=src, scalar=rs[:, c:c + 1], in1=nwb,
                                                                 op0=ALU.mult, op1=ALU.mult),
                         reads=src_keys + [("rs", c), nwb_key], writes=[stg_key])
                return c

            def norm_b(stg, stg_key, dstT, dcol, dkeys, bset):
                for half in range(2):
                    b = bset[half]
                    tr_group([(bankbf(b)[:, i * 128:(i + 1) * 128], stg[:, (half * 8 + i) * 128:(half * 8 + i + 1) * 128])
                              for i in range(8)], reads=[stg_key], writes=[("ps", b)])
                    copy_op(half, dstT[:, half * 8:half * 8 + 8, dcol:dcol + 128],
                            bankbf(b).rearrange("p (a b) -> p a b", b=128),
                            reads=[("ps", b)], writes=[dkeys[half]])

            def p0_a(t):
                sl = t % 2
                xl = t % NXS
                x_toks[t] = S.dma("sp", lambda e: e.dma_start(out=xs[xl], in_=xin[t * 128:(t + 1) * 128, :]),
                                  f"xs{xl}", writes=[("xs", xl)])
                norm_a(xs[xl], [("xs", xl)], nwb1, "nwb1", us[sl], ("us", sl))

            def p0_b(t):
                sl = t % 2
                bset = (0, 1) if t % 2 == 0 else (2, 3)
                if t == 0:
                    norm_b(us[sl], ("us", sl), uTh, 0, [("uTh", 0), ("uTh", 1)], bset)
                else:
                    norm_b(us[sl], ("us", sl), uT, (t - 1) * 128, [("uT", t - 1, 0), ("uT", t - 1, 1)], bset)

            XS_KEYS = [("xs", i) for i in range(NXS)] + [("us", 0), ("us", 1)]
            UTH_KEYS = [("uTh", 0), ("uTh", 1)]
            UT_KEYS = [("uT", j, h) for j in range(NTM) for h in range(2)]
            p0_a(0)
            sp_load(ident, c_ident, "ident")
            S.dma("pool", lambda e: e.dma_start(out=gw2a, in_=gw2a_d), "gw2a", writes=["gw2a"])
            S.dma("pool", lambda e: e.dma_start(out=wglr, in_=w_in_v[:, :, 3072:3088]), "wglr", writes=["wglr"])
            for t in range(NTA):
                if t + 1 < NTA:
                    p0_a(t + 1)
                p0_b(t)
            sp_load(mask4, c_mask4, "mask4")
            sp_load(gnw, gnw_d, "gnw")
            sp_load(psc, psc_d, "psc")
            sp_load(flags, flags_d, "flags")
            for g in range(4):
                S.op("dve", lambda e, g=g: e.tensor_scalar(out=pscw[:, 2 * g:2 * g + 2], in0=psc[:, 2 * g:2 * g + 2],
                                                           scalar1=1.0 / POOL_W[g], scalar2=None, op0=ALU.mult),
                     reads=["psc"], writes=[("pscw", g)])
            S.op("dve", lambda e: e.tensor_copy(out=uTtail, in_=uTh[:, :, 112:128]), reads=UTH_KEYS, writes=["uTtail"])
            checkpoint("p0", {"uT": (uT, UT_KEYS, [128, 16, TM], BF16)})

            chunk_ctr = [0]

            def next_bset():
                chunk_ctr[0] += 1
                return (0, 1, 2) if chunk_ctr[0] % 2 else (3, 4, 5)

            tm_ctr = [0]

            def next_tm_bank():
                tm_ctr[0] += 1
                return 3 + tm_ctr[0] % 5

            ev = [0]

            def fm_mms(w_ap_fn, bs_, with_head, tail=False):
                mms = []
                for kc in range(16):
                    for tb in range(2):
                        mms.append((bank(bs_[tb]), w_ap_fn(kc), uT[:, kc, tb * 512:(tb + 1) * 512], kc == 0, kc == 15))
                    if with_head:
                        mms.append((bank(bs_[2], 128), w_ap_fn(kc), uTh[:, kc, :], kc == 0, kc == 15))
                    if tail:
                        mms.append((bank(bs_[2], 16), w_ap_fn(kc), uTtail[:, kc, :], kc == 0, kc == 15))
                return mms

            def gates(T, K, glr_ap, glr_key, bg, dec_ap, dkey, want_eG, head):
                mm_group([(bank(bg)[:, h * 128:(h + 1) * 128], gw2a[0:17, h * 128:(h + 1) * 128], glr_ap, True, True)
                          for h in range(4)], reads=["gw2a", glr_key], writes=[("ps", bg)])
                S.op("act", lambda e: e.activation(out=T["e"], in_=bank(bg), func=AF.Exp, scale=-1.0),
                     reads=[("ps", bg)], writes=[K + "e"])
                S.op("act", lambda e: e.activation(out=T["e"], in_=T["e"], func=AF.Ln, bias=1.0),
                     reads=[K + "e"], writes=[K + "e"])
                if head:
                    S.op("dve", lambda e: e.tensor_scalar(out=T["e"], in0=T["e"], scalar1=flags[:, 0:1], scalar2=None,
                                                          op0=ALU.mult), reads=[K + "e", "flags"], writes=[K + "e"])
                for h in range(4):
                    S.op("dve", lambda e, h=h: e.tensor_tensor_scan(out=T["L"][:, h * 128:(h + 1) * 128],
                                                                     data0=T["e"][:, h * 128:(h + 1) * 128],
                                                                     data1=T["e"][:, h * 128:(h + 1) * 128],
                                                                     initial=0.0, op0=ALU.add, op1=ALU.bypass),
                         reads=[K + "e"], writes=[K + "L%d" % h])
                Lk = [K + "L%d" % h for h in range(4)]
                S.op("act", lambda e: e.activation(out=T["eN"], in_=T["L"], func=AF.Exp, scale=1.0 / 16),
                     reads=Lk, writes=[K + "eN"])
                S.op("act", lambda e: e.activation(out=dec_ap, in_=T["L"].rearrange("p (h t) -> p h t", t=128)[:, :, 127],
                                                   func=AF.Exp, scale=-1.0 / 16),
                     reads=Lk, writes=[dkey])
                if want_eG:
                    S.op("act", lambda e: e.activation(out=T["eG"], in_=T["L"], func=AF.Exp, scale=-1.0 / 16),
                         reads=Lk, writes=["ga_eG"])

            def ke_transpose(src, src_keys, bk, dst, dkey):
                tr_group([(bankbf(bk)[:, h * 128:(h + 1) * 128], src[:, h * 128:(h + 1) * 128]) for h in range(4)],
                         reads=src_keys, writes=[("ps", bk)])
                S.op("act", lambda e: e.copy(out=dst, in_=bankbf(bk)[:, 0:512]), reads=[("ps", bk)], writes=[dkey])

            def state_update(t, bd):
                mm_group([(PS[:, bd * 512 + h * 256:bd * 512 + (h + 1) * 256], ke_all[:, t, h * 128:(h + 1) * 128],
                           vv[:, t, h * 256:(h + 1) * 256], True, True) for h in range(4)],
                         reads=[("ke", t), ("vv", t, 0), ("vv", t, 1)], writes=[("ps", bd), ("ps", bd + 1)])
                for h in range(4):
                    S.op("dve", lambda e, h=h: e.scalar_tensor_tensor(
                        out=Sf[:, h * 256:(h + 1) * 256], in0=Sf[:, h * 256:(h + 1) * 256], scalar=dec_all[:, t, h:h + 1],
                        in1=PS[:, bd * 512 + h * 256:bd * 512 + (h + 1) * 256], op0=ALU.mult, op1=ALU.add),
                        reads=["Sf", ("dec", t), ("ps", bd), ("ps", bd + 1)], writes=["Sf"])

            GAKEYS = ["ki_t", "keTt", "ga_eG", "ga_ki32", "Sf", ("dec", NTA)] + \
                     [(n, t) for n in ("qd", "sc", "ke", "dec") for t in range(NTA)]
            for par in range(2):
                GAKEYS += ["ga%d_" % par + n for n in ["e", "L0", "L1", "L2", "L3", "eN"]]
            S.fence(XS_KEYS + ["nwb1"], GAKEYS)
            def stage_a1(t, bg):
                par = t % 2
                T = ga(par)
                K = "ga%d_" % par
                kcols = slice(t * 128, (t + 1) * 128)
                head = (t == 0)
                gates(T, K, glrT[0:17, kcols], "glrT", bg, dec_all[:, t, :], ("dec", t), not head, head)
                eN3 = T["eN"].rearrange("p (h t) -> p h t", t=128)
                if not head:
                    j = t - 1
                    qcols = slice(j * 128, (j + 1) * 128)
                    eG3 = T["eG"].rearrange("p (h t) -> p h t", t=128)
                    S.op("dve", lambda e: e.scalar_tensor_tensor(
                        out=qd_all[:, j, :].rearrange("p (h t) -> p h t", t=128), in0=qT[:, :, qcols],
                        scalar=float(128 ** -0.5), in1=eG3, op0=ALU.mult, op1=ALU.mult),
                        reads=[("qT", m, j // 4) for m in range(4)] + ["ga_eG"], writes=[("qd", t)])
                S.op("dve", lambda e: e.tensor_tensor(
                    out=T["ki32"].rearrange("p (h t) -> p h t", t=128), in0=kT[:, :, kcols], in1=eN3, op=ALU.mult),
                    reads=[("kT", m) for m in range(4)] + [K + "eN"], writes=["ga_ki32"])
                dsc, dsk = dec_all[:, t, :], ("dec", t)
                if head:
                    S.op("dve", lambda e: e.tensor_scalar(out=dec_all[:, NTA, :], in0=dec_all[:, 0, :], scalar1=flags[:, 0:1],
                                                          scalar2=None, op0=ALU.mult),
                         reads=[("dec", 0), "flags"], writes=[("dec", NTA)])
                    dsc, dsk = dec_all[:, NTA, :], ("dec", NTA)
                else:
                    S.op("act", lambda e: e.copy(out=ki_t, in_=T["ki32"]), reads=["ga_ki32"], writes=["ki_t"])
                for h in range(4):
                    S.op("dve", lambda e, h=h: e.tensor_scalar(
                        out=keTt[:, h * 128:(h + 1) * 128], in0=T["ki32"][:, h * 128:(h + 1) * 128],
                        scalar1=dsc[:, h:h + 1], scalar2=None, op0=ALU.mult),
                        reads=["ga_ki32", dsk], writes=["keTt"])

            def stage_a2(t, bk, bsx):
                head = (t == 0)
                if not head:
                    j = t - 1
                    mm_group([(bank(bsx)[:, h * 128:(h + 1) * 128], ki_t[:, h * 128:(h + 1) * 128],
                               qd_all[:, j, h * 128:(h + 1) * 128], True, True) for h in range(4)],
                             reads=["ki_t", ("qd", t)], writes=[("ps", bsx)])
                    S.op("dve", lambda e: e.tensor_tensor(out=sc_all[:, j, :], in0=bank(bsx), in1=mask4, op=ALU.mult),
                         reads=[("ps", bsx), "mask4"], writes=[("sc", t)])
                ke_transpose(keTt, ["keTt"], bk, ke_all[:, t, :], ("ke", t))

            sa_state = {"next": 0, "pend": None}

            def sa_hook(bg, bsx, bk=None):
                if sa_state["pend"] is not None:
                    stage_a2(sa_state["pend"], bg if bk is None else bk, bsx)
                    sa_state["pend"] = None
                if sa_state["next"] < NTA:
                    stage_a1(sa_state["next"], bg)
                    sa_state["pend"] = sa_state["next"]
                    sa_state["next"] += 1

            S.op("dve", lambda e: e.memset(glrT_full, 1.0), writes=["glrT"])
            bs_ = next_bset()
            mms = []
            for kc in range(16):
                for tb in range(2):
                    mms.append((PS[0:16, bs_[tb] * 512:(bs_[tb] + 1) * 512], wglr[:, kc, :],
                                uT[:, kc, tb * 512:(tb + 1) * 512], kc == 0, kc == 15))
                mms.append((PS[0:16, bs_[2] * 512:bs_[2] * 512 + 128], wglr[:, kc, :], uTh[:, kc, :], kc == 0, kc == 15))
            mm_group(mms, reads=["wglr"] + UT_KEYS + UTH_KEYS, writes=[("ps", b) for b in bs_])
            for tb in range(2):
                ev[0] += 1
                copy_op(ev[0], glrT[0:16, 128 + tb * 512:128 + (tb + 1) * 512], PS[0:16, bs_[tb] * 512:(bs_[tb] + 1) * 512],
                        reads=[("ps", bs_[tb])], writes=["glrT"])
            ev[0] += 1
            copy_op(ev[0], glrT[0:16, 0:128], PS[0:16, bs_[2] * 512:bs_[2] * 512 + 128], reads=[("ps", bs_[2])], writes=["glrT"])

            s = ring_load(w_in_v[:, :, 512:1024])
            for m in range(4):
                bs_ = next_bset()
                mm_group(fm_mms(lambda kc, m=m, s=s: ring[s][:, kc, m * 128:(m + 1) * 128], bs_, True),
                         reads=[("ring", s)] + UT_KEYS + UTH_KEYS, writes=[("ps", b) for b in bs_])
                for tb in range(2):
                    S.op("act" if tb == 0 else "dve",
                         (lambda e, m=m, tb=tb, b=bs_[tb]: e.copy(out=kT[:, m, 128 + tb * 512:128 + (tb + 1) * 512], in_=bank(b)))
                         if tb == 0 else
                         (lambda e, m=m, tb=tb, b=bs_[tb]: e.tensor_copy(out=kT[:, m, 128 + tb * 512:128 + (tb + 1) * 512], in_=bank(b))),
                         reads=[("ps", bs_[tb])], writes=[("kT", m)])
                S.op("act", lambda e, m=m, b=bs_[2]: e.copy(out=kT[:, m, 0:128], in_=bank(b, 128)),
                     reads=[("ps", bs_[2])], writes=[("kT", m)])

            def fm_block(col0, evac, hook=None, slot=None):
                s = ring_load(w_in_v[:, :, col0:col0 + 512]) if slot is None else slot
                for m in range(4):
                    bs_ = next_bset()
                    mm_group(fm_mms(lambda kc, m=m, s=s: ring[s][:, kc, m * 128:(m + 1) * 128], bs_, False),
                             reads=[("ring", s)] + UT_KEYS, writes=[("ps", bs_[0]), ("ps", bs_[1])])
                    evac(m, bs_)
                    if hook is not None:
                        hook(m)

            def ev_copy(dst, name):
                def f(m, bs_):
                    for tb in range(2):
                        ev[0] += 1
                        copy_op(ev[0], dst[:, m, tb * 512:(tb + 1) * 512], bank(bs_[tb]),
                                reads=[("ps", bs_[tb])], writes=[(name, m, tb)])
                return f

            fm_block(0, ev_copy(qT, "qT"))
            for g in range(4):
                S.dma("pool", lambda e, g=g: e.dma_start(out=poolw[:, g], in_=pool_w[g].rearrange("(cc p) d -> p cc d", p=128)),
                      "poolw%d" % g, writes=[("poolw", g)])
            MAINB2 = [("siluT", c, tb) for c in range(8) for tb in range(2)] + [("yT", c) for c in range(8)]
            PUT_KEYS = [("put", i) for i in range(3)]
            for pb in range(2):
                s = ring_load(w_in_v[:, :, 3088 + pb * 512:3088 + (pb + 1) * 512])
                for m in range(4):
                    c = pb * 4 + m
                    g = c // 2
                    w = POOL_W[g]
                    bs_ = next_bset()
                    mm_group(fm_mms(lambda kc, m=m, s=s: ring[s][:, kc, m * 128:(m + 1) * 128], bs_, False, tail=True),
                             reads=[("ring", s), "uTtail"] + UT_KEYS, writes=[("ps", b) for b in bs_])
                    S.op("act", lambda e, b=bs_[2]: e.copy(out=put[0][:, 0:16], in_=bank(b, 16)),
                         reads=[("ps", bs_[2])], writes=[("put", 0)])
                    S.op("act", lambda e, b=bs_[0]: e.copy(out=put[0][:, 16:528], in_=bank(b)),
                         reads=[("ps", bs_[0])], writes=[("put", 0)])
                    S.op("act", lambda e, b=bs_[1]: e.copy(out=put[0][:, 528:1040], in_=bank(b)),
                         reads=[("ps", bs_[1])], writes=[("put", 0)])
                    if w == 2:
                        S.op("dve", lambda e, c=c: e.tensor_tensor(out=yT[:, c, :], in0=put[0][:, 15:1039], in1=put[0][:, 16:1040],
                                                                   op=ALU.subtract),
                             reads=[("put", 0)], writes=[("yT", c)])
                    else:
                        src, cur_i, sh, lo = put[0], 1, 1, 1
                        srckey = ("put", 0)
                        while sh < w:
                            dst = put[cur_i]
                            S.op("dve", lambda e, src=src, dst=dst, sh=sh, lo=lo: e.tensor_tensor(
                                out=dst[:, lo:1040], in0=src[:, lo:1040], in1=src[:, lo - sh:1040 - sh], op=ALU.add),
                                reads=[srckey], writes=[("put", cur_i)])
                            src, srckey = dst, ("put", cur_i)
                            cur_i = 3 - cur_i
                            sh *= 2
                            lo = 2 * sh - 1
                        S.op("dve", lambda e, c=c, src=src, w=w: e.scalar_tensor_tensor(
                            out=yT[:, c, :], in0=put[0][:, 16:1040], scalar=-float(w), in1=src[:, 16:1040],
                            op0=ALU.mult, op1=ALU.add),
                            reads=[("put", 0), srckey], writes=[("yT", c)])
                    if c % 2 == 1:
                        sa_hook(6, 7)

            for half in range(2):
                s = ring_load(w_in_v[:, :, 1024 + half * 512:1024 + (half + 1) * 512])
                for t in range(NTA):
                    b = next_tm_bank()
                    lh = (lambda kc: uTh[:, kc, :]) if t == 0 else (lambda kc, t=t: uT[:, kc, (t - 1) * 128:t * 128])
                    rk = UTH_KEYS if t == 0 else [("uT", t - 1, 0), ("uT", t - 1, 1)]
                    mm_group([(bank(b), lh(kc), ring[s][:, kc, :], kc == 0, kc == 15) for kc in range(16)],
                             reads=[("ring", s)] + rk, writes=[("ps", b)])
                    ev[0] += 1
                    copy_op(ev[0], vv[:, t, half * 512:(half + 1) * 512], bank(b),
                            reads=[("ps", b)], writes=[("vv", t, half)])
                    if t in (2, 5, 8):
                        sa_hook(0, 1, 2)
                    if half == 1 and t >= 2:
                        if t == 2:
                            S.op("dve", lambda e: e.memset(Sf, 0.0), writes=["Sf"])
                        state_update(t - 2, 0)

            while sa_state["pend"] is not None or sa_state["next"] < NTA:
                sa_hook(0, 1, 2)
            r_slots = [ring_load(w_in_v[:, :, 2048 + rb * 512:2048 + (rb + 1) * 512]) for rb in range(2)]
            state_update(NTA - 2, 0)
            state_update(NTA - 1, 0)
            S.dma("sp", lambda e: e.dma_start(out=s_bounce.ap(), in_=Sf), "sbo", reads=["Sf"], writes=["s_bounce"])
            S._deps("pool", ["s_bounce"], ["s_gath"])
            S._sem("cc")
            S.cnt["cc"] += 1
            cc_tok = ("cc", S.cnt["cc"])
            S.prog["pool"].append(("op", lambda e: e.collective_compute(
                "AllGather", ALU.bypass, replica_groups=[[0, 1], [2, 3], [4, 5], [6, 7]],
                ins=[s_bounce.ap()], outs=[s_gath.ap()]), "cc", 1))
            S._commit(cc_tok, ["s_bounce"], ["s_gath"])
            S.dma("sp", lambda e: e.dma_start(out=Sf, in_=s_gath.ap()[0:128, :]), "sgi", reads=["s_gath"], writes=["Sf"])
            S.op("dve", lambda e: e.tensor_scalar(out=Sf, in0=Sf, scalar1=flags[:, 1:2], scalar2=None, op0=ALU.mult),
                 reads=["Sf", "flags"], writes=["Sf"])
            state_update(0, 6)
            S.op("dve", lambda e: e.tensor_copy(out=Sb, in_=Sf), reads=["Sf"], writes=["Sb"])

            for rb in range(2):
                def ev_silu(m, bs_, rb=rb):
                    for tb in range(2):
                        S.op("act", lambda e, m=m, tb=tb, b=bs_[tb]: e.activation(
                            out=siluT[:, rb * 4 + m, tb * 512:(tb + 1) * 512], in_=bank(b), func=AF.Silu),
                            reads=[("ps", bs_[tb])], writes=[("siluT", rb * 4 + m, tb)])
                fm_block(2048 + rb * 512, ev_silu, slot=r_slots[rb])

            checkpoint("p3", {"qT": (qT, [("qT", m, tb) for m in range(4) for tb in range(2)], [128, 4, TM], BF16),
                              "siluT": (siluT, [("siluT", c, tb) for c in range(8) for tb in range(2)], [128, 8, TM], BF16),
                              "yT": (yT, [("yT", c) for c in range(8)], [128, 8, TM], BF16)})

            MIX_KEYS = [("mix", c, tb) for c in range(16) for tb in range(2)] + [("mixg", j) for j in range(NTM)]
            S.fence(UT_KEYS, MIX_KEYS)
            mixedT = uT
            for g in range(4):
                for dc in range(2):
                    bs_ = (0, 1) if (g * 2 + dc) % 2 == 0 else (2, 3)
                    c = g * 2 + dc
                    for tb in range(2):
                        mm_group([(bank(bs_[tb]), poolw[:, g, cc, dc * 128:(dc + 1) * 128],
                                   yT[:, g * 2 + cc, tb * 512:(tb + 1) * 512], cc == 0, cc == 1) for cc in range(2)],
                                 reads=[("poolw", g), ("yT", g * 2), ("yT", g * 2 + 1)], writes=[("ps", bs_[tb])])
                        S.op("dve", lambda e, c=c, tb=tb, b=bs_[tb]: e.tensor_scalar(
                            out=mixedT[:, 8 + c, tb * 512:(tb + 1) * 512], in0=bank(b), scalar1=pscw[:, c:c + 1],
                            scalar2=None, op0=ALU.mult),
                            reads=[("ps", bs_[tb]), ("pscw", g)], writes=[("mix", 8 + c, tb)])

            SBKEYS = []
            for par in range(2):
                SBKEYS += ["sb%d_" % par + n for n in ["sq0", "sq1", "rstd", "A0", "A1", "osb0", "osb1"]]
            S.fence(PUT_KEYS + UTH_KEYS + [k_ for k_ in GAKEYS if k_ != "Sf" and not (isinstance(k_, tuple) and k_[0] in ("qd", "sc", "ke", "dec", "poolw"))] + XS_KEYS, SBKEYS)

            def p4_main(j):
                t = j + 1
                par = j % 2
                bo = 0 if par == 0 else 2
                mms = []
                for h in range(4):
                    for vc in range(2):
                        o_ap = PS[:, bo * 512 + (h * 2 + vc) * 128:bo * 512 + (h * 2 + vc + 1) * 128]
                        mms.append((o_ap, vv[:, t, h * 256 + vc * 128:h * 256 + (vc + 1) * 128],
                                    sc_all[:, j, h * 128:(h + 1) * 128], True, False))
                        mms.append((o_ap, Sb[:, h * 256 + vc * 128:h * 256 + (vc + 1) * 128],
                                    qd_all[:, j, h * 128:(h + 1) * 128], False, True))
                mm_group(mms, reads=[("sc", t), ("qd", t), "Sb", ("vv", t, 0), ("vv", t, 1)],
                         writes=[("ps", bo), ("ps", bo + 1)])
                if j < NTM - 1:
                    state_update(t, 4)
                    S.op("dve", lambda e: e.tensor_copy(out=Sb, in_=Sf), reads=["Sf"], writes=["Sb"])
                T = sbt(par)
                K = "sb%d_" % par
                for hb in range(2):
                    S.op("act", lambda e, hb=hb: e.activation(out=T["sq"][:, hb * 512:(hb + 1) * 512],
                                                               in_=bank(bo + hb), func=AF.Square),
                         reads=[("ps", bo + hb)], writes=[K + "sq%d" % hb])
                    S.op("act", lambda e, hb=hb: e.copy(out=T["osb"][:, hb * 512:(hb + 1) * 512], in_=bank(bo + hb)),
                         reads=[("ps", bo + hb)], writes=[K + "osb%d" % hb])

            def p4_post(j):
                par = j % 2
                bg = 6 + par
                T = sbt(par)
                K = "sb%d_" % par
                mms = []
                for h in range(4):
                    for vc in range(2):
                        mms.append((bank(bg)[:, h * 128:(h + 1) * 128], ones,
                                    T["sq"][:, (h * 2 + vc) * 128:(h * 2 + vc + 1) * 128], vc == 0, vc == 1))
                mm_group(mms, reads=["ones", K + "sq0", K + "sq1"], writes=[("ps", bg)])
                S.op("act", lambda e: e.activation(out=T["rstd"], in_=bank(bg), func=AF.Ln, scale=1.0 / 256, bias=EPS),
                     reads=[("ps", bg)], writes=[K + "rstd"])
                S.op("act", lambda e: e.activation(out=T["rstd"], in_=T["rstd"], func=AF.Exp, scale=-0.5),
                     reads=[K + "rstd"], writes=[K + "rstd"])

            def p4_fin(j):
                par = j % 2
                T = sbt(par)
                K = "sb%d_" % par
                cols = slice(j * 128, (j + 1) * 128)
                r3 = T["rstd"].rearrange("p (h t) -> p h t", t=128)
                A4 = T["A"].rearrange("p (h v t) -> p h v t", v=2, t=128)
                o4 = T["osb"].rearrange("p (h v t) -> p h v t", v=2, t=128)
                silu4 = siluT[:, :, cols].rearrange("p (h v) t -> p h v t", v=2)
                mix4 = mixedT[:, 0:8, cols].rearrange("p (h v) t -> p h v t", v=2)
                for vc in range(2):
                    S.op("dve", lambda e, vc=vc: e.tensor_tensor(
                        out=A4[:, :, vc, :], in0=silu4[:, :, vc, :], in1=r3, op=ALU.mult),
                        reads=[("siluT", c, j // 4) for c in range(8)] + [K + "rstd"], writes=[K + "A%d" % vc])
                for vc in range(2):
                    S.op("dve", lambda e, vc=vc: e.scalar_tensor_tensor(
                        out=mix4[:, :, vc, :], in0=o4[:, :, vc, :], scalar=gnw[:, vc:vc + 1], in1=A4[:, :, vc, :],
                        op0=ALU.mult, op1=ALU.mult),
                        reads=[K + "osb0", K + "osb1", "gnw", K + "A%d" % vc], writes=[("mixg", j)])

            for j in range(NTM):
                p4_main(j)
                if j >= 1:
                    p4_post(j - 1)
                    p4_fin(j - 1)
            p4_post(NTM - 1)
            p4_fin(NTM - 1)

            checkpoint("p4", {"mixedT": (uT, MIX_KEYS, [128, 16, TM], BF16)})
            H_KEYS = [("h", j, fc) for j in range(NTM) for fc in range(4)]
            S.fence(GAKEYS + SBKEYS + PUT_KEYS + UTH_KEYS + [("poolw", g) for g in range(4)] + ["wglr", "nwb1"] + XS_KEYS, H_KEYS)
            for j in range(NTM):
                S.dma("sp", lambda e, j=j: e.dma_start(out=hh[:, j, :], in_=xin[128 + j * 128:128 + (j + 1) * 128, :]),
                      f"hld{j}", writes=[("h", j, fc) for fc in range(4)])
            N2_KEYS = [("n2T", j, h) for j in range(NTM) for h in range(2)]
            MAINB1 = [("qT", m, tb) for m in range(4) for tb in range(2)] + [("kT", m) for m in range(4)] + \
                     [("vv", t, h) for t in range(NTA) for h in range(2)] + ["glrT"]
            B_OLD = MAINB1 + MAINB2
            B_NEW = [("us2", 0), ("us2", 1), ("us2", 2), "nwb2", ("w2r", 0), ("w2r", 1), ("rtmp", 0), ("rtmp", 1)] + \
                    [("aT", sl, m, tb) for sl in range(2) for m in range(4) for tb in range(2)]
            S.fence(B_OLD, B_NEW)
            n2T = uT
            sp_load(nwb2, nw2.partition_broadcast(128), "nwb2")

            def p6_a(j):
                sl = j % 3
                norm_a(hh[:, j, :], [("h", j, fc) for fc in range(4)], nwb2, "nwb2", us2[sl], ("us2", sl))

            def p6_b(j):
                sl = j % 3
                bset = (0, 1) if j % 2 == 0 else (2, 3)
                S.fence([("mixr", j), ("mixg", j)], [("n2T", j, 0), ("n2T", j, 1)])
                norm_b(us2[sl], ("us2", sl), n2T, j * 128, [("n2T", j, 0), ("n2T", j, 1)], bset)

            rb = [0]

            def next_rot():
                rb[0] += 1
                return 4 + rb[0] % 4

            for cb in range(4):
                s = ring_load(w_out_v[:, :, cb * 512:(cb + 1) * 512])
                for j in range(NTM):
                    b = next_rot()
                    mm_group([(bank(b), mixedT[:, kc, j * 128:(j + 1) * 128], ring[s][:, kc, :], kc == 0, kc == 15)
                              for kc in range(16)],
                             reads=[("ring", s), ("mixg", j), ("mixr", j)] + [("mix", c, j // 4) for c in range(8, 16)],
                             writes=[("ps", b)])
                    S.op("dve", lambda e, j=j, cb=cb, b=b: e.tensor_tensor(
                        out=hh[:, j, cb * 512:(cb + 1) * 512], in0=bank(b), in1=hh[:, j, cb * 512:(cb + 1) * 512], op=ALU.add),
                        reads=[("ps", b), ("h", j, cb)], writes=[("h", j, cb)])
                    if cb == 3:
                        p6_a(j)
                        if j >= 2:
                            p6_b(j - 2)

            checkpoint("p6", {"n2T": (uT, N2_KEYS, [128, 16, TM], BF16)})
            NG = DFF // 512

            def mlp_p1(g):
                s = ring_load(w1_v[:, :, g * 512:(g + 1) * 512])
                a = aT[g % 2]
                for m in range(4):
                    bs_ = (0, 1) if (g * 4 + m) % 2 == 0 else (2, 3)
                    mms = []
                    for kc in range(16):
                        for tb in range(2):
                            mms.append((bank(bs_[tb]), ring[s][:, kc, m * 128:(m + 1) * 128],
                                        n2T[:, kc, tb * 512:(tb + 1) * 512], kc == 0, kc == 15))
                    mm_group(mms, reads=[("ring", s)] + N2_KEYS, writes=[("ps", bs_[0]), ("ps", bs_[1])])
                    for tb in range(2):
                        S.op("act", lambda e, tb=tb, b=bs_[tb]: e.activation(out=rtmp[tb], in_=bank(b), func=AF.Relu),
                             reads=[("ps", bs_[tb])], writes=[("rtmp", tb)])
                        S.op("act", lambda e, tb=tb, m=m, a=a: e.activation(out=a[:, m, tb * 512:(tb + 1) * 512],
                                                                             in_=rtmp[tb], func=AF.Square),
                             reads=[("rtmp", tb)], writes=[("aT", g % 2, m, tb)])

            def mlp_p1_first():
                s = ring_load(w1_v[:, :, 0:512])
                a = aT[0]
                for tb in range(2):
                    keys = [("n2T", j, h) for j in range(tb * 4, tb * 4 + 4) for h in range(2)]
                    if tb == 1:
                        p6_b(NTM - 2)
                        p6_b(NTM - 1)
                    for m in range(4):
                        b = 4 + m if tb == 0 else m
                        mm_group([(bank(b), ring[s][:, kc, m * 128:(m + 1) * 128], n2T[:, kc, tb * 512:(tb + 1) * 512],
                                   kc == 0, kc == 15) for kc in range(16)],
                                 reads=[("ring", s)] + keys, writes=[("ps", b)])
                        S.op("act", lambda e, tb=tb, b=b: e.activation(out=rtmp[tb], in_=bank(b), func=AF.Relu),
                             reads=[("ps", b)], writes=[("rtmp", tb)])
                        S.op("act", lambda e, tb=tb, m=m: e.activation(out=a[:, m, tb * 512:(tb + 1) * 512],
                                                                        in_=rtmp[tb], func=AF.Square),
                             reads=[("rtmp", tb)], writes=[("aT", 0, m, tb)])

            out_toks = []

            fin_c = {}

            def final_a(j):
                sl = j % 2
                hk = [("h", j, fc) for fc in range(4)]
                fin_c[j] = norm_a(hh[:, j, :], hk, None, None, us2[sl], ("us2", sl), apply=False)

            def final_b(j):
                sl = j % 2
                hk = [("h", j, fc) for fc in range(4)]
                c = fin_c[j]
                S.op("dve", lambda e: e.scalar_tensor_tensor(
                    out=ostage[sl], in0=hh[:, j, :], scalar=rs[:, c:c + 1], in1=nwb2, op0=ALU.mult, op1=ALU.mult),
                    reads=hk + [("rs", c), "nwb2"], writes=[("ost", sl)])
                out_toks.append(S.dma("sp", lambda e: e.dma_start(out=out[j * 128:(j + 1) * 128, :], in_=ostage[sl]),
                                      f"ost{sl}", reads=[("ost", sl)]))

            w2n = [0]

            def mlp_p2(g, last=False):
                s2 = w2n[0] % 2
                w2n[0] += 1
                S.dma("pool", lambda e, g=g, s2=s2: e.dma_start(out=w2r[s2], in_=w2_v[:, g * 4:(g + 1) * 4, :]),
                      f"w2r{s2}", writes=[("w2r", s2)])
                a = aT[g % 2]
                for j in range(NTM):
                    for fc in range(4):
                        b = 4 + (j * 4 + fc) % 4
                        mm_group([(bank(b), a[:, kc, j * 128:(j + 1) * 128], w2r[s2][:, kc, fc * 512:(fc + 1) * 512],
                                   kc == 0, kc == 3) for kc in range(4)],
                                 reads=[("w2r", s2)] + [("aT", g % 2, kc, j // 4) for kc in range(4)], writes=[("ps", b)])
                        S.op("dve", lambda e, j=j, fc=fc, b=b: e.tensor_tensor(
                            out=hh[:, j, fc * 512:(fc + 1) * 512], in0=bank(b), in1=hh[:, j, fc * 512:(fc + 1) * 512], op=ALU.add),
                            reads=[("ps", b), ("h", j, fc)], writes=[("h", j, fc)])
                    if last:
                        final_a(j)
                        if j >= 1:
                            final_b(j - 1)
                if last:
                    final_b(NTM - 1)

            def w2_load(g):
                s2 = w2n[0] % 2
                w2n[0] += 1
                S.dma("pool", lambda e: e.dma_start(out=w2r[s2], in_=w2_v[:, g * 4:(g + 1) * 4, :]),
                      f"w2r{s2}", writes=[("w2r", s2)])
                return s2

            def mlp_p2_last_pair(ga_, gb_, sa_, sb_):
                srcs = [(aT[ga_ % 2], w2r[sa_], ga_ % 2, sa_), (aT[gb_ % 2], w2r[sb_], gb_ % 2, sb_)]
                for j in range(NTM):
                    for fc in range(4):
                        b = 4 + (j * 4 + fc) % 4
                        mms, rd = [], []
                        for gi, (a, w, ap_, ws_) in enumerate(srcs):
                            for kc in range(4):
                                mms.append((bank(b), a[:, kc, j * 128:(j + 1) * 128], w[:, kc, fc * 512:(fc + 1) * 512],
                                            gi == 0 and kc == 0, gi == 1 and kc == 3))
                            rd += [("w2r", ws_)] + [("aT", ap_, kc, j // 4) for kc in range(4)]
                        mm_group(mms, reads=rd, writes=[("ps", b)])
                        S.op("dve", lambda e, j=j, fc=fc, b=b: e.tensor_tensor(
                            out=hh[:, j, fc * 512:(fc + 1) * 512], in0=bank(b), in1=hh[:, j, fc * 512:(fc + 1) * 512], op=ALU.add),
                            reads=[("ps", b), ("h", j, fc)], writes=[("h", j, fc)])
                    final_a(j)
                    if j >= 1:
                        final_b(j - 1)
                final_b(NTM - 1)

            S.fence([("us2", 2)], [("aT", 0, m, tb) for m in range(4) for tb in range(2)])
            mlp_p1_first()
            for g in range(NG):
                if g + 1 < NG:
                    mlp_p1(g + 1)
                if g < NG - 2:
                    mlp_p2(g)
                elif g == NG - 2:
                    s_pen = w2_load(g)
                else:
                    s_last = w2_load(g)
                    S.fence([("ring", 0), ("ring", 1)], [("ost", 0), ("ost", 1)])
                    sp_load(nwb2, nwf.partition_broadcast(128), "nwb2")
                    mlp_p2_last_pair(NG - 2, NG - 1, s_pen, s_last)
            checkpoint("p7", {"h2": (hh, H_KEYS, [128, NTM, D], F32)})
        except _Stop:
            pass
        for k_ in list(S.cnt):
            if not k_.startswith('E_'):
                S.wait('sp', (k_, S.cnt[k_]))
        S.emit()
    return nc


_CACHE = {}


def make_in_maps(x, meta_tokens, norm1_w, w_in, gate_w2, gate_b, gla_norm_w, pool_w, pool_scale,
                 w_out, norm2_w, mlp_w1, mlp_w2, final_norm_w, cores=None):
    f = lambda a: np.ascontiguousarray(np.asarray(a, dtype=np.float32))
    x = f(x); meta = f(meta_tokens)
    B = x.shape[0]
    cores = list(range(2 * B)) if cores is None else cores
    shared = {
        "w_in": f(w_in)[0], "w_out": f(w_out)[0], "w1": f(mlp_w1)[0], "w2": f(mlp_w2)[0],
        "pool_w": f(pool_w)[0],
        "gw2a": np.ascontiguousarray(np.concatenate([f(gate_w2)[0], f(gate_b)[0][None, :]], axis=0)),
        "gnw": np.ascontiguousarray(f(gla_norm_w)[0].reshape(2, 128).T),
        "psc": np.ascontiguousarray(f(pool_scale)[0].reshape(8, 128).T),
        "nw1": f(norm1_w)[0], "nw2": f(norm2_w)[0], "nwf": f(final_norm_w),
        "c_ident": np.eye(128, dtype=np.float32).astype(ml_dtypes.bfloat16),
        "c_mask4": np.ascontiguousarray(np.tile(np.triu(np.ones((128, 128), np.float32)), (1, 4))).astype(ml_dtypes.bfloat16),
    }
    in_maps = []
    for c in cores:
        b, half = divmod(c, 2)
        xin = np.zeros((TA, D), np.float32)
        fl = np.zeros((128, 2), np.float32)
        if half == 0:
            xin[112:128] = meta
            xin[128:] = x[b, 0:1024]
            fl[:, 0] = 1.0
        else:
            xin[112:128] = x[b, 1008:1024]
            xin[128:] = x[b, 1024:2048]
            fl[:, 1] = 1.0
        m = dict(shared)
        m["xin"] = xin
        m["flags"] = fl
        in_maps.append(m)
    return in_maps


def kernel(x, meta_tokens, norm1_w, w_in, gate_w2, gate_b, gla_norm_w, pool_w, pool_scale,
           w_out, norm2_w, mlp_w1, mlp_w2, final_norm_w):
    B = np.asarray(x).shape[0]
    n_cores = 2 * B
    in_maps = make_in_maps(x, meta_tokens, norm1_w, w_in, gate_w2, gate_b, gla_norm_w, pool_w, pool_scale,
                           w_out, norm2_w, mlp_w1, mlp_w2, final_norm_w)
    if "nc" not in _CACHE:
        _CACHE["nc"] = build_program()
    res = run_bass_kernel_spmd(_CACHE["nc"], in_maps, core_ids=list(range(n_cores)))
    outp = np.empty((B, 2048, D), np.float32)
    for c in range(n_cores):
        b, half = divmod(c, 2)
        outp[b, half * 1024:(half + 1) * 1024] = np.asarray(res.results[c]["out"], dtype=np.float32)
    return outp
```

```python
import numpy as np
import ml_dtypes
import concourse.bass as bass
import concourse.mybir as mybir
from concourse.bass_utils import run_bass_kernel_spmd
from contextlib import ExitStack

F32 = mybir.dt.float32
BF16 = mybir.dt.bfloat16
AF = mybir.ActivationFunctionType
ALU = mybir.AluOpType

ENGS = ("pe", "act", "dve", "pool", "sp")

D = 2048
NTM = 8
NTA = NTM + 1
TM, TA = NTM * 128, NTA * 128
DIN = 4112
DFF = 8192
EPS = 1e-6
POOL_W = (2, 4, 8, 16)


class Sched:
    def __init__(self, nc, es):
        self.nc, self.es = nc, es
        self.prog = {e: [] for e in ENGS}
        self.cnt = {}
        self.semh = {}
        self.waited = {}
        self.lastw = {}
        self.readers = {}

    def _sem(self, key):
        if key not in self.semh:
            self.semh[key] = self.es.enter_context(self.nc.semaphore("s_" + key))
            self.cnt[key] = 0
        return self.semh[key]

    def _wait(self, eng, tok):
        if tok is None:
            return
        key, val = tok
        if key == "E_pe" and eng == "pe":
            return
        if self.waited.get((eng, key), 0) >= val:
            return
        self.waited[(eng, key)] = val
        self.prog[eng].append(("wait", key, val))

    def _deps(self, eng, reads, writes):
        for k in reads:
            self._wait(eng, self.lastw.get(k))
        for k in writes:
            self._wait(eng, self.lastw.get(k))
            for sk, v in self.readers.get(k, {}).items():
                self._wait(eng, (sk, v))

    def _commit(self, tok, reads, writes):
        for k in reads:
            r = self.readers.setdefault(k, {})
            r[tok[0]] = max(r.get(tok[0], 0), tok[1])
        for k in writes:
            self.lastw[k] = tok
            self.readers[k] = {}

    def op(self, eng, fn, reads=(), writes=()):
        self._deps(eng, reads, writes)
        key = "E_" + eng
        self._sem(key)
        self.cnt[key] += 1
        tok = (key, self.cnt[key])
        self.prog[eng].append(("op", fn, key, 1))
        self._commit(tok, reads, writes)
        return tok

    def dma(self, eng, fn, semkey, reads=(), writes=()):
        self._deps(eng, reads, writes)
        self._sem(semkey)
        self.cnt[semkey] += 16
        tok = (semkey, self.cnt[semkey])
        self.prog[eng].append(("op", fn, semkey, 16))
        self._commit(tok, reads, writes)
        return tok

    def fence(self, old_keys, new_keys):
        acc = {}
        for k in old_keys:
            t = self.lastw.get(k)
            if t is not None:
                acc[t[0]] = max(acc.get(t[0], 0), t[1])
            for sk, v in self.readers.get(k, {}).items():
                acc[sk] = max(acc.get(sk, 0), v)
        for k in new_keys:
            r = self.readers.setdefault(k, {})
            for sk, v in acc.items():
                r[sk] = max(r.get(sk, 0), v)

    def wait(self, eng, tok):
        self._wait(eng, tok)

    def emit(self):
        nc = self.nc
        with nc.Block() as block:
            def replay(name, e):
                for it in self.prog[name]:
                    if it[0] == "wait":
                        e.wait_ge(self.semh[it[1]], it[2])
                    else:
                        ins = it[1](e)
                        ins.then_inc(self.semh[it[2]], it[3])

            @block.tensor
            def _(e):
                replay("pe", e)

            @block.scalar
            def _(e):
                replay("act", e)

            @block.vector
            def _(e):
                replay("dve", e)

            @block.gpsimd
            def _(e):
                replay("pool", e)

            @block.sync
            def _(e):
                replay("sp", e)


A_OFF = 0
B_OFF = 32768
C_OFF = B_OFF + 70912
D_OFF = C_OFF + 32768
E_OFF = D_OFF + 65536
E_SIZE = 10240
ARENA_BYTES = E_OFF + E_SIZE


class _Stop(Exception):
    pass


def build_program(stop=None, dumps=()):
    nc = bass.Bass("TRN2", target_bir_lowering=False)

    def din(name, shape, dt=F32):
        return nc.dram_tensor(name, list(shape), dt, kind="ExternalInput").ap()

    xin = din("xin", [TA, D])
    flags_d = din("flags", [128, 2])
    w_in = din("w_in", [D, DIN])
    gw2a_d = din("gw2a", [17, 512])
    gnw_d = din("gnw", [128, 2])
    psc_d = din("psc", [128, 8])
    pool_w = din("pool_w", [4, 256, 256])
    w_out = din("w_out", [D, D])
    nw1 = din("nw1", [D])
    nw2 = din("nw2", [D])
    nwf = din("nwf", [D])
    w1 = din("w1", [D, DFF])
    w2 = din("w2", [DFF, D])
    c_ident = din("c_ident", [128, 128], BF16)
    c_mask4 = din("c_mask4", [128, 512], BF16)
    out = nc.dram_tensor("out", [TM, D], F32, kind="ExternalOutput").ap()

    with ExitStack() as es:
        S = Sched(nc, es)

        def checkpoint(name, bufs):
            for bname, (ap, keys, shape, dt) in bufs.items():
                if bname in dumps:
                    d_ap = nc.dram_tensor("dbg_" + bname, list(shape), dt, kind="ExternalOutput").ap()
                    S.dma("sp", lambda e, d_ap=d_ap, ap=ap: e.dma_start(out=d_ap, in_=ap), "dbg_" + bname, reads=keys)
            if stop == name:
                raise _Stop()

        AR = es.enter_context(nc.sbuf_tensor("arena", [128, ARENA_BYTES // 4], F32))
        PS = es.enter_context(nc.psum_tensor("ps", [128, 4096], F32))

        def V(off, shape, dt=F32):
            n = int(np.prod(shape[1:]))
            isz = 4 if dt == F32 else 2
            assert off % 4 == 0 and (n * isz) % 4 == 0
            ap = AR[:, off // 4:(off + n * isz) // 4]
            if dt != F32:
                ap = ap.bitcast(dt)
            if len(shape) == 3:
                ap = ap.rearrange("p (a b) -> p a b", b=shape[2])
            elif len(shape) == 4:
                ap = ap.rearrange("p (a b c) -> p a b c", b=shape[2], c=shape[3])
            if shape[0] != 128:
                ap = ap[0:shape[0]]
            return ap

        def bank(b, n=512):
            return PS[:, b * 512:b * 512 + n]

        def bankbf(b):
            return PS[:, b * 512:(b + 1) * 512].bitcast(BF16)

        uT = V(A_OFF, [128, 16, TM], BF16)
        qT = V(B_OFF, [128, 4, TM], BF16)
        kT = V(B_OFF + 8192, [128, 4, TA], BF16)
        vv = V(B_OFF + 17408, [128, NTA, 1024], BF16)
        glrT = V(B_OFF + 35840, [17, TA], BF16)
        glrT_full = V(B_OFF + 35840, [128, TA], BF16)
        siluT = V(B_OFF + 38144, [128, 8, TM], BF16)
        yT = V(B_OFF + 54528, [128, 8, TM], BF16)
        us2 = [V(B_OFF + o_, [128, D], BF16) for o_ in (0, 4096, 16384)]
        nwb2 = V(B_OFF + 8192, [128, D], F32)
        aT = [V(B_OFF + 16384 + i * 8192, [128, 4, TM], BF16) for i in range(2)]
        w2r = [V(B_OFF + 32768 + i * 16384, [128, 4, D], BF16) for i in range(2)]
        rtmp = [V(B_OFF + 65536 + i * 2048, [128, 512], F32) for i in range(2)]
        ring = [V(C_OFF + i * 16384, [128, 16, 512], BF16) for i in range(2)]
        ostage = [V(C_OFF + i * 8192, [128, D], F32) for i in range(2)]
        NXS = 3
        xs = [V(D_OFF + 4096 + i * 8192, [128, D], F32) for i in range(NXS)]
        us = [V(D_OFF + 28672 + i * 4096, [128, D], BF16) for i in range(2)]
        nwb1 = V(D_OFF + 36864, [128, D], F32)
        put = [V(D_OFF + i * 4160, [128, 1040], F32) for i in range(3)]
        hh = V(D_OFF, [128, NTM, D], F32)

        def ga(par):
            o = D_OFF + 12480
            return dict(e=V(o + par * 2048, [128, 512]), L=V(o + 4096 + par * 2048, [128, 512]),
                        eN=V(o + 8192 + par * 2048, [128, 512]), eG=V(o + 12288, [128, 512]),
                        ki32=V(o + 14336, [128, 512]))
        Sf = V(D_OFF + 28864, [128, 1024])
        ki_t = V(D_OFF + 32960, [128, 512], BF16)
        keTt = V(D_OFF + 33984, [128, 512], BF16)
        Sb2 = V(D_OFF + 32960, [128, 1024], BF16)
        qd_all = V(D_OFF + 35008, [128, NTM, 512], BF16)
        sc_all = V(D_OFF + 43200, [128, NTM, 512], BF16)
        ke_all = V(D_OFF + 51392, [128, NTA, 512], BF16)
        dec_all = V(D_OFF + 60608, [128, NTA + 1, 4], F32)
        poolw = V(D_OFF + 60768, [128, 4, 2, 256], BF16)
        wglr = V(D_OFF + 64864, [128, 16, 16], BF16)

        def sbt(par):
            o = D_OFF + par * 14336
            return dict(sq=V(o, [128, 1024]), rstd=V(o + 4096, [128, 512]), A=V(o + 6144, [128, 1024]),
                        osb=V(o + 10240, [128, 1024]))

        eo = E_OFF
        ident = V(eo, [128, 128], BF16); eo += 256
        mask4 = V(eo, [128, 512], BF16); eo += 1024
        ones = V(eo, [128, 128]); eo += 512
        gw2a = V(eo, [17, 512], BF16); eo += 1024
        gnw = V(eo, [128, 2]); eo += 8
        psc = V(eo, [128, 8]); eo += 32
        pscw = V(eo, [128, 8]); eo += 32
        flags = V(eo, [128, 2]); eo += 8
        st = V(eo, [128, 64]); eo += 256
        rs = V(eo, [128, 64]); eo += 256
        Sb = V(eo, [128, 1024], BF16); eo += 2048
        uTtail = V(eo, [128, 16, 16], BF16); eo += 512
        uTh = V(eo, [128, 16, 128], BF16); eo += 4096
        assert eo <= E_OFF + E_SIZE, eo - E_OFF
        assert D_OFF + 64864 + 512 <= E_OFF

        s_bounce = nc.dram_tensor("s_bounce", [128, 1024], F32)
        s_gath = nc.dram_tensor("s_gath", [256, 1024], F32)

        w_in_v = w_in.rearrange("(kc p) n -> p kc n", p=128)
        w_out_v = w_out.rearrange("(kc p) n -> p kc n", p=128)
        w1_v = w1.rearrange("(kc p) n -> p kc n", p=128)
        w2_v = w2.rearrange("(kc p) n -> p kc n", p=128)

        def sp_load(dst, src, key):
            return S.dma("sp", lambda e: e.dma_start(out=dst, in_=src), "c_" + str(key), writes=[key])

        def copy_op(i, dst, src, reads, writes):
            if i % 2 == 0:
                return S.op("act", lambda e: e.copy(out=dst, in_=src), reads=reads, writes=writes)
            return S.op("dve", lambda e: e.tensor_copy(out=dst, in_=src), reads=reads, writes=writes)

        def mm_group(mms, reads, writes):
            def fn(e):
                last = None
                for (o, l, r, a, b) in mms:
                    last = e.matmul(o, l, r, start=a, stop=b)
                return last
            return S.op("pe", fn, reads=reads, writes=writes)

        def tr_group(trs, reads, writes):
            def fn(e):
                last = None
                for (o, i) in trs:
                    last = e.transpose(out=o, in_=i, identity=ident)
                return last
            return S.op("pe", fn, reads=reads + ["ident"], writes=writes)

        try:
            sp_load(nwb1, nw1.partition_broadcast(128), "nwb1")
            S.op("dve", lambda e: e.memset(ones, 1.0), writes=["ones"])
            S.op("dve", lambda e: e.memset(st, 0.0), writes=["st"])
            ring_n = [0]

            def ring_load(src_ap, ncols=512):
                if ring_n[0] == 0:
                    S.wait("pool", x_toks[3])
                s = ring_n[0] % 2
                ring_n[0] += 1
                dst = ring[s] if ncols == 512 else ring[s][:, :, 0:ncols]
                S.dma("pool", lambda e: e.dma_start(out=dst, in_=src_ap), f"ring{s}", writes=[("ring", s)])
                return s

            stat_col = [0]
            x_toks = {}

            def norm_a(src, src_keys, nwb, nwb_key, stg, stg_key, apply=True):
                c = stat_col[0]
                stat_col[0] += 1
                S.op("act", lambda e: e.activation(out=stg, in_=src, func=AF.Square, scale=float(D ** -0.5),
                                                   accum_out=st[:, c:c + 1]),
                     reads=src_keys + ["st"], writes=[stg_key, ("st", c)])
                S.op("act", lambda e: e.activation(out=rs[:, c:c + 1], in_=st[:, c:c + 1], func=AF.Ln, bias=EPS),
                     reads=[("st", c)], writes=[("rs", c)])
                S.op("act", lambda e: e.activation(out=rs[:, c:c + 1], in_=rs[:, c:c + 1], func=AF.Exp, scale=-0.5),
                     reads=[("rs", c)], writes=[("rs", c)])
                if apply:
                    S.op("dve", lambda e: e.scalar_tensor_tensor(out=stg, in0=src, scalar=rs[:, c:c + 1], in1=nwb,
                                                                 op0=ALU.mult, op1=ALU.mult),
                         reads=src_keys + [("rs", c), nwb_key], writes=[stg_key])
                return c

            def norm_b(stg, stg_key, dstT, dcol, dkeys, bset):
                for half in range(2):
                    b = bset[half]
                    tr_group([(bankbf(b)[:, i * 128:(i + 1) * 128], stg[:, (half * 8 + i) * 128:(half * 8 + i + 1) * 128])
                              for i in range(8)], reads=[stg_key], writes=[("ps", b)])
                    copy_op(half, dstT[:, half * 8:half * 8 + 8, dcol:dcol + 128],
                            bankbf(b).rearrange("p (a b) -> p a b", b=128),
                            reads=[("ps", b)], writes=[dkeys[half]])

            def p0_a(t):
                sl = t % 2
                xl = t % NXS
                x_toks[t] = S.dma("sp", lambda e: e.dma_start(out=xs[xl], in_=xin[t * 128:(t + 1) * 128, :]),
                                  f"xs{xl}", writes=[("xs", xl)])
                norm_a(xs[xl], [("xs", xl)], nwb1, "nwb1", us[sl], ("us", sl))

            def p0_b(t):
                sl = t % 2
                bset = (0, 1) if t % 2 == 0 else (2, 3)
                if t == 0:
                    norm_b(us[sl], ("us", sl), uTh, 0, [("uTh", 0), ("uTh", 1)], bset)
                else:
                    norm_b(us[sl], ("us", sl), uT, (t - 1) * 128, [("uT", t - 1, 0), ("uT", t - 1, 1)], bset)

            XS_KEYS = [("xs", i) for i in range(NXS)] + [("us", 0), ("us", 1)]
            UTH_KEYS = [("uTh", 0), ("uTh", 1)]
            UT_KEYS = [("uT", j, h) for j in range(NTM) for h in range(2)]
            p0_a(0)
            sp_load(ident, c_ident, "ident")
            S.dma("pool", lambda e: e.dma_start(out=gw2a, in_=gw2a_d), "gw2a", writes=["gw2a"])
            S.dma("pool", lambda e: e.dma_start(out=wglr, in_=w_in_v[:, :, 3072:3088]), "wglr", writes=["wglr"])
            for t in range(NTA):
                if t + 1 < NTA:
                    p0_a(t + 1)
                p0_b(t)
            sp_load(mask4, c_mask4, "mask4")
            sp_load(gnw, gnw_d, "gnw")
            sp_load(psc, psc_d, "psc")
            sp_load(flags, flags_d, "flags")
            for g in range(4):
                S.op("dve", lambda e, g=g: e.tensor_scalar(out=pscw[:, 2 * g:2 * g + 2], in0=psc[:, 2 * g:2 * g + 2],
                                                           scalar1=1.0 / POOL_W[g], scalar2=None, op0=ALU.mult),
                     reads=["psc"], writes=[("pscw", g)])
            S.op("dve", lambda e: e.tensor_copy(out=uTtail, in_=uTh[:, :, 112:128]), reads=UTH_KEYS, writes=["uTtail"])
            checkpoint("p0", {"uT": (uT, UT_KEYS, [128, 16, TM], BF16)})

            chunk_ctr = [0]

            def next_bset():
                chunk_ctr[0] += 1
                return (0, 1, 2) if chunk_ctr[0] % 2 else (3, 4, 5)

            tm_ctr = [0]

            def next_tm_bank():
                tm_ctr[0] += 1
                return 3 + tm_ctr[0] % 5

            ev = [0]

            def fm_mms(w_ap_fn, bs_, with_head, tail=False):
                mms = []
                for kc in range(16):
                    for tb in range(2):
                        mms.append((bank(bs_[tb]), w_ap_fn(kc), uT[:, kc, tb * 512:(tb + 1) * 512], kc == 0, kc == 15))
                    if with_head:
                        mms.append((bank(bs_[2], 128), w_ap_fn(kc), uTh[:, kc, :], kc == 0, kc == 15))
                    if tail:
                        mms.append((bank(bs_[2], 16), w_ap_fn(kc), uTtail[:, kc, :], kc == 0, kc == 15))
                return mms

            def gates(T, K, glr_ap, glr_key, bg, dec_ap, dkey, want_eG, head):
                mm_group([(bank(bg)[:, h * 128:(h + 1) * 128], gw2a[0:17, h * 128:(h + 1) * 128], glr_ap, True, True)
                          for h in range(4)], reads=["gw2a", glr_key], writes=[("ps", bg)])
                S.op("act", lambda e: e.activation(out=T["e"], in_=bank(bg), func=AF.Exp, scale=-1.0),
                     reads=[("ps", bg)], writes=[K + "e"])
                S.op("act", lambda e: e.activation(out=T["e"], in_=T["e"], func=AF.Ln, bias=1.0),
                     reads=[K + "e"], writes=[K + "e"])
                if head:
                    S.op("dve", lambda e: e.tensor_scalar(out=T["e"], in0=T["e"], scalar1=flags[:, 0:1], scalar2=None,
                                                          op0=ALU.mult), reads=[K + "e", "flags"], writes=[K + "e"])
                for h in range(4):
                    S.op("dve", lambda e, h=h: e.tensor_tensor_scan(out=T["L"][:, h * 128:(h + 1) * 128],
                                                                     data0=T["e"][:, h * 128:(h + 1) * 128],
                                                                     data1=T["e"][:, h * 128:(h + 1) * 128],
                                                                     initial=0.0, op0=ALU.add, op1=ALU.bypass),
                         reads=[K + "e"], writes=[K + "L%d" % h])
                Lk = [K + "L%d" % h for h in range(4)]
                S.op("act", lambda e: e.activation(out=T["eN"], in_=T["L"], func=AF.Exp, scale=1.0 / 16),
                     reads=Lk, writes=[K + "eN"])
                S.op("act", lambda e: e.activation(out=dec_ap, in_=T["L"].rearrange("p (h t) -> p h t", t=128)[:, :, 127],
                                                   func=AF.Exp, scale=-1.0 / 16),
                     reads=Lk, writes=[dkey])
                if want_eG:
                    S.op("act", lambda e: e.activation(out=T["eG"], in_=T["L"], func=AF.Exp, scale=-1.0 / 16),
                         reads=Lk, writes=["ga_eG"])

            def ke_transpose(src, src_keys, bk, dst, dkey):
                tr_group([(bankbf(bk)[:, h * 128:(h + 1) * 128], src[:, h * 128:(h + 1) * 128]) for h in range(4)],
                         reads=src_keys, writes=[("ps", bk)])
                S.op("act", lambda e: e.copy(out=dst, in_=bankbf(bk)[:, 0:512]), reads=[("ps", bk)], writes=[dkey])

            def su_mm(t, bd):
                mm_group([(PS[:, bd * 512 + h * 256:bd * 512 + (h + 1) * 256], ke_all[:, t, h * 128:(h + 1) * 128],
                           vv[:, t, h * 256:(h + 1) * 256], True, True) for h in range(4)],
                         reads=[("ke", t), ("vv", t, 0), ("vv", t, 1)], writes=[("ps", bd), ("ps", bd + 1)])

            def su_stt(t, bd):
                for h in range(4):
                    S.op("dve", lambda e, h=h: e.scalar_tensor_tensor(
                        out=Sf[:, h * 256:(h + 1) * 256], in0=Sf[:, h * 256:(h + 1) * 256], scalar=dec_all[:, t, h:h + 1],
                        in1=PS[:, bd * 512 + h * 256:bd * 512 + (h + 1) * 256], op0=ALU.mult, op1=ALU.add),
                        reads=["Sf", ("dec", t), ("ps", bd), ("ps", bd + 1)], writes=["Sf"])

            def state_update(t, bd):
                su_mm(t, bd)
                su_stt(t, bd)

            GAKEYS = ["ki_t", "keTt", "ga_eG", "ga_ki32", "Sf", ("dec", NTA)] + \
                     [(n, t) for n in ("qd", "sc", "ke", "dec") for t in range(NTA)]
            for par in range(2):
                GAKEYS += ["ga%d_" % par + n for n in ["e", "L0", "L1", "L2", "L3", "eN"]]
            S.fence(XS_KEYS + ["nwb1"], GAKEYS)
            def stage_a1(t, bg):
                par = t % 2
                T = ga(par)
                K = "ga%d_" % par
                kcols = slice(t * 128, (t + 1) * 128)
                head = (t == 0)
                gates(T, K, glrT[0:17, kcols], "glrT", bg, dec_all[:, t, :], ("dec", t), not head, head)
                eN3 = T["eN"].rearrange("p (h t) -> p h t", t=128)
                if not head:
                    j = t - 1
                    qcols = slice(j * 128, (j + 1) * 128)
                    eG3 = T["eG"].rearrange("p (h t) -> p h t", t=128)
                    S.op("dve", lambda e: e.scalar_tensor_tensor(
                        out=qd_all[:, j, :].rearrange("p (h t) -> p h t", t=128), in0=qT[:, :, qcols],
                        scalar=float(128 ** -0.5), in1=eG3, op0=ALU.mult, op1=ALU.mult),
                        reads=[("qT", m, j // 4) for m in range(4)] + ["ga_eG"], writes=[("qd", t)])
                S.op("dve", lambda e: e.tensor_tensor(
                    out=T["ki32"].rearrange("p (h t) -> p h t", t=128), in0=kT[:, :, kcols], in1=eN3, op=ALU.mult),
                    reads=[("kT", m) for m in range(4)] + [K + "eN"], writes=["ga_ki32"])
                dsc, dsk = dec_all[:, t, :], ("dec", t)
                if head:
                    S.op("dve", lambda e: e.tensor_scalar(out=dec_all[:, NTA, :], in0=dec_all[:, 0, :], scalar1=flags[:, 0:1],
                                                          scalar2=None, op0=ALU.mult),
                         reads=[("dec", 0), "flags"], writes=[("dec", NTA)])
                    dsc, dsk = dec_all[:, NTA, :], ("dec", NTA)
                else:
                    S.op("act", lambda e: e.copy(out=ki_t, in_=T["ki32"]), reads=["ga_ki32"], writes=["ki_t"])
                for h in range(4):
                    S.op("dve", lambda e, h=h: e.tensor_scalar(
                        out=keTt[:, h * 128:(h + 1) * 128], in0=T["ki32"][:, h * 128:(h + 1) * 128],
                        scalar1=dsc[:, h:h + 1], scalar2=None, op0=ALU.mult),
                        reads=["ga_ki32", dsk], writes=["keTt"])

            def stage_a2(t, bk, bsx):
                head = (t == 0)
                if not head:
                    j = t - 1
                    mm_group([(bank(bsx)[:, h * 128:(h + 1) * 128], ki_t[:, h * 128:(h + 1) * 128],
                               qd_all[:, j, h * 128:(h + 1) * 128], True, True) for h in range(4)],
                             reads=["ki_t", ("qd", t)], writes=[("ps", bsx)])
                    S.op("dve", lambda e: e.tensor_tensor(out=sc_all[:, j, :], in0=bank(bsx), in1=mask4, op=ALU.mult),
                         reads=[("ps", bsx), "mask4"], writes=[("sc", t)])
                ke_transpose(keTt, ["keTt"], bk, ke_all[:, t, :], ("ke", t))

            sa_state = {"next": 0, "pend": None}

            def sa_hook(bg, bsx, bk=None):
                if sa_state["pend"] is not None:
                    stage_a2(sa_state["pend"], bg if bk is None else bk, bsx)
                    sa_state["pend"] = None
                if sa_state["next"] < NTA:
                    stage_a1(sa_state["next"], bg)
                    sa_state["pend"] = sa_state["next"]
                    sa_state["next"] += 1

            S.op("dve", lambda e: e.memset(glrT_full, 1.0), writes=["glrT"])
            bs_ = next_bset()
            mms = []
            for kc in range(16):
                for tb in range(2):
                    mms.append((PS[0:16, bs_[tb] * 512:(bs_[tb] + 1) * 512], wglr[:, kc, :],
                                uT[:, kc, tb * 512:(tb + 1) * 512], kc == 0, kc == 15))
                mms.append((PS[0:16, bs_[2] * 512:bs_[2] * 512 + 128], wglr[:, kc, :], uTh[:, kc, :], kc == 0, kc == 15))
            mm_group(mms, reads=["wglr"] + UT_KEYS + UTH_KEYS, writes=[("ps", b) for b in bs_])
            for tb in range(2):
                ev[0] += 1
                copy_op(ev[0], glrT[0:16, 128 + tb * 512:128 + (tb + 1) * 512], PS[0:16, bs_[tb] * 512:(bs_[tb] + 1) * 512],
                        reads=[("ps", bs_[tb])], writes=["glrT"])
            ev[0] += 1
            copy_op(ev[0], glrT[0:16, 0:128], PS[0:16, bs_[2] * 512:bs_[2] * 512 + 128], reads=[("ps", bs_[2])], writes=["glrT"])

            s = ring_load(w_in_v[:, :, 512:1024])
            for m in range(4):
                bs_ = next_bset()
                mm_group(fm_mms(lambda kc, m=m, s=s: ring[s][:, kc, m * 128:(m + 1) * 128], bs_, True),
                         reads=[("ring", s)] + UT_KEYS + UTH_KEYS, writes=[("ps", b) for b in bs_])
                for tb in range(2):
                    S.op("act" if tb == 0 else "dve",
                         (lambda e, m=m, tb=tb, b=bs_[tb]: e.copy(out=kT[:, m, 128 + tb * 512:128 + (tb + 1) * 512], in_=bank(b)))
                         if tb == 0 else
                         (lambda e, m=m, tb=tb, b=bs_[tb]: e.tensor_copy(out=kT[:, m, 128 + tb * 512:128 + (tb + 1) * 512], in_=bank(b))),
                         reads=[("ps", bs_[tb])], writes=[("kT", m)])
                S.op("act", lambda e, m=m, b=bs_[2]: e.copy(out=kT[:, m, 0:128], in_=bank(b, 128)),
                     reads=[("ps", bs_[2])], writes=[("kT", m)])

            def fm_block(col0, evac, hook=None, slot=None):
                s = ring_load(w_in_v[:, :, col0:col0 + 512]) if slot is None else slot
                for m in range(4):
                    bs_ = next_bset()
                    mm_group(fm_mms(lambda kc, m=m, s=s: ring[s][:, kc, m * 128:(m + 1) * 128], bs_, False),
                             reads=[("ring", s)] + UT_KEYS, writes=[("ps", bs_[0]), ("ps", bs_[1])])
                    evac(m, bs_)
                    if hook is not None:
                        hook(m)

            def ev_copy(dst, name):
                def f(m, bs_):
                    for tb in range(2):
                        ev[0] += 1
                        copy_op(ev[0], dst[:, m, tb * 512:(tb + 1) * 512], bank(bs_[tb]),
                                reads=[("ps", bs_[tb])], writes=[(name, m, tb)])
                return f

            fm_block(0, ev_copy(qT, "qT"))
            for g in range(4):
                S.dma("pool", lambda e, g=g: e.dma_start(out=poolw[:, g], in_=pool_w[g].rearrange("(cc p) d -> p cc d", p=128)),
                      "poolw%d" % g, writes=[("poolw", g)])
            MAINB2 = [("siluT", c, tb) for c in range(8) for tb in range(2)] + [("yT", c) for c in range(8)]
            PUT_KEYS = [("put", i) for i in range(3)]
            for pb in range(2):
                s = ring_load(w_in_v[:, :, 3088 + pb * 512:3088 + (pb + 1) * 512])
                for m in range(4):
                    c = pb * 4 + m
                    g = c // 2
                    w = POOL_W[g]
                    bs_ = next_bset()
                    mm_group(fm_mms(lambda kc, m=m, s=s: ring[s][:, kc, m * 128:(m + 1) * 128], bs_, False, tail=True),
                             reads=[("ring", s), "uTtail"] + UT_KEYS, writes=[("ps", b) for b in bs_])
                    S.op("act", lambda e, b=bs_[2]: e.copy(out=put[0][:, 0:16], in_=bank(b, 16)),
                         reads=[("ps", bs_[2])], writes=[("put", 0)])
                    S.op("act", lambda e, b=bs_[0]: e.copy(out=put[0][:, 16:528], in_=bank(b)),
                         reads=[("ps", bs_[0])], writes=[("put", 0)])
                    S.op("act", lambda e, b=bs_[1]: e.copy(out=put[0][:, 528:1040], in_=bank(b)),
                         reads=[("ps", bs_[1])], writes=[("put", 0)])
                    if w == 2:
                        S.op("dve", lambda e, c=c: e.tensor_tensor(out=yT[:, c, :], in0=put[0][:, 15:1039], in1=put[0][:, 16:1040],
                                                                   op=ALU.subtract),
                             reads=[("put", 0)], writes=[("yT", c)])
                    else:
                        src, cur_i, sh, lo = put[0], 1, 1, 1
                        srckey = ("put", 0)
                        while sh < w:
                            dst = put[cur_i]
                            S.op("dve", lambda e, src=src, dst=dst, sh=sh, lo=lo: e.tensor_tensor(
                                out=dst[:, lo:1040], in0=src[:, lo:1040], in1=src[:, lo - sh:1040 - sh], op=ALU.add),
                                reads=[srckey], writes=[("put", cur_i)])
                            src, srckey = dst, ("put", cur_i)
                            cur_i = 3 - cur_i
                            sh *= 2
                            lo = 2 * sh - 1
                        S.op("dve", lambda e, c=c, src=src, w=w: e.scalar_tensor_tensor(
                            out=yT[:, c, :], in0=put[0][:, 16:1040], scalar=-float(w), in1=src[:, 16:1040],
                            op0=ALU.mult, op1=ALU.add),
                            reads=[("put", 0), srckey], writes=[("yT", c)])
                    if c % 2 == 1:
                        sa_hook(6, 7)

            for half in range(2):
                s = ring_load(w_in_v[:, :, 1024 + half * 512:1024 + (half + 1) * 512])
                for t in range(NTA):
                    b = next_tm_bank()
                    lh = (lambda kc: uTh[:, kc, :]) if t == 0 else (lambda kc, t=t: uT[:, kc, (t - 1) * 128:t * 128])
                    rk = UTH_KEYS if t == 0 else [("uT", t - 1, 0), ("uT", t - 1, 1)]
                    mm_group([(bank(b), lh(kc), ring[s][:, kc, :], kc == 0, kc == 15) for kc in range(16)],
                             reads=[("ring", s)] + rk, writes=[("ps", b)])
                    ev[0] += 1
                    copy_op(ev[0], vv[:, t, half * 512:(half + 1) * 512], bank(b),
                            reads=[("ps", b)], writes=[("vv", t, half)])
                    if t in (2, 5, 8):
                        sa_hook(0, 1, 2)
                    if half == 1 and t >= 2:
                        if t == 2:
                            S.op("dve", lambda e: e.memset(Sf, 0.0), writes=["Sf"])
                        state_update(t - 2, 0)

            while sa_state["pend"] is not None or sa_state["next"] < NTA:
                sa_hook(0, 1, 2)
            r_slots = [ring_load(w_in_v[:, :, 2048 + rb * 512:2048 + (rb + 1) * 512]) for rb in range(2)]
            state_update(NTA - 2, 0)
            state_update(NTA - 1, 0)
            S.dma("sp", lambda e: e.dma_start(out=s_bounce.ap(), in_=Sf), "sbo", reads=["Sf"], writes=["s_bounce"])
            S._deps("pool", ["s_bounce"], ["s_gath"])
            S._sem("cc")
            S.cnt["cc"] += 1
            cc_tok = ("cc", S.cnt["cc"])
            S.prog["pool"].append(("op", lambda e: e.collective_compute(
                "AllGather", ALU.bypass, replica_groups=[[0, 1], [2, 3], [4, 5], [6, 7]],
                ins=[s_bounce.ap()], outs=[s_gath.ap()]), "cc", 1))
            S._commit(cc_tok, ["s_bounce"], ["s_gath"])
            S.dma("sp", lambda e: e.dma_start(out=Sf, in_=s_gath.ap()[0:128, :]), "sgi", reads=["s_gath"], writes=["Sf"])
            S.op("dve", lambda e: e.tensor_scalar(out=Sf, in0=Sf, scalar1=flags[:, 1:2], scalar2=None, op0=ALU.mult),
                 reads=["Sf", "flags"], writes=["Sf"])
            state_update(0, 6)
            S.op("dve", lambda e: e.tensor_copy(out=Sb, in_=Sf), reads=["Sf"], writes=["Sb0"])

            for rb in range(2):
                def ev_silu(m, bs_, rb=rb):
                    for tb in range(2):
                        S.op("act", lambda e, m=m, tb=tb, b=bs_[tb]: e.activation(
                            out=siluT[:, rb * 4 + m, tb * 512:(tb + 1) * 512], in_=bank(b), func=AF.Silu),
                            reads=[("ps", bs_[tb])], writes=[("siluT", rb * 4 + m, tb)])
                fm_block(2048 + rb * 512, ev_silu, slot=r_slots[rb])

            checkpoint("p3", {"qT": (qT, [("qT", m, tb) for m in range(4) for tb in range(2)], [128, 4, TM], BF16),
                              "siluT": (siluT, [("siluT", c, tb) for c in range(8) for tb in range(2)], [128, 8, TM], BF16),
                              "yT": (yT, [("yT", c) for c in range(8)], [128, 8, TM], BF16)})

            MIX_KEYS = [("mix", c, tb) for c in range(16) for tb in range(2)] + [("mixg", j) for j in range(NTM)]
            S.fence(UT_KEYS, MIX_KEYS)
            mixedT = uT
            for g in range(4):
                for dc in range(2):
                    bs_ = (0, 1) if (g * 2 + dc) % 2 == 0 else (2, 3)
                    c = g * 2 + dc
                    for tb in range(2):
                        mm_group([(bank(bs_[tb]), poolw[:, g, cc, dc * 128:(dc + 1) * 128],
                                   yT[:, g * 2 + cc, tb * 512:(tb + 1) * 512], cc == 0, cc == 1) for cc in range(2)],
                                 reads=[("poolw", g), ("yT", g * 2), ("yT", g * 2 + 1)], writes=[("ps", bs_[tb])])
                        S.op("dve", lambda e, c=c, tb=tb, b=bs_[tb]: e.tensor_scalar(
                            out=mixedT[:, 8 + c, tb * 512:(tb + 1) * 512], in0=bank(b), scalar1=pscw[:, c:c + 1],
                            scalar2=None, op0=ALU.mult),
                            reads=[("ps", bs_[tb]), ("pscw", g)], writes=[("mix", 8 + c, tb)])

            SBKEYS = []
            for par in range(2):
                SBKEYS += ["sb%d_" % par + n for n in ["sq0", "sq1", "rstd", "A0", "A1", "osb0", "osb1"]]
            S.fence(PUT_KEYS + UTH_KEYS + [k_ for k_ in GAKEYS if k_ != "Sf" and not (isinstance(k_, tuple) and k_[0] in ("qd", "sc", "ke", "dec", "poolw"))] + XS_KEYS, SBKEYS)

            Sbs = [Sb, Sb2]
            S.fence(["ki_t", "keTt"], ["Sb1"])

            def p4_main(j):
                t = j + 1
                par = j % 2
                bo = 0 if par == 0 else 2
                Scur, Snxt = Sbs[par], Sbs[1 - par]
                if j < NTM - 1:
                    su_mm(t, 4)
                mms = []
                for h in range(4):
                    for vc in range(2):
                        o_ap = PS[:, bo * 512 + (h * 2 + vc) * 128:bo * 512 + (h * 2 + vc + 1) * 128]
                        mms.append((o_ap, vv[:, t, h * 256 + vc * 128:h * 256 + (vc + 1) * 128],
                                    sc_all[:, j, h * 128:(h + 1) * 128], True, False))
                        mms.append((o_ap, Scur[:, h * 256 + vc * 128:h * 256 + (vc + 1) * 128],
                                    qd_all[:, j, h * 128:(h + 1) * 128], False, True))
                mm_group(mms, reads=[("sc", t), ("qd", t), "Sb%d" % par, ("vv", t, 0), ("vv", t, 1)],
                         writes=[("ps", bo), ("ps", bo + 1)])
                if j < NTM - 1:
                    su_stt(t, 4)
                    S.op("dve", lambda e: e.tensor_copy(out=Snxt, in_=Sf), reads=["Sf"], writes=["Sb%d" % (1 - par)])
                T = sbt(par)
                K = "sb%d_" % par
                for hb in range(2):
                    S.op("act", lambda e, hb=hb: e.activation(out=T["sq"][:, hb * 512:(hb + 1) * 512],
                                                               in_=bank(bo + hb), func=AF.Square),
                         reads=[("ps", bo + hb)], writes=[K + "sq%d" % hb])
                    S.op("act", lambda e, hb=hb: e.copy(out=T["osb"][:, hb * 512:(hb + 1) * 512], in_=bank(bo + hb)),
                         reads=[("ps", bo + hb)], writes=[K + "osb%d" % hb])

            def p4_post(j):
                par = j % 2
                bg = 6 + par
                T = sbt(par)
                K = "sb%d_" % par
                mms = []
                for h in range(4):
                    for vc in range(2):
                        mms.append((bank(bg)[:, h * 128:(h + 1) * 128], ones,
                                    T["sq"][:, (h * 2 + vc) * 128:(h * 2 + vc + 1) * 128], vc == 0, vc == 1))
                mm_group(mms, reads=["ones", K + "sq0", K + "sq1"], writes=[("ps", bg)])
                S.op("act", lambda e: e.activation(out=T["rstd"], in_=bank(bg), func=AF.Ln, scale=1.0 / 256, bias=EPS),
                     reads=[("ps", bg)], writes=[K + "rstd"])
                S.op("act", lambda e: e.activation(out=T["rstd"], in_=T["rstd"], func=AF.Exp, scale=-0.5),
                     reads=[K + "rstd"], writes=[K + "rstd"])

            def p4_fin(j):
                par = j % 2
                T = sbt(par)
                K = "sb%d_" % par
                cols = slice(j * 128, (j + 1) * 128)
                r3 = T["rstd"].rearrange("p (h t) -> p h t", t=128)
                A4 = T["A"].rearrange("p (h v t) -> p h v t", v=2, t=128)
                o4 = T["osb"].rearrange("p (h v t) -> p h v t", v=2, t=128)
                silu4 = siluT[:, :, cols].rearrange("p (h v) t -> p h v t", v=2)
                mix4 = mixedT[:, 0:8, cols].rearrange("p (h v) t -> p h v t", v=2)
                for vc in range(2):
                    S.op("dve", lambda e, vc=vc: e.tensor_tensor(
                        out=A4[:, :, vc, :], in0=silu4[:, :, vc, :], in1=r3, op=ALU.mult),
                        reads=[("siluT", c, j // 4) for c in range(8)] + [K + "rstd"], writes=[K + "A%d" % vc])
                for vc in range(2):
                    S.op("dve", lambda e, vc=vc: e.scalar_tensor_tensor(
                        out=mix4[:, :, vc, :], in0=o4[:, :, vc, :], scalar=gnw[:, vc:vc + 1], in1=A4[:, :, vc, :],
                        op0=ALU.mult, op1=ALU.mult),
                        reads=[K + "osb0", K + "osb1", "gnw", K + "A%d" % vc], writes=[("mixg", j)])

            for j in range(NTM):
                p4_main(j)
                if j >= 1:
                    p4_post(j - 1)
                    p4_fin(j - 1)
            p4_post(NTM - 1)
            p4_fin(NTM - 1)

            checkpoint("p4", {"mixedT": (uT, MIX_KEYS, [128, 16, TM], BF16)})
            H_KEYS = [("h", j, fc) for j in range(NTM) for fc in range(4)]
            S.fence(GAKEYS + SBKEYS + PUT_KEYS + UTH_KEYS + [("poolw", g) for g in range(4)] + ["wglr", "nwb1", "Sb1"] + XS_KEYS, H_KEYS)
            for j in range(NTM):
                S.dma("sp", lambda e, j=j: e.dma_start(out=hh[:, j, :], in_=xin[128 + j * 128:128 + (j + 1) * 128, :]),
                      f"hld{j}", writes=[("h", j, fc) for fc in range(4)])
            N2_KEYS = [("n2T", j, h) for j in range(NTM) for h in range(2)]
            MAINB1 = [("qT", m, tb) for m in range(4) for tb in range(2)] + [("kT", m) for m in range(4)] + \
                     [("vv", t, h) for t in range(NTA) for h in range(2)] + ["glrT"]
            B_OLD = MAINB1 + MAINB2
            B_NEW = [("us2", 0), ("us2", 1), ("us2", 2), "nwb2", ("w2r", 0), ("w2r", 1), ("rtmp", 0), ("rtmp", 1)] + \
                    [("aT", sl, m, tb) for sl in range(2) for m in range(4) for tb in range(2)]
            S.fence(B_OLD, B_NEW)
            n2T = uT
            sp_load(nwb2, nw2.partition_broadcast(128), "nwb2")

            def p6_a(j):
                sl = j % 3
                norm_a(hh[:, j, :], [("h", j, fc) for fc in range(4)], nwb2, "nwb2", us2[sl], ("us2", sl))

            def p6_b(j):
                sl = j % 3
                bset = (0, 1) if j % 2 == 0 else (2, 3)
                S.fence([("mixr", j), ("mixg", j)], [("n2T", j, 0), ("n2T", j, 1)])
                norm_b(us2[sl], ("us2", sl), n2T, j * 128, [("n2T", j, 0), ("n2T", j, 1)], bset)

            rb = [0]

            def next_rot():
                rb[0] += 1
                return 4 + rb[0] % 4

            for cb in range(4):
                s = ring_load(w_out_v[:, :, cb * 512:(cb + 1) * 512])
                for j in range(NTM):
                    b = next_rot()
                    mm_group([(bank(b), mixedT[:, kc, j * 128:(j + 1) * 128], ring[s][:, kc, :], kc == 0, kc == 15)
                              for kc in range(16)],
                             reads=[("ring", s), ("mixg", j), ("mixr", j)] + [("mix", c, j // 4) for c in range(8, 16)],
                             writes=[("ps", b)])
                    S.op("dve", lambda e, j=j, cb=cb, b=b: e.tensor_tensor(
                        out=hh[:, j, cb * 512:(cb + 1) * 512], in0=bank(b), in1=hh[:, j, cb * 512:(cb + 1) * 512], op=ALU.add),
                        reads=[("ps", b), ("h", j, cb)], writes=[("h", j, cb)])
                    if cb == 3:
                        p6_a(j)
                        if j >= 2:
                            p6_b(j - 2)

            checkpoint("p6", {"n2T": (uT, N2_KEYS, [128, 16, TM], BF16)})
            NG = DFF // 512

            def mlp_p1(g):
                s = ring_load(w1_v[:, :, g * 512:(g + 1) * 512])
                a = aT[g % 2]
                for m in range(4):
                    bs_ = (0, 1) if (g * 4 + m) % 2 == 0 else (2, 3)
                    mms = []
                    for kc in range(16):
                        for tb in range(2):
                            mms.append((bank(bs_[tb]), ring[s][:, kc, m * 128:(m + 1) * 128],
                                        n2T[:, kc, tb * 512:(tb + 1) * 512], kc == 0, kc == 15))
                    mm_group(mms, reads=[("ring", s)] + N2_KEYS, writes=[("ps", bs_[0]), ("ps", bs_[1])])
                    for tb in range(2):
                        S.op("act", lambda e, tb=tb, b=bs_[tb]: e.activation(out=rtmp[tb], in_=bank(b), func=AF.Relu),
                             reads=[("ps", bs_[tb])], writes=[("rtmp", tb)])
                        S.op("act", lambda e, tb=tb, m=m, a=a: e.activation(out=a[:, m, tb * 512:(tb + 1) * 512],
                                                                             in_=rtmp[tb], func=AF.Square),
                             reads=[("rtmp", tb)], writes=[("aT", g % 2, m, tb)])

            def mlp_p1_first():
                s = ring_load(w1_v[:, :, 0:512])
                a = aT[0]
                for tb in range(2):
                    keys = [("n2T", j, h) for j in range(tb * 4, tb * 4 + 4) for h in range(2)]
                    if tb == 1:
                        p6_b(NTM - 2)
                        p6_b(NTM - 1)
                    for m in range(4):
                        b = 4 + m if tb == 0 else m
                        mm_group([(bank(b), ring[s][:, kc, m * 128:(m + 1) * 128], n2T[:, kc, tb * 512:(tb + 1) * 512],
                                   kc == 0, kc == 15) for kc in range(16)],
                                 reads=[("ring", s)] + keys, writes=[("ps", b)])
                        S.op("act", lambda e, tb=tb, b=b: e.activation(out=rtmp[tb], in_=bank(b), func=AF.Relu),
                             reads=[("ps", b)], writes=[("rtmp", tb)])
                        S.op("act", lambda e, tb=tb, m=m: e.activation(out=a[:, m, tb * 512:(tb + 1) * 512],
                                                                        in_=rtmp[tb], func=AF.Square),
                             reads=[("rtmp", tb)], writes=[("aT", 0, m, tb)])

            out_toks = []

            fin_c = {}

            def final_a(j):
                sl = j % 2
                hk = [("h", j, fc) for fc in range(4)]
                fin_c[j] = norm_a(hh[:, j, :], hk, None, None, us2[sl], ("us2", sl), apply=False)

            def final_b(j):
                sl = j % 2
                hk = [("h", j, fc) for fc in range(4)]
                c = fin_c[j]
                S.op("dve", lambda e: e.scalar_tensor_tensor(
                    out=ostage[sl], in0=hh[:, j, :], scalar=rs[:, c:c + 1], in1=nwb2, op0=ALU.mult, op1=ALU.mult),
                    reads=hk + [("rs", c), "nwb2"], writes=[("ost", sl)])
                out_toks.append(S.dma("sp", lambda e: e.dma_start(out=out[j * 128:(j + 1) * 128, :], in_=ostage[sl]),
                                      f"ost{sl}", reads=[("ost", sl)]))

            w2n = [0]

            def mlp_p2(g, last=False):
                s2 = w2n[0] % 2
                w2n[0] += 1
                S.dma("pool", lambda e, g=g, s2=s2: e.dma_start(out=w2r[s2], in_=w2_v[:, g * 4:(g + 1) * 4, :]),
                      f"w2r{s2}", writes=[("w2r", s2)])
                a = aT[g % 2]
                for j in range(NTM):
                    for fc in range(4):
                        b = 4 + (j * 4 + fc) % 4
                        mm_group([(bank(b), a[:, kc, j * 128:(j + 1) * 128], w2r[s2][:, kc, fc * 512:(fc + 1) * 512],
                                   kc == 0, kc == 3) for kc in range(4)],
                                 reads=[("w2r", s2)] + [("aT", g % 2, kc, j // 4) for kc in range(4)], writes=[("ps", b)])
                        S.op("dve", lambda e, j=j, fc=fc, b=b: e.tensor_tensor(
                            out=hh[:, j, fc * 512:(fc + 1) * 512], in0=bank(b), in1=hh[:, j, fc * 512:(fc + 1) * 512], op=ALU.add),
                            reads=[("ps", b), ("h", j, fc)], writes=[("h", j, fc)])
                    if last:
                        final_a(j)
                        if j >= 1:
                            final_b(j - 1)
                if last:
                    final_b(NTM - 1)

            def w2_load(g):
                s2 = w2n[0] % 2
                w2n[0] += 1
                S.dma("pool", lambda e: e.dma_start(out=w2r[s2], in_=w2_v[:, g * 4:(g + 1) * 4, :]),
                      f"w2r{s2}", writes=[("w2r", s2)])
                return s2

            def mlp_p2_last_pair(ga_, gb_, sa_, sb_):
                srcs = [(aT[ga_ % 2], w2r[sa_], ga_ % 2, sa_), (aT[gb_ % 2], w2r[sb_], gb_ % 2, sb_)]
                for j in range(NTM):
                    for fc in range(4):
                        b = 4 + (j * 4 + fc) % 4
                        mms, rd = [], []
                        for gi, (a, w, ap_, ws_) in enumerate(srcs):
                            for kc in range(4):
                                mms.append((bank(b), a[:, kc, j * 128:(j + 1) * 128], w[:, kc, fc * 512:(fc + 1) * 512],
                                            gi == 0 and kc == 0, gi == 1 and kc == 3))
                            rd += [("w2r", ws_)] + [("aT", ap_, kc, j // 4) for kc in range(4)]
                        mm_group(mms, reads=rd, writes=[("ps", b)])
                        S.op("dve", lambda e, j=j, fc=fc, b=b: e.tensor_tensor(
                            out=hh[:, j, fc * 512:(fc + 1) * 512], in0=bank(b), in1=hh[:, j, fc * 512:(fc + 1) * 512], op=ALU.add),
                            reads=[("ps", b), ("h", j, fc)], writes=[("h", j, fc)])
                    final_a(j)
                    if j >= 1:
                        final_b(j - 1)
                final_b(NTM - 1)

            S.fence([("us2", 2)], [("aT", 0, m, tb) for m in range(4) for tb in range(2)])
            mlp_p1_first()
            for g in range(NG):
                if g + 1 < NG:
                    mlp_p1(g + 1)
                if g < NG - 2:
                    mlp_p2(g)
                elif g == NG - 2:
                    s_pen = w2_load(g)
                else:
                    s_last = w2_load(g)
                    S.fence([("ring", 0), ("ring", 1)], [("ost", 0), ("ost", 1)])
                    sp_load(nwb2, nwf.partition_broadcast(128), "nwb2")
                    mlp_p2_last_pair(NG - 2, NG - 1, s_pen, s_last)
            checkpoint("p7", {"h2": (hh, H_KEYS, [128, NTM, D], F32)})
        except _Stop:
            pass
        for k_ in list(S.cnt):
            if not k_.startswith('E_'):
                S.wait('sp', (k_, S.cnt[k_]))
        S.emit()
    return nc


_CACHE = {}


def make_in_maps(x, meta_tokens, norm1_w, w_in, gate_w2, gate_b, gla_norm_w, pool_w, pool_scale,
                 w_out, norm2_w, mlp_w1, mlp_w2, final_norm_w, cores=None):
    f = lambda a: np.ascontiguousarray(np.asarray(a, dtype=np.float32))
    x = f(x); meta = f(meta_tokens)
    B = x.shape[0]
    cores = list(range(2 * B)) if cores is None else cores
    shared = {
        "w_in": f(w_in)[0], "w_out": f(w_out)[0], "w1": f(mlp_w1)[0], "w2": f(mlp_w2)[0],
        "pool_w": f(pool_w)[0],
        "gw2a": np.ascontiguousarray(np.concatenate([f(gate_w2)[0], f(gate_b)[0][None, :]], axis=0)),
        "gnw": np.ascontiguousarray(f(gla_norm_w)[0].reshape(2, 128).T),
        "psc": np.ascontiguousarray(f(pool_scale)[0].reshape(8, 128).T),
        "nw1": f(norm1_w)[0], "nw2": f(norm2_w)[0], "nwf": f(final_norm_w),
        "c_ident": np.eye(128, dtype=np.float32).astype(ml_dtypes.bfloat16),
        "c_mask4": np.ascontiguousarray(np.tile(np.triu(np.ones((128, 128), np.float32)), (1, 4))).astype(ml_dtypes.bfloat16),
    }
    in_maps = []
    for c in cores:
        b, half = divmod(c, 2)
        xin = np.zeros((TA, D), np.float32)
        fl = np.zeros((128, 2), np.float32)
        if half == 0:
            xin[112:128] = meta
            xin[128:] = x[b, 0:1024]
            fl[:, 0] = 1.0
        else:
            xin[112:128] = x[b, 1008:1024]
            xin[128:] = x[b, 1024:2048]
            fl[:, 1] = 1.0
        m = dict(shared)
        m["xin"] = xin
        m["flags"] = fl
        in_maps.append(m)
    return in_maps


def kernel(x, meta_tokens, norm1_w, w_in, gate_w2, gate_b, gla_norm_w, pool_w, pool_scale,
           w_out, norm2_w, mlp_w1, mlp_w2, final_norm_w):
    B = np.asarray(x).shape[0]
    n_cores = 2 * B
    in_maps = make_in_maps(x, meta_tokens, norm1_w, w_in, gate_w2, gate_b, gla_norm_w, pool_w, pool_scale,
                           w_out, norm2_w, mlp_w1, mlp_w2, final_norm_w)
    if "nc" not in _CACHE:
        _CACHE["nc"] = build_program()
    res = run_bass_kernel_spmd(_CACHE["nc"], in_maps, core_ids=list(range(n_cores)))
    outp = np.empty((B, 2048, D), np.float32)
    for c in range(n_cores):
        b, half = divmod(c, 2)
        outp[b, half * 1024:(half + 1) * 1024] = np.asarray(res.results[c]["out"], dtype=np.float32)
    return outp
```

```python
import numpy as np
import ml_dtypes
import concourse.bass as bass
import concourse.mybir as mybir
from concourse.bass_utils import run_bass_kernel_spmd
from contextlib import ExitStack

F32 = mybir.dt.float32
BF16 = mybir.dt.bfloat16
AF = mybir.ActivationFunctionType
ALU = mybir.AluOpType

ENGS = ("pe", "act", "dve", "pool", "sp")

D = 2048
NTM = 8
NTA = NTM + 1
TM, TA = NTM * 128, NTA * 128
DIN = 4112
DFF = 8192
EPS = 1e-6
POOL_W = (2, 4, 8, 16)


class Sched:
    def __init__(self, nc, es):
        self.nc, self.es = nc, es
        self.prog = {e: [] for e in ENGS}
        self.cnt = {}
        self.semh = {}
        self.waited = {}
        self.lastw = {}
        self.readers = {}

    def _sem(self, key):
        if key not in self.semh:
            self.semh[key] = self.es.enter_context(self.nc.semaphore("s_" + key))
            self.cnt[key] = 0
        return self.semh[key]

    def _wait(self, eng, tok):
        if tok is None:
            return
        key, val = tok
        if key == "E_pe" and eng == "pe":
            return
        if self.waited.get((eng, key), 0) >= val:
            return
        self.waited[(eng, key)] = val
        self.prog[eng].append(("wait", key, val))

    def _deps(self, eng, reads, writes):
        for k in reads:
            self._wait(eng, self.lastw.get(k))
        for k in writes:
            self._wait(eng, self.lastw.get(k))
            for sk, v in self.readers.get(k, {}).items():
                self._wait(eng, (sk, v))

    def _commit(self, tok, reads, writes):
        for k in reads:
            r = self.readers.setdefault(k, {})
            r[tok[0]] = max(r.get(tok[0], 0), tok[1])
        for k in writes:
            self.lastw[k] = tok
            self.readers[k] = {}

    def op(self, eng, fn, reads=(), writes=()):
        self._deps(eng, reads, writes)
        key = "E_" + eng
        self._sem(key)
        self.cnt[key] += 1
        tok = (key, self.cnt[key])
        self.prog[eng].append(("op", fn, key, 1))
        self._commit(tok, reads, writes)
        return tok

    def dma(self, eng, fn, semkey, reads=(), writes=()):
        self._deps(eng, reads, writes)
        self._sem(semkey)
        self.cnt[semkey] += 16
        tok = (semkey, self.cnt[semkey])
        self.prog[eng].append(("op", fn, semkey, 16))
        self._commit(tok, reads, writes)
        return tok

    def fence(self, old_keys, new_keys):
        acc = {}
        for k in old_keys:
            t = self.lastw.get(k)
            if t is not None:
                acc[t[0]] = max(acc.get(t[0], 0), t[1])
            for sk, v in self.readers.get(k, {}).items():
                acc[sk] = max(acc.get(sk, 0), v)
        for k in new_keys:
            r = self.readers.setdefault(k, {})
            for sk, v in acc.items():
                r[sk] = max(r.get(sk, 0), v)

    def wait(self, eng, tok):
        self._wait(eng, tok)

    def emit(self):
        nc = self.nc
        with nc.Block() as block:
            def replay(name, e):
                for it in self.prog[name]:
                    if it[0] == "wait":
                        e.wait_ge(self.semh[it[1]], it[2])
                    else:
                        ins = it[1](e)
                        ins.then_inc(self.semh[it[2]], it[3])

            @block.tensor
            def _(e):
                replay("pe", e)

            @block.scalar
            def _(e):
                replay("act", e)

            @block.vector
            def _(e):
                replay("dve", e)

            @block.gpsimd
            def _(e):
                replay("pool", e)

            @block.sync
            def _(e):
                replay("sp", e)


A_OFF = 0
B_OFF = 32768
C_OFF = B_OFF + 70912
D_OFF = C_OFF + 32768
E_OFF = D_OFF + 65536
E_SIZE = 10240
ARENA_BYTES = E_OFF + E_SIZE


class _Stop(Exception):
    pass


def build_program(stop=None, dumps=()):
    nc = bass.Bass("TRN2", target_bir_lowering=False)

    def din(name, shape, dt=F32):
        return nc.dram_tensor(name, list(shape), dt, kind="ExternalInput").ap()

    xin = din("xin", [TA, D])
    flags_d = din("flags", [128, 2])
    w_in = din("w_in", [D, DIN])
    gw2a_d = din("gw2a", [17, 512])
    gnw_d = din("gnw", [128, 2])
    psc_d = din("psc", [128, 8])
    pool_w = din("pool_w", [4, 256, 256])
    w_out = din("w_out", [D, D])
    nw1 = din("nw1", [D])
    nw2 = din("nw2", [D])
    nwf = din("nwf", [D])
    w1 = din("w1", [D, DFF])
    w2 = din("w2", [DFF, D])
    c_ident = din("c_ident", [128, 128], BF16)
    c_mask4 = din("c_mask4", [128, 512], BF16)
    out = nc.dram_tensor("out", [TM, D], F32, kind="ExternalOutput").ap()

    with ExitStack() as es:
        S = Sched(nc, es)

        def checkpoint(name, bufs):
            for bname, (ap, keys, shape, dt) in bufs.items():
                if bname in dumps:
                    d_ap = nc.dram_tensor("dbg_" + bname, list(shape), dt, kind="ExternalOutput").ap()
                    S.dma("sp", lambda e, d_ap=d_ap, ap=ap: e.dma_start(out=d_ap, in_=ap), "dbg_" + bname, reads=keys)
            if stop == name:
                raise _Stop()

        AR = es.enter_context(nc.sbuf_tensor("arena", [128, ARENA_BYTES // 4], F32))
        PS = es.enter_context(nc.psum_tensor("ps", [128, 4096], F32))

        def V(off, shape, dt=F32):
            n = int(np.prod(shape[1:]))
            isz = 4 if dt == F32 else 2
            assert off % 4 == 0 and (n * isz) % 4 == 0
            ap = AR[:, off // 4:(off + n * isz) // 4]
            if dt != F32:
                ap = ap.bitcast(dt)
            if len(shape) == 3:
                ap = ap.rearrange("p (a b) -> p a b", b=shape[2])
            elif len(shape) == 4:
                ap = ap.rearrange("p (a b c) -> p a b c", b=shape[2], c=shape[3])
            if shape[0] != 128:
                ap = ap[0:shape[0]]
            return ap

        def bank(b, n=512):
            return PS[:, b * 512:b * 512 + n]

        def bankbf(b):
            return PS[:, b * 512:(b + 1) * 512].bitcast(BF16)

        uT = V(A_OFF, [128, 16, TM], BF16)
        qT = V(B_OFF, [128, 4, TM], BF16)
        kT = V(B_OFF + 8192, [128, 4, TA], BF16)
        vv = V(B_OFF + 17408, [128, NTA, 1024], BF16)
        glrT = V(B_OFF + 35840, [17, TA], BF16)
        glrT_full = V(B_OFF + 35840, [128, TA], BF16)
        siluT = V(B_OFF + 38144, [128, 8, TM], BF16)
        yT = V(B_OFF + 54528, [128, 8, TM], BF16)
        us2 = [V(B_OFF + o_, [128, D], BF16) for o_ in (0, 4096, 16384)]
        nwb2 = V(B_OFF + 8192, [128, D], F32)
        aT = [V(B_OFF + 16384 + i * 8192, [128, 4, TM], BF16) for i in range(2)]
        w2r = [V(B_OFF + 32768 + i * 16384, [128, 4, D], BF16) for i in range(2)]
        rtmp = [V(B_OFF + 65536 + i * 2048, [128, 512], F32) for i in range(2)]
        ring = [V(C_OFF + i * 16384, [128, 16, 512], BF16) for i in range(2)]
        ostage = [V(C_OFF + i * 8192, [128, D], F32) for i in range(2)]
        NXS = 3
        xs = [V(D_OFF + 4096 + i * 8192, [128, D], F32) for i in range(NXS)]
        us = [V(D_OFF + 28672 + i * 4096, [128, D], BF16) for i in range(2)]
        nwb1 = V(D_OFF + 36864, [128, D], F32)
        put = [V(D_OFF + i * 4160, [128, 1040], F32) for i in range(3)]
        hh = V(D_OFF, [128, NTM, D], F32)

        def ga(par):
            o = D_OFF + 12480
            return dict(e=V(o + par * 2048, [128, 512]), L=V(o + 4096 + par * 2048, [128, 512]),
                        eN=V(o + 8192 + par * 2048, [128, 512]), eG=V(o + 12288, [128, 512]),
                        ki32=V(o + 14336, [128, 512]))
        Sf = V(D_OFF + 28864, [128, 1024])
        ki_t = V(D_OFF + 32960, [128, 512], BF16)
        keTt = V(D_OFF + 33984, [128, 512], BF16)
        Sb2 = V(D_OFF + 32960, [128, 1024], BF16)
        qd_all = V(D_OFF + 35008, [128, NTM, 512], BF16)
        sc_all = V(D_OFF + 43200, [128, NTM, 512], BF16)
        ke_all = V(D_OFF + 51392, [128, NTA, 512], BF16)
        dec_all = V(D_OFF + 60608, [128, NTA + 1, 4], F32)
        poolw = V(D_OFF + 60768, [128, 4, 2, 256], BF16)
        wglr = V(D_OFF + 64864, [128, 16, 16], BF16)

        def sbt(par):
            o = D_OFF + par * 14336
            return dict(sq=V(o, [128, 1024]), rstd=V(o + 4096, [128, 512]), A=V(o + 6144, [128, 1024]),
                        osb=V(o + 10240, [128, 1024]))

        eo = E_OFF
        ident = V(eo, [128, 128], BF16); eo += 256
        mask4 = V(eo, [128, 512], BF16); eo += 1024
        ones = V(eo, [128, 128]); eo += 512
        gw2a = V(eo, [17, 512], BF16); eo += 1024
        gnw = V(eo, [128, 2]); eo += 8
        psc = V(eo, [128, 8]); eo += 32
        pscw = V(eo, [128, 8]); eo += 32
        flags = V(eo, [128, 2]); eo += 8
        st = V(eo, [128, 64]); eo += 256
        rs = V(eo, [128, 64]); eo += 256
        Sb = V(eo, [128, 1024], BF16); eo += 2048
        uTtail = V(eo, [128, 16, 16], BF16); eo += 512
        uTh = V(eo, [128, 16, 128], BF16); eo += 4096
        assert eo <= E_OFF + E_SIZE, eo - E_OFF
        assert D_OFF + 64864 + 512 <= E_OFF

        s_bounce = nc.dram_tensor("s_bounce", [128, 1024], F32)
        s_gath = nc.dram_tensor("s_gath", [256, 1024], F32)

        w_in_v = w_in.rearrange("(kc p) n -> p kc n", p=128)
        w_out_v = w_out.rearrange("(kc p) n -> p kc n", p=128)
        w1_v = w1.rearrange("(kc p) n -> p kc n", p=128)
        w2_v = w2.rearrange("(kc p) n -> p kc n", p=128)

        def sp_load(dst, src, key):
            return S.dma("sp", lambda e: e.dma_start(out=dst, in_=src), "c_" + str(key), writes=[key])

        def copy_op(i, dst, src, reads, writes):
            if i % 2 == 0:
                return S.op("act", lambda e: e.copy(out=dst, in_=src), reads=reads, writes=writes)
            return S.op("dve", lambda e: e.tensor_copy(out=dst, in_=src), reads=reads, writes=writes)

        def mm_group(mms, reads, writes):
            def fn(e):
                last = None
                for (o, l, r, a, b) in mms:
                    last = e.matmul(o, l, r, start=a, stop=b)
                return last
            return S.op("pe", fn, reads=reads, writes=writes)

        def tr_group(trs, reads, writes):
            def fn(e):
                last = None
                for (o, i) in trs:
                    last = e.transpose(out=o, in_=i, identity=ident)
                return last
            return S.op("pe", fn, reads=reads + ["ident"], writes=writes)

        try:
            sp_load(nwb1, nw1.partition_broadcast(128), "nwb1")
            S.op("dve", lambda e: e.memset(ones, 1.0), writes=["ones"])
            S.op("dve", lambda e: e.memset(st, 0.0), writes=["st"])
            ring_n = [0]

            def ring_load(src_ap, ncols=512):
                if ring_n[0] == 0:
                    S.wait("pool", x_toks[3])
                s = ring_n[0] % 2
                ring_n[0] += 1
                dst = ring[s] if ncols == 512 else ring[s][:, :, 0:ncols]
                S.dma("pool", lambda e: e.dma_start(out=dst, in_=src_ap), f"ring{s}", writes=[("ring", s)])
                return s

            stat_col = [0]
            x_toks = {}

            def norm_a(src, src_keys, nwb, nwb_key, stg, stg_key, apply=True):
                c = stat_col[0]
                stat_col[0] += 1
                S.op("act", lambda e: e.activation(out=stg, in_=src, func=AF.Square, scale=float(D ** -0.5),
                                                   accum_out=st[:, c:c + 1]),
                     reads=src_keys + ["st"], writes=[stg_key, ("st", c)])
                S.op("act", lambda e: e.activation(out=rs[:, c:c + 1], in_=st[:, c:c + 1], func=AF.Ln, bias=EPS),
                     reads=[("st", c)], writes=[("rs", c)])
                S.op("act", lambda e: e.activation(out=rs[:, c:c + 1], in_=rs[:, c:c + 1], func=AF.Exp, scale=-0.5),
                     reads=[("rs", c)], writes=[("rs", c)])
                if apply:
                    S.op("dve", lambda e: e.scalar_tensor_tensor(out=stg, in0=src, scalar=rs[:, c:c + 1], in1=nwb,
                                                                 op0=ALU.mult, op1=ALU.mult),
                         reads=src_keys + [("rs", c), nwb_key], writes=[stg_key])
                return c

            def norm_b(stg, stg_key, dstT, dcol, dkeys, bset):
                for half in range(2):
                    b = bset[half]
                    tr_group([(bankbf(b)[:, i * 128:(i + 1) * 128], stg[:, (half * 8 + i) * 128:(half * 8 + i + 1) * 128])
                              for i in range(8)], reads=[stg_key], writes=[("ps", b)])
                    copy_op(half, dstT[:, half * 8:half * 8 + 8, dcol:dcol + 128],
                            bankbf(b).rearrange("p (a b) -> p a b", b=128),
                            reads=[("ps", b)], writes=[dkeys[half]])

            def p0_a(t):
                sl = t % 2
                xl = t % NXS
                x_toks[t] = S.dma("sp", lambda e: e.dma_start(out=xs[xl], in_=xin[t * 128:(t + 1) * 128, :]),
                                  f"xs{xl}", writes=[("xs", xl)])
                norm_a(xs[xl], [("xs", xl)], nwb1, "nwb1", us[sl], ("us", sl))

            def p0_b(t):
                sl = t % 2
                bset = (0, 1) if t % 2 == 0 else (2, 3)
                if t == 0:
                    norm_b(us[sl], ("us", sl), uTh, 0, [("uTh", 0), ("uTh", 1)], bset)
                else:
                    norm_b(us[sl], ("us", sl), uT, (t - 1) * 128, [("uT", t - 1, 0), ("uT", t - 1, 1)], bset)

            XS_KEYS = [("xs", i) for i in range(NXS)] + [("us", 0), ("us", 1)]
            UTH_KEYS = [("uTh", 0), ("uTh", 1)]
            UT_KEYS = [("uT", j, h) for j in range(NTM) for h in range(2)]
            p0_a(0)
            sp_load(ident, c_ident, "ident")
            S.dma("pool", lambda e: e.dma_start(out=gw2a, in_=gw2a_d), "gw2a", writes=["gw2a"])
            S.dma("pool", lambda e: e.dma_start(out=wglr, in_=w_in_v[:, :, 3072:3088]), "wglr", writes=["wglr"])
            for t in range(NTA):
                if t + 1 < NTA:
                    p0_a(t + 1)
                p0_b(t)
            sp_load(mask4, c_mask4, "mask4")
            sp_load(gnw, gnw_d, "gnw")
            sp_load(psc, psc_d, "psc")
            sp_load(flags, flags_d, "flags")
            for g in range(4):
                S.op("dve", lambda e, g=g: e.tensor_scalar(out=pscw[:, 2 * g:2 * g + 2], in0=psc[:, 2 * g:2 * g + 2],
                                                           scalar1=1.0 / POOL_W[g], scalar2=None, op0=ALU.mult),
                     reads=["psc"], writes=[("pscw", g)])
            S.op("dve", lambda e: e.tensor_copy(out=uTtail, in_=uTh[:, :, 112:128]), reads=UTH_KEYS, writes=["uTtail"])
            checkpoint("p0", {"uT": (uT, UT_KEYS, [128, 16, TM], BF16)})

            chunk_ctr = [0]

            def next_bset():
                chunk_ctr[0] += 1
                return (0, 1, 2) if chunk_ctr[0] % 2 else (3, 4, 5)

            tm_ctr = [0]

            def next_tm_bank():
                tm_ctr[0] += 1
                return 3 + tm_ctr[0] % 5

            ev = [0]

            def fm_mms(w_ap_fn, bs_, with_head, tail=False):
                mms = []
                for kc in range(16):
                    for tb in range(2):
                        mms.append((bank(bs_[tb]), w_ap_fn(kc), uT[:, kc, tb * 512:(tb + 1) * 512], kc == 0, kc == 15))
                    if with_head:
                        mms.append((bank(bs_[2], 128), w_ap_fn(kc), uTh[:, kc, :], kc == 0, kc == 15))
                    if tail:
                        mms.append((bank(bs_[2], 16), w_ap_fn(kc), uTtail[:, kc, :], kc == 0, kc == 15))
                return mms

            def gates(T, K, glr_ap, glr_key, bg, dec_ap, dkey, want_eG, head):
                mm_group([(bank(bg)[:, h * 128:(h + 1) * 128], gw2a[0:17, h * 128:(h + 1) * 128], glr_ap, True, True)
                          for h in range(4)], reads=["gw2a", glr_key], writes=[("ps", bg)])
                S.op("act", lambda e: e.activation(out=T["e"], in_=bank(bg), func=AF.Exp, scale=-1.0),
                     reads=[("ps", bg)], writes=[K + "e"])
                S.op("act", lambda e: e.activation(out=T["e"], in_=T["e"], func=AF.Ln, bias=1.0),
                     reads=[K + "e"], writes=[K + "e"])
                if head:
                    S.op("dve", lambda e: e.tensor_scalar(out=T["e"], in0=T["e"], scalar1=flags[:, 0:1], scalar2=None,
                                                          op0=ALU.mult), reads=[K + "e", "flags"], writes=[K + "e"])
                for h in range(4):
                    S.op("dve", lambda e, h=h: e.tensor_tensor_scan(out=T["L"][:, h * 128:(h + 1) * 128],
                                                                     data0=T["e"][:, h * 128:(h + 1) * 128],
                                                                     data1=T["e"][:, h * 128:(h + 1) * 128],
                                                                     initial=0.0, op0=ALU.add, op1=ALU.bypass),
                         reads=[K + "e"], writes=[K + "L%d" % h])
                Lk = [K + "L%d" % h for h in range(4)]
                S.op("act", lambda e: e.activation(out=T["eN"], in_=T["L"], func=AF.Exp, scale=1.0 / 16),
                     reads=Lk, writes=[K + "eN"])
                S.op("act", lambda e: e.activation(out=dec_ap, in_=T["L"].rearrange("p (h t) -> p h t", t=128)[:, :, 127],
                                                   func=AF.Exp, scale=-1.0 / 16),
                     reads=Lk, writes=[dkey])
                if want_eG:
                    S.op("act", lambda e: e.activation(out=T["eG"], in_=T["L"], func=AF.Exp, scale=-1.0 / 16),
                         reads=Lk, writes=["ga_eG"])

            def ke_transpose(src, src_keys, bk, dst, dkey):
                tr_group([(bankbf(bk)[:, h * 128:(h + 1) * 128], src[:, h * 128:(h + 1) * 128]) for h in range(4)],
                         reads=src_keys, writes=[("ps", bk)])
                S.op("act", lambda e: e.copy(out=dst, in_=bankbf(bk)[:, 0:512]), reads=[("ps", bk)], writes=[dkey])

            def su_mm(t, bd):
                mm_group([(PS[:, bd * 512 + h * 256:bd * 512 + (h + 1) * 256], ke_all[:, t, h * 128:(h + 1) * 128],
                           vv[:, t, h * 256:(h + 1) * 256], True, True) for h in range(4)],
                         reads=[("ke", t), ("vv", t, 0), ("vv", t, 1)], writes=[("ps", bd), ("ps", bd + 1)])

            def su_stt(t, bd):
                for h in range(4):
                    S.op("dve", lambda e, h=h: e.scalar_tensor_tensor(
                        out=Sf[:, h * 256:(h + 1) * 256], in0=Sf[:, h * 256:(h + 1) * 256], scalar=dec_all[:, t, h:h + 1],
                        in1=PS[:, bd * 512 + h * 256:bd * 512 + (h + 1) * 256], op0=ALU.mult, op1=ALU.add),
                        reads=["Sf", ("dec", t), ("ps", bd), ("ps", bd + 1)], writes=["Sf"])

            def state_update(t, bd):
                su_mm(t, bd)
                su_stt(t, bd)

            GAKEYS = ["ki_t", "keTt", "ga_eG", "ga_ki32", "Sf", ("dec", NTA)] + \
                     [(n, t) for n in ("qd", "sc", "ke", "dec") for t in range(NTA)]
            for par in range(2):
                GAKEYS += ["ga%d_" % par + n for n in ["e", "L0", "L1", "L2", "L3", "eN"]]
            S.fence(XS_KEYS + ["nwb1"], GAKEYS)
            def stage_a1(t, bg):
                par = t % 2
                T = ga(par)
                K = "ga%d_" % par
                kcols = slice(t * 128, (t + 1) * 128)
                head = (t == 0)
                gates(T, K, glrT[0:17, kcols], "glrT", bg, dec_all[:, t, :], ("dec", t), not head, head)
                eN3 = T["eN"].rearrange("p (h t) -> p h t", t=128)
                if not head:
                    j = t - 1
                    qcols = slice(j * 128, (j + 1) * 128)
                    eG3 = T["eG"].rearrange("p (h t) -> p h t", t=128)
                    S.op("dve", lambda e: e.scalar_tensor_tensor(
                        out=qd_all[:, j, :].rearrange("p (h t) -> p h t", t=128), in0=qT[:, :, qcols],
                        scalar=float(128 ** -0.5), in1=eG3, op0=ALU.mult, op1=ALU.mult),
                        reads=[("qT", m, j // 4) for m in range(4)] + ["ga_eG"], writes=[("qd", t)])
                S.op("dve", lambda e: e.tensor_tensor(
                    out=T["ki32"].rearrange("p (h t) -> p h t", t=128), in0=kT[:, :, kcols], in1=eN3, op=ALU.mult),
                    reads=[("kT", m) for m in range(4)] + [K + "eN"], writes=["ga_ki32"])
                dsc, dsk = dec_all[:, t, :], ("dec", t)
                if head:
                    S.op("dve", lambda e: e.tensor_scalar(out=dec_all[:, NTA, :], in0=dec_all[:, 0, :], scalar1=flags[:, 0:1],
                                                          scalar2=None, op0=ALU.mult),
                         reads=[("dec", 0), "flags"], writes=[("dec", NTA)])
                    dsc, dsk = dec_all[:, NTA, :], ("dec", NTA)
                else:
                    S.op("act", lambda e: e.copy(out=ki_t, in_=T["ki32"]), reads=["ga_ki32"], writes=["ki_t"])
                for h in range(4):
                    S.op("dve", lambda e, h=h: e.tensor_scalar(
                        out=keTt[:, h * 128:(h + 1) * 128], in0=T["ki32"][:, h * 128:(h + 1) * 128],
                        scalar1=dsc[:, h:h + 1], scalar2=None, op0=ALU.mult),
                        reads=["ga_ki32", dsk], writes=["keTt"])

            def stage_a2(t, bk, bsx):
                head = (t == 0)
                if not head:
                    j = t - 1
                    mm_group([(bank(bsx)[:, h * 128:(h + 1) * 128], ki_t[:, h * 128:(h + 1) * 128],
                               qd_all[:, j, h * 128:(h + 1) * 128], True, True) for h in range(4)],
                             reads=["ki_t", ("qd", t)], writes=[("ps", bsx)])
                    S.op("dve", lambda e: e.tensor_tensor(out=sc_all[:, j, :], in0=bank(bsx), in1=mask4, op=ALU.mult),
                         reads=[("ps", bsx), "mask4"], writes=[("sc", t)])
                ke_transpose(keTt, ["keTt"], bk, ke_all[:, t, :], ("ke", t))

            sa_state = {"next": 0, "pend": None}

            def sa_hook(bg, bsx, bk=None):
                if sa_state["pend"] is not None:
                    stage_a2(sa_state["pend"], bg if bk is None else bk, bsx)
                    sa_state["pend"] = None
                if sa_state["next"] < NTA:
                    stage_a1(sa_state["next"], bg)
                    sa_state["pend"] = sa_state["next"]
                    sa_state["next"] += 1

            S.op("dve", lambda e: e.memset(glrT_full, 1.0), writes=["glrT"])
            bs_ = next_bset()
            mms = []
            for kc in range(16):
                for tb in range(2):
                    mms.append((PS[0:16, bs_[tb] * 512:(bs_[tb] + 1) * 512], wglr[:, kc, :],
                                uT[:, kc, tb * 512:(tb + 1) * 512], kc == 0, kc == 15))
                mms.append((PS[0:16, bs_[2] * 512:bs_[2] * 512 + 128], wglr[:, kc, :], uTh[:, kc, :], kc == 0, kc == 15))
            mm_group(mms, reads=["wglr"] + UT_KEYS + UTH_KEYS, writes=[("ps", b) for b in bs_])
            for tb in range(2):
                ev[0] += 1
                copy_op(ev[0], glrT[0:16, 128 + tb * 512:128 + (tb + 1) * 512], PS[0:16, bs_[tb] * 512:(bs_[tb] + 1) * 512],
                        reads=[("ps", bs_[tb])], writes=["glrT"])
            ev[0] += 1
            copy_op(ev[0], glrT[0:16, 0:128], PS[0:16, bs_[2] * 512:bs_[2] * 512 + 128], reads=[("ps", bs_[2])], writes=["glrT"])

            s = ring_load(w_in_v[:, :, 512:1024])
            for m in range(4):
                bs_ = next_bset()
                mm_group(fm_mms(lambda kc, m=m, s=s: ring[s][:, kc, m * 128:(m + 1) * 128], bs_, True),
                         reads=[("ring", s)] + UT_KEYS + UTH_KEYS, writes=[("ps", b) for b in bs_])
                for tb in range(2):
                    S.op("act" if tb == 0 else "dve",
                         (lambda e, m=m, tb=tb, b=bs_[tb]: e.copy(out=kT[:, m, 128 + tb * 512:128 + (tb + 1) * 512], in_=bank(b)))
                         if tb == 0 else
                         (lambda e, m=m, tb=tb, b=bs_[tb]: e.tensor_copy(out=kT[:, m, 128 + tb * 512:128 + (tb + 1) * 512], in_=bank(b))),
                         reads=[("ps", bs_[tb])], writes=[("kT", m)])
                S.op("act", lambda e, m=m, b=bs_[2]: e.copy(out=kT[:, m, 0:128], in_=bank(b, 128)),
                     reads=[("ps", bs_[2])], writes=[("kT", m)])

            def fm_block(col0, evac, hook=None, slot=None):
                s = ring_load(w_in_v[:, :, col0:col0 + 512]) if slot is None else slot
                for m in range(4):
                    bs_ = next_bset()
                    mm_group(fm_mms(lambda kc, m=m, s=s: ring[s][:, kc, m * 128:(m + 1) * 128], bs_, False),
                             reads=[("ring", s)] + UT_KEYS, writes=[("ps", bs_[0]), ("ps", bs_[1])])
                    evac(m, bs_)
                    if hook is not None:
                        hook(m)

            def ev_copy(dst, name):
                def f(m, bs_):
                    for tb in range(2):
                        ev[0] += 1
                        copy_op(ev[0], dst[:, m, tb * 512:(tb + 1) * 512], bank(bs_[tb]),
                                reads=[("ps", bs_[tb])], writes=[(name, m, tb)])
                return f

            fm_block(0, ev_copy(qT, "qT"))
            for g in range(4):
                S.dma("pool", lambda e, g=g: e.dma_start(out=poolw[:, g], in_=pool_w[g].rearrange("(cc p) d -> p cc d", p=128)),
                      "poolw%d" % g, writes=[("poolw", g)])
            MAINB2 = [("siluT", c, tb) for c in range(8) for tb in range(2)] + [("yT", c) for c in range(8)]
            PUT_KEYS = [("put", i) for i in range(3)]
            for pb in range(2):
                s = ring_load(w_in_v[:, :, 3088 + pb * 512:3088 + (pb + 1) * 512])
                for m in range(4):
                    c = pb * 4 + m
                    g = c // 2
                    w = POOL_W[g]
                    bs_ = next_bset()
                    mm_group(fm_mms(lambda kc, m=m, s=s: ring[s][:, kc, m * 128:(m + 1) * 128], bs_, False, tail=True),
                             reads=[("ring", s), "uTtail"] + UT_KEYS, writes=[("ps", b) for b in bs_])
                    S.op("act", lambda e, b=bs_[2]: e.copy(out=put[0][:, 0:16], in_=bank(b, 16)),
                         reads=[("ps", bs_[2])], writes=[("put", 0)])
                    S.op("act", lambda e, b=bs_[0]: e.copy(out=put[0][:, 16:528], in_=bank(b)),
                         reads=[("ps", bs_[0])], writes=[("put", 0)])
                    S.op("act", lambda e, b=bs_[1]: e.copy(out=put[0][:, 528:1040], in_=bank(b)),
                         reads=[("ps", bs_[1])], writes=[("put", 0)])
                    if w == 2:
                        S.op("dve", lambda e, c=c: e.tensor_tensor(out=yT[:, c, :], in0=put[0][:, 15:1039], in1=put[0][:, 16:1040],
                                                                   op=ALU.subtract),
                             reads=[("put", 0)], writes=[("yT", c)])
                    else:
                        src, cur_i, sh, lo = put[0], 1, 1, 1
                        srckey = ("put", 0)
                        while sh < w:
                            dst = put[cur_i]
                            S.op("dve", lambda e, src=src, dst=dst, sh=sh, lo=lo: e.tensor_tensor(
                                out=dst[:, lo:1040], in0=src[:, lo:1040], in1=src[:, lo - sh:1040 - sh], op=ALU.add),
                                reads=[srckey], writes=[("put", cur_i)])
                            src, srckey = dst, ("put", cur_i)
                            cur_i = 3 - cur_i
                            sh *= 2
                            lo = 2 * sh - 1
                        S.op("dve", lambda e, c=c, src=src, w=w: e.scalar_tensor_tensor(
                            out=yT[:, c, :], in0=put[0][:, 16:1040], scalar=-float(w), in1=src[:, 16:1040],
                            op0=ALU.mult, op1=ALU.add),
                            reads=[("put", 0), srckey], writes=[("yT", c)])
                    if c % 2 == 1:
                        sa_hook(6, 7)

            for half in range(2):
                s = ring_load(w_in_v[:, :, 1024 + half * 512:1024 + (half + 1) * 512])
                for t in range(NTA):
                    b = next_tm_bank()
                    lh = (lambda kc: uTh[:, kc, :]) if t == 0 else (lambda kc, t=t: uT[:, kc, (t - 1) * 128:t * 128])
                    rk = UTH_KEYS if t == 0 else [("uT", t - 1, 0), ("uT", t - 1, 1)]
                    mm_group([(bank(b), lh(kc), ring[s][:, kc, :], kc == 0, kc == 15) for kc in range(16)],
                             reads=[("ring", s)] + rk, writes=[("ps", b)])
                    ev[0] += 1
                    copy_op(ev[0], vv[:, t, half * 512:(half + 1) * 512], bank(b),
                            reads=[("ps", b)], writes=[("vv", t, half)])
                    if t in (2, 5, 8):
                        sa_hook(0, 1, 2)
                    if half == 1 and t >= 2:
                        if t == 2:
                            S.op("dve", lambda e: e.memset(Sf, 0.0), writes=["Sf"])
                        state_update(t - 2, 0)

            while sa_state["pend"] is not None or sa_state["next"] < NTA:
                sa_hook(0, 1, 2)
            r_slots = [ring_load(w_in_v[:, :, 2048 + rb * 512:2048 + (rb + 1) * 512]) for rb in range(2)]
            state_update(NTA - 2, 0)
            state_update(NTA - 1, 0)
            S.dma("sp", lambda e: e.dma_start(out=s_bounce.ap(), in_=Sf), "sbo", reads=["Sf"], writes=["s_bounce"])
            S._deps("pool", ["s_bounce"], ["s_gath"])
            S._sem("cc")
            S.cnt["cc"] += 1
            cc_tok = ("cc", S.cnt["cc"])
            S.prog["pool"].append(("op", lambda e: e.collective_compute(
                "AllGather", ALU.bypass, replica_groups=[[0, 1], [2, 3], [4, 5], [6, 7]],
                ins=[s_bounce.ap()], outs=[s_gath.ap()]), "cc", 1))
            S._commit(cc_tok, ["s_bounce"], ["s_gath"])
            S.dma("sp", lambda e: e.dma_start(out=Sf, in_=s_gath.ap()[0:128, :]), "sgi", reads=["s_gath"], writes=["Sf"])
            S.op("dve", lambda e: e.tensor_scalar(out=Sf, in0=Sf, scalar1=flags[:, 1:2], scalar2=None, op0=ALU.mult),
                 reads=["Sf", "flags"], writes=["Sf"])
            state_update(0, 6)
            S.op("dve", lambda e: e.tensor_copy(out=Sb, in_=Sf), reads=["Sf"], writes=["Sb0"])

            for rb in range(2):
                def ev_silu(m, bs_, rb=rb):
                    for tb in range(2):
                        S.op("act", lambda e, m=m, tb=tb, b=bs_[tb]: e.activation(
                            out=siluT[:, rb * 4 + m, tb * 512:(tb + 1) * 512], in_=bank(b), func=AF.Silu),
                            reads=[("ps", bs_[tb])], writes=[("siluT", rb * 4 + m, tb)])
                fm_block(2048 + rb * 512, ev_silu, slot=r_slots[rb])

            checkpoint("p3", {"qT": (qT, [("qT", m, tb) for m in range(4) for tb in range(2)], [128, 4, TM], BF16),
                              "siluT": (siluT, [("siluT", c, tb) for c in range(8) for tb in range(2)], [128, 8, TM], BF16),
                              "yT": (yT, [("yT", c) for c in range(8)], [128, 8, TM], BF16)})

            MIX_KEYS = [("mix", c, tb) for c in range(16) for tb in range(2)] + [("mixg", j) for j in range(NTM)]
            S.fence(UT_KEYS, MIX_KEYS)
            mixedT = uT
            for g in range(4):
                for dc in range(2):
                    bs_ = (0, 1) if (g * 2 + dc) % 2 == 0 else (2, 3)
                    c = g * 2 + dc
                    for tb in range(2):
                        mm_group([(bank(bs_[tb]), poolw[:, g, cc, dc * 128:(dc + 1) * 128],
                                   yT[:, g * 2 + cc, tb * 512:(tb + 1) * 512], cc == 0, cc == 1) for cc in range(2)],
                                 reads=[("poolw", g), ("yT", g * 2), ("yT", g * 2 + 1)], writes=[("ps", bs_[tb])])
                        S.op("dve", lambda e, c=c, tb=tb, b=bs_[tb]: e.tensor_scalar(
                            out=mixedT[:, 8 + c, tb * 512:(tb + 1) * 512], in0=bank(b), scalar1=pscw[:, c:c + 1],
                            scalar2=None, op0=ALU.mult),
                            reads=[("ps", bs_[tb]), ("pscw", g)], writes=[("mix", 8 + c, tb)])

            SBKEYS = []
            for par in range(2):
                SBKEYS += ["sb%d_" % par + n for n in ["sq0", "sq1", "rstd", "A0", "A1", "osb0", "osb1"]]
            S.fence(PUT_KEYS + UTH_KEYS + [k_ for k_ in GAKEYS if k_ != "Sf" and not (isinstance(k_, tuple) and k_[0] in ("qd", "sc", "ke", "dec", "poolw"))] + XS_KEYS, SBKEYS)

            Sbs = [Sb, Sb2]
            S.fence(["ki_t", "keTt"], ["Sb1"])

            def p4_main(j):
                t = j + 1
                par = j % 2
                bo = 0 if par == 0 else 2
                Scur, Snxt = Sbs[par], Sbs[1 - par]
                if j < NTM - 1:
                    su_mm(t, 4)
                mms = []
                for h in range(4):
                    for vc in range(2):
                        o_ap = PS[:, bo * 512 + (h * 2 + vc) * 128:bo * 512 + (h * 2 + vc + 1) * 128]
                        mms.append((o_ap, vv[:, t, h * 256 + vc * 128:h * 256 + (vc + 1) * 128],
                                    sc_all[:, j, h * 128:(h + 1) * 128], True, False))
                        mms.append((o_ap, Scur[:, h * 256 + vc * 128:h * 256 + (vc + 1) * 128],
                                    qd_all[:, j, h * 128:(h + 1) * 128], False, True))
                mm_group(mms, reads=[("sc", t), ("qd", t), "Sb%d" % par, ("vv", t, 0), ("vv", t, 1)],
                         writes=[("ps", bo), ("ps", bo + 1)])
                if j < NTM - 1:
                    su_stt(t, 4)
                    S.op("dve", lambda e: e.tensor_copy(out=Snxt, in_=Sf), reads=["Sf"], writes=["Sb%d" % (1 - par)])
                T = sbt(par)
                K = "sb%d_" % par
                for hb in range(2):
                    S.op("act", lambda e, hb=hb: e.activation(out=T["sq"][:, hb * 512:(hb + 1) * 512],
                                                               in_=bank(bo + hb), func=AF.Square),
                         reads=[("ps", bo + hb)], writes=[K + "sq%d" % hb])
                    S.op("act", lambda e, hb=hb: e.copy(out=T["osb"][:, hb * 512:(hb + 1) * 512], in_=bank(bo + hb)),
                         reads=[("ps", bo + hb)], writes=[K + "osb%d" % hb])

            def p4_post(j):
                par = j % 2
                bg = 6 + par
                T = sbt(par)
                K = "sb%d_" % par
                mms = []
                for h in range(4):
                    for vc in range(2):
                        mms.append((bank(bg)[:, h * 128:(h + 1) * 128], ones,
                                    T["sq"][:, (h * 2 + vc) * 128:(h * 2 + vc + 1) * 128], vc == 0, vc == 1))
                mm_group(mms, reads=["ones", K + "sq0", K + "sq1"], writes=[("ps", bg)])
                S.op("act", lambda e: e.activation(out=T["rstd"], in_=bank(bg), func=AF.Ln, scale=1.0 / 256, bias=EPS),
                     reads=[("ps", bg)], writes=[K + "rstd"])
                S.op("act", lambda e: e.activation(out=T["rstd"], in_=T["rstd"], func=AF.Exp, scale=-0.5),
                     reads=[K + "rstd"], writes=[K + "rstd"])

            def p4_fin(j):
                par = j % 2
                T = sbt(par)
                K = "sb%d_" % par
                cols = slice(j * 128, (j + 1) * 128)
                r3 = T["rstd"].rearrange("p (h t) -> p h t", t=128)
                A4 = T["A"].rearrange("p (h v t) -> p h v t", v=2, t=128)
                o4 = T["osb"].rearrange("p (h v t) -> p h v t", v=2, t=128)
                silu4 = siluT[:, :, cols].rearrange("p (h v) t -> p h v t", v=2)
                mix4 = mixedT[:, 0:8, cols].rearrange("p (h v) t -> p h v t", v=2)
                for vc in range(2):
                    S.op("dve", lambda e, vc=vc: e.tensor_tensor(
                        out=A4[:, :, vc, :], in0=silu4[:, :, vc, :], in1=r3, op=ALU.mult),
                        reads=[("siluT", c, j // 4) for c in range(8)] + [K + "rstd"], writes=[K + "A%d" % vc])
                for vc in range(2):
                    S.op("dve", lambda e, vc=vc: e.scalar_tensor_tensor(
                        out=mix4[:, :, vc, :], in0=o4[:, :, vc, :], scalar=gnw[:, vc:vc + 1], in1=A4[:, :, vc, :],
                        op0=ALU.mult, op1=ALU.mult),
                        reads=[K + "osb0", K + "osb1", "gnw", K + "A%d" % vc], writes=[("mixg", j)])

            for j in range(NTM):
                p4_main(j)
                if j >= 1:
                    p4_post(j - 1)
                    p4_fin(j - 1)
            p4_post(NTM - 1)
            p4_fin(NTM - 1)

            checkpoint("p4", {"mixedT": (uT, MIX_KEYS, [128, 16, TM], BF16)})
            H_KEYS = [("h", j, fc) for j in range(NTM) for fc in range(4)]
            S.fence(GAKEYS + SBKEYS + PUT_KEYS + UTH_KEYS + [("poolw", g) for g in range(4)] + ["wglr", "nwb1", "Sb1"] + XS_KEYS, H_KEYS)
            for j in range(NTM):
                S.dma("sp", lambda e, j=j: e.dma_start(out=hh[:, j, :], in_=xin[128 + j * 128:128 + (j + 1) * 128, :]),
                      f"hld{j}", writes=[("h", j, fc) for fc in range(4)])
            N2_KEYS = [("n2T", j, h) for j in range(NTM) for h in range(2)]
            MAINB1 = [("qT", m, tb) for m in range(4) for tb in range(2)] + [("kT", m) for m in range(4)] + \
                     [("vv", t, h) for t in range(NTA) for h in range(2)] + ["glrT"]
            B_OLD = MAINB1 + MAINB2
            B_NEW = [("us2", 0), ("us2", 1), ("us2", 2), "nwb2", ("w2r", 0), ("w2r", 1), ("rtmp", 0), ("rtmp", 1)] + \
                    [("aT", sl, m, tb) for sl in range(2) for m in range(4) for tb in range(2)]
            S.fence(B_OLD, B_NEW)
            n2T = uT
            sp_load(nwb2, nw2.partition_broadcast(128), "nwb2")

            p6_c = {}

            def p6_a(j):
                sl = j % 3
                p6_c[j] = norm_a(hh[:, j, :], [("h", j, fc) for fc in range(4)], nwb2, "nwb2", us2[sl], ("us2", sl),
                                 apply=False)

            def p6_apply(j):
                sl = j % 3
                c = p6_c[j]
                S.op("dve", lambda e: e.scalar_tensor_tensor(out=us2[sl], in0=hh[:, j, :], scalar=rs[:, c:c + 1], in1=nwb2,
                                                             op0=ALU.mult, op1=ALU.mult),
                     reads=[("h", j, fc) for fc in range(4)] + [("rs", c), "nwb2"], writes=[("us2", sl)])

            def p6_b(j):
                sl = j % 3
                bset = (0, 1) if j % 2 == 0 else (2, 3)
                S.fence([("mixr", j), ("mixg", j)], [("n2T", j, 0), ("n2T", j, 1)])
                norm_b(us2[sl], ("us2", sl), n2T, j * 128, [("n2T", j, 0), ("n2T", j, 1)], bset)

            rb = [0]

            def next_rot():
                rb[0] += 1
                return 4 + rb[0] % 4

            for cb in range(4):
                s = ring_load(w_out_v[:, :, cb * 512:(cb + 1) * 512])
                for j in range(NTM):
                    b = next_rot()
                    mm_group([(bank(b), mixedT[:, kc, j * 128:(j + 1) * 128], ring[s][:, kc, :], kc == 0, kc == 15)
                              for kc in range(16)],
                             reads=[("ring", s), ("mixg", j), ("mixr", j)] + [("mix", c, j // 4) for c in range(8, 16)],
                             writes=[("ps", b)])
                    S.op("dve", lambda e, j=j, cb=cb, b=b: e.tensor_tensor(
                        out=hh[:, j, cb * 512:(cb + 1) * 512], in0=bank(b), in1=hh[:, j, cb * 512:(cb + 1) * 512], op=ALU.add),
                        reads=[("ps", b), ("h", j, cb)], writes=[("h", j, cb)])
                    if cb == 3:
                        p6_a(j)
                        if j >= 1:
                            p6_apply(j - 1)
                        if j >= 2:
                            p6_b(j - 2)
            p6_apply(NTM - 1)

            checkpoint("p6", {"n2T": (uT, N2_KEYS, [128, 16, TM], BF16)})
            NG = DFF // 512

            def mlp_p1(g):
                s = ring_load(w1_v[:, :, g * 512:(g + 1) * 512])
                a = aT[g % 2]
                for m in range(4):
                    bs_ = (0, 1) if (g * 4 + m) % 2 == 0 else (2, 3)
                    mms = []
                    for kc in range(16):
                        for tb in range(2):
                            mms.append((bank(bs_[tb]), ring[s][:, kc, m * 128:(m + 1) * 128],
                                        n2T[:, kc, tb * 512:(tb + 1) * 512], kc == 0, kc == 15))
                    mm_group(mms, reads=[("ring", s)] + N2_KEYS, writes=[("ps", bs_[0]), ("ps", bs_[1])])
                    for tb in range(2):
                        S.op("act", lambda e, tb=tb, b=bs_[tb]: e.activation(out=rtmp[tb], in_=bank(b), func=AF.Relu),
                             reads=[("ps", bs_[tb])], writes=[("rtmp", tb)])
                        S.op("act", lambda e, tb=tb, m=m, a=a: e.activation(out=a[:, m, tb * 512:(tb + 1) * 512],
                                                                             in_=rtmp[tb], func=AF.Square),
                             reads=[("rtmp", tb)], writes=[("aT", g % 2, m, tb)])

            def mlp_p1_first():
                s = ring_load(w1_v[:, :, 0:512])
                a = aT[0]
                for tb in range(2):
                    keys = [("n2T", j, h) for j in range(tb * 4, tb * 4 + 4) for h in range(2)]
                    if tb == 1:
                        p6_b(NTM - 2)
                        p6_b(NTM - 1)
                    for m in range(4):
                        b = 4 + m if tb == 0 else m
                        mm_group([(bank(b), ring[s][:, kc, m * 128:(m + 1) * 128], n2T[:, kc, tb * 512:(tb + 1) * 512],
                                   kc == 0, kc == 15) for kc in range(16)],
                                 reads=[("ring", s)] + keys, writes=[("ps", b)])
                        S.op("act", lambda e, tb=tb, b=b: e.activation(out=rtmp[tb], in_=bank(b), func=AF.Relu),
                             reads=[("ps", b)], writes=[("rtmp", tb)])
                        S.op("act", lambda e, tb=tb, m=m: e.activation(out=a[:, m, tb * 512:(tb + 1) * 512],
                                                                        in_=rtmp[tb], func=AF.Square),
                             reads=[("rtmp", tb)], writes=[("aT", 0, m, tb)])

            out_toks = []

            fin_c = {}

            def final_a(j):
                sl = j % 2
                hk = [("h", j, fc) for fc in range(4)]
                fin_c[j] = norm_a(hh[:, j, :], hk, None, None, us2[sl], ("us2", sl), apply=False)

            def final_b(j):
                sl = j % 2
                hk = [("h", j, fc) for fc in range(4)]
                c = fin_c[j]
                S.op("dve", lambda e: e.scalar_tensor_tensor(
                    out=ostage[sl], in0=hh[:, j, :], scalar=rs[:, c:c + 1], in1=nwb2, op0=ALU.mult, op1=ALU.mult),
                    reads=hk + [("rs", c), "nwb2"], writes=[("ost", sl)])
                out_toks.append(S.dma("sp", lambda e: e.dma_start(out=out[j * 128:(j + 1) * 128, :], in_=ostage[sl]),
                                      f"ost{sl}", reads=[("ost", sl)]))

            w2n = [0]

            def mlp_p2(g, last=False):
                s2 = w2n[0] % 2
                w2n[0] += 1
                S.dma("pool", lambda e, g=g, s2=s2: e.dma_start(out=w2r[s2], in_=w2_v[:, g * 4:(g + 1) * 4, :]),
                      f"w2r{s2}", writes=[("w2r", s2)])
                a = aT[g % 2]
                for j in range(NTM):
                    for fc in range(4):
                        b = 4 + (j * 4 + fc) % 4
                        mm_group([(bank(b), a[:, kc, j * 128:(j + 1) * 128], w2r[s2][:, kc, fc * 512:(fc + 1) * 512],
                                   kc == 0, kc == 3) for kc in range(4)],
                                 reads=[("w2r", s2)] + [("aT", g % 2, kc, j // 4) for kc in range(4)], writes=[("ps", b)])
                        S.op("dve", lambda e, j=j, fc=fc, b=b: e.tensor_tensor(
                            out=hh[:, j, fc * 512:(fc + 1) * 512], in0=bank(b), in1=hh[:, j, fc * 512:(fc + 1) * 512], op=ALU.add),
                            reads=[("ps", b), ("h", j, fc)], writes=[("h", j, fc)])
                    if last:
                        final_a(j)
                        if j >= 1:
                            final_b(j - 1)
                if last:
                    final_b(NTM - 1)

            def w2_load(g):
                s2 = w2n[0] % 2
                w2n[0] += 1
                S.dma("pool", lambda e: e.dma_start(out=w2r[s2], in_=w2_v[:, g * 4:(g + 1) * 4, :]),
                      f"w2r{s2}", writes=[("w2r", s2)])
                return s2

            def mlp_p2_last_pair(ga_, gb_, sa_, sb_):
                srcs = [(aT[ga_ % 2], w2r[sa_], ga_ % 2, sa_), (aT[gb_ % 2], w2r[sb_], gb_ % 2, sb_)]
                for j in range(NTM):
                    for fc in range(4):
                        b = 4 + (j * 4 + fc) % 4
                        mms, rd = [], []
                        for gi, (a, w, ap_, ws_) in enumerate(srcs):
                            for kc in range(4):
                                mms.append((bank(b), a[:, kc, j * 128:(j + 1) * 128], w[:, kc, fc * 512:(fc + 1) * 512],
                                            gi == 0 and kc == 0, gi == 1 and kc == 3))
                            rd += [("w2r", ws_)] + [("aT", ap_, kc, j // 4) for kc in range(4)]
                        mm_group(mms, reads=rd, writes=[("ps", b)])
                        S.op("dve", lambda e, j=j, fc=fc, b=b: e.tensor_tensor(
                            out=hh[:, j, fc * 512:(fc + 1) * 512], in0=bank(b), in1=hh[:, j, fc * 512:(fc + 1) * 512], op=ALU.add),
                            reads=[("ps", b), ("h", j, fc)], writes=[("h", j, fc)])
                    final_a(j)
                    if j >= 1:
                        final_b(j - 1)
                final_b(NTM - 1)

            S.fence([("us2", 2)], [("aT", 0, m, tb) for m in range(4) for tb in range(2)])
            mlp_p1_first()
            for g in range(NG):
                if g + 1 < NG:
                    mlp_p1(g + 1)
                if g < NG - 2:
                    mlp_p2(g)
                elif g == NG - 2:
                    s_pen = w2_load(g)
                else:
                    s_last = w2_load(g)
                    S.fence([("ring", 0), ("ring", 1)], [("ost", 0), ("ost", 1)])
                    sp_load(nwb2, nwf.partition_broadcast(128), "nwb2")
                    mlp_p2_last_pair(NG - 2, NG - 1, s_pen, s_last)
            checkpoint("p7", {"h2": (hh, H_KEYS, [128, NTM, D], F32)})
        except _Stop:
            pass
        for k_ in list(S.cnt):
            if not k_.startswith('E_'):
                S.wait('sp', (k_, S.cnt[k_]))
        S.emit()
    return nc


_CACHE = {}


def make_in_maps(x, meta_tokens, norm1_w, w_in, gate_w2, gate_b, gla_norm_w, pool_w, pool_scale,
                 w_out, norm2_w, mlp_w1, mlp_w2, final_norm_w, cores=None):
    f = lambda a: np.ascontiguousarray(np.asarray(a, dtype=np.float32))
    x = f(x); meta = f(meta_tokens)
    B = x.shape[0]
    cores = list(range(2 * B)) if cores is None else cores
    shared = {
        "w_in": f(w_in)[0], "w_out": f(w_out)[0], "w1": f(mlp_w1)[0], "w2": f(mlp_w2)[0],
        "pool_w": f(pool_w)[0],
        "gw2a": np.ascontiguousarray(np.concatenate([f(gate_w2)[0], f(gate_b)[0][None, :]], axis=0)),
        "gnw": np.ascontiguousarray(f(gla_norm_w)[0].reshape(2, 128).T),
        "psc": np.ascontiguousarray(f(pool_scale)[0].reshape(8, 128).T),
        "nw1": f(norm1_w)[0], "nw2": f(norm2_w)[0], "nwf": f(final_norm_w),
        "c_ident": np.eye(128, dtype=np.float32).astype(ml_dtypes.bfloat16),
        "c_mask4": np.ascontiguousarray(np.tile(np.triu(np.ones((128, 128), np.float32)), (1, 4))).astype(ml_dtypes.bfloat16),
    }
    in_maps = []
    for c in cores:
        b, half = divmod(c, 2)
        xin = np.zeros((TA, D), np.float32)
        fl = np.zeros((128, 2), np.float32)
        if half == 0:
            xin[112:128] = meta
            xin[128:] = x[b, 0:1024]
            fl[:, 0] = 1.0
        else:
            xin[112:128] = x[b, 1008:1024]
            xin[128:] = x[b, 1024:2048]
            fl[:, 1] = 1.0
        m = dict(shared)
        m["xin"] = xin
        m["flags"] = fl
        in_maps.append(m)
    return in_maps


def kernel(x, meta_tokens, norm1_w, w_in, gate_w2, gate_b, gla_norm_w, pool_w, pool_scale,
           w_out, norm2_w, mlp_w1, mlp_w2, final_norm_w):
    B = np.asarray(x).shape[0]
    n_cores = 2 * B
    in_maps = make_in_maps(x, meta_tokens, norm1_w, w_in, gate_w2, gate_b, gla_norm_w, pool_w, pool_scale,
                           w_out, norm2_w, mlp_w1, mlp_w2, final_norm_w)
    if "nc" not in _CACHE:
        _CACHE["nc"] = build_program()
    res = run_bass_kernel_spmd(_CACHE["nc"], in_maps, core_ids=list(range(n_cores)))
    outp = np.empty((B, 2048, D), np.float32)
    for c in range(n_cores):
        b, half = divmod(c, 2)
        outp[b, half * 1024:(half + 1) * 1024] = np.asarray(res.results[c]["out"], dtype=np.float32)
    return outp
```

```python
import numpy as np
import ml_dtypes
import concourse.bass as bass
import concourse.mybir as mybir
from concourse.bass_utils import run_bass_kernel_spmd
from contextlib import ExitStack

F32 = mybir.dt.float32
BF16 = mybir.dt.bfloat16
AF = mybir.ActivationFunctionType
ALU = mybir.AluOpType

ENGS = ("pe", "act", "dve", "pool", "sp")

D = 2048
NTM = 8
NTA = NTM + 1
TM, TA = NTM * 128, NTA * 128
DIN = 4112
DFF = 8192
EPS = 1e-6
POOL_W = (2, 4, 8, 16)


class Sched:
    def __init__(self, nc, es):
        self.nc, self.es = nc, es
        self.prog = {e: [] for e in ENGS}
        self.cnt = {}
        self.semh = {}
        self.waited = {}
        self.lastw = {}
        self.readers = {}

    def _sem(self, key):
        if key not in self.semh:
            self.semh[key] = self.es.enter_context(self.nc.semaphore("s_" + key))
            self.cnt[key] = 0
        return self.semh[key]

    def _wait(self, eng, tok):
        if tok is None:
            return
        key, val = tok
        if key == "E_pe" and eng == "pe":
            return
        if self.waited.get((eng, key), 0) >= val:
            return
        self.waited[(eng, key)] = val
        self.prog[eng].append(("wait", key, val))

    def _deps(self, eng, reads, writes):
        for k in reads:
            self._wait(eng, self.lastw.get(k))
        for k in writes:
            self._wait(eng, self.lastw.get(k))
            for sk, v in self.readers.get(k, {}).items():
                self._wait(eng, (sk, v))

    def _commit(self, tok, reads, writes):
        for k in reads:
            r = self.readers.setdefault(k, {})
            r[tok[0]] = max(r.get(tok[0], 0), tok[1])
        for k in writes:
            self.lastw[k] = tok
            self.readers[k] = {}

    def op(self, eng, fn, reads=(), writes=()):
        self._deps(eng, reads, writes)
        key = "E_" + eng
        self._sem(key)
        self.cnt[key] += 1
        tok = (key, self.cnt[key])
        self.prog[eng].append(("op", fn, key, 1))
        self._commit(tok, reads, writes)
        return tok

    def dma(self, eng, fn, semkey, reads=(), writes=()):
        self._deps(eng, reads, writes)
        self._sem(semkey)
        self.cnt[semkey] += 16
        tok = (semkey, self.cnt[semkey])
        self.prog[eng].append(("op", fn, semkey, 16))
        self._commit(tok, reads, writes)
        return tok

    def fence(self, old_keys, new_keys):
        acc = {}
        for k in old_keys:
            t = self.lastw.get(k)
            if t is not None:
                acc[t[0]] = max(acc.get(t[0], 0), t[1])
            for sk, v in self.readers.get(k, {}).items():
                acc[sk] = max(acc.get(sk, 0), v)
        for k in new_keys:
            r = self.readers.setdefault(k, {})
            for sk, v in acc.items():
                r[sk] = max(r.get(sk, 0), v)

    def wait(self, eng, tok):
        self._wait(eng, tok)

    def emit(self):
        nc = self.nc
        with nc.Block() as block:
            def replay(name, e):
                for it in self.prog[name]:
                    if it[0] == "wait":
                        e.wait_ge(self.semh[it[1]], it[2])
                    else:
                        ins = it[1](e)
                        ins.then_inc(self.semh[it[2]], it[3])

            @block.tensor
            def _(e):
                replay("pe", e)

            @block.scalar
            def _(e):
                replay("act", e)

            @block.vector
            def _(e):
                replay("dve", e)

            @block.gpsimd
            def _(e):
                replay("pool", e)

            @block.sync
            def _(e):
                replay("sp", e)


A_OFF = 0
B_OFF = 32768
C_OFF = B_OFF + 70912
D_OFF = C_OFF + 32768
E_OFF = D_OFF + 65536
E_SIZE = 10240
ARENA_BYTES = E_OFF + E_SIZE


class _Stop(Exception):
    pass


def build_program(stop=None, dumps=()):
    nc = bass.Bass("TRN2", target_bir_lowering=False)

    def din(name, shape, dt=F32):
        return nc.dram_tensor(name, list(shape), dt, kind="ExternalInput").ap()

    xin = din("xin", [TA, D])
    flags_d = din("flags", [128, 2])
    w_in = din("w_in", [D, DIN])
    gw2a_d = din("gw2a", [17, 512])
    gnw_d = din("gnw", [128, 2])
    psc_d = din("psc", [128, 8])
    pool_w = din("pool_w", [4, 256, 256])
    w_out = din("w_out", [D, D])
    nw1 = din("nw1", [D])
    nw2 = din("nw2", [D])
    nwf = din("nwf", [D])
    w1 = din("w1", [D, DFF])
    w2 = din("w2", [DFF, D])
    c_ident = din("c_ident", [128, 128], BF16)
    c_mask4 = din("c_mask4", [128, 512], BF16)
    out = nc.dram_tensor("out", [TM, D], F32, kind="ExternalOutput").ap()

    with ExitStack() as es:
        S = Sched(nc, es)

        def checkpoint(name, bufs):
            for bname, (ap, keys, shape, dt) in bufs.items():
                if bname in dumps:
                    d_ap = nc.dram_tensor("dbg_" + bname, list(shape), dt, kind="ExternalOutput").ap()
                    S.dma("sp", lambda e, d_ap=d_ap, ap=ap: e.dma_start(out=d_ap, in_=ap), "dbg_" + bname, reads=keys)
            if stop == name:
                raise _Stop()

        AR = es.enter_context(nc.sbuf_tensor("arena", [128, ARENA_BYTES // 4], F32))
        PS = es.enter_context(nc.psum_tensor("ps", [128, 4096], F32))

        def V(off, shape, dt=F32):
            n = int(np.prod(shape[1:]))
            isz = 4 if dt == F32 else 2
            assert off % 4 == 0 and (n * isz) % 4 == 0
            ap = AR[:, off // 4:(off + n * isz) // 4]
            if dt != F32:
                ap = ap.bitcast(dt)
            if len(shape) == 3:
                ap = ap.rearrange("p (a b) -> p a b", b=shape[2])
            elif len(shape) == 4:
                ap = ap.rearrange("p (a b c) -> p a b c", b=shape[2], c=shape[3])
            if shape[0] != 128:
                ap = ap[0:shape[0]]
            return ap

        def bank(b, n=512):
            return PS[:, b * 512:b * 512 + n]

        def bankbf(b):
            return PS[:, b * 512:(b + 1) * 512].bitcast(BF16)

        uT = V(A_OFF, [128, 16, TM], BF16)
        qT = V(B_OFF, [128, 4, TM], BF16)
        kT = V(B_OFF + 8192, [128, 4, TA], BF16)
        vv = V(B_OFF + 17408, [128, NTA, 1024], BF16)
        glrT = V(B_OFF + 35840, [17, TA], BF16)
        glrT_full = V(B_OFF + 35840, [128, TA], BF16)
        siluT = V(B_OFF + 38144, [128, 8, TM], BF16)
        yT = V(B_OFF + 54528, [128, 8, TM], BF16)
        us2 = [V(B_OFF + o_, [128, D], BF16) for o_ in (0, 4096, 16384)]
        nwb2 = V(B_OFF + 8192, [128, D], F32)
        aT = [V(B_OFF + 16384 + i * 8192, [128, 4, TM], BF16) for i in range(2)]
        w2r = [V(B_OFF + 32768 + i * 16384, [128, 4, D], BF16) for i in range(2)]
        rtmp = [V(B_OFF + 65536 + i * 2048, [128, 512], F32) for i in range(2)]
        ring = [V(C_OFF + i * 16384, [128, 16, 512], BF16) for i in range(2)]
        ostage = [V(C_OFF + i * 8192, [128, D], F32) for i in range(2)]
        NXS = 3
        xs = [V(D_OFF + 4096 + i * 8192, [128, D], F32) for i in range(NXS)]
        us = [V(D_OFF + 28672 + i * 4096, [128, D], BF16) for i in range(2)]
        nwb1 = V(D_OFF + 36864, [128, D], F32)
        put = [V(D_OFF + i * 4160, [128, 1040], F32) for i in range(3)]
        hh = V(D_OFF, [128, NTM, D], F32)

        def ga(par):
            o = D_OFF + 12480
            return dict(e=V(o + par * 2048, [128, 512]), L=V(o + 4096 + par * 2048, [128, 512]),
                        eN=V(o + 8192 + par * 2048, [128, 512]), eG=V(o + 12288, [128, 512]),
                        ki32=V(o + 14336, [128, 512]))
        Sf = V(D_OFF + 28864, [128, 1024])
        ki_t = V(D_OFF + 32960, [128, 512], BF16)
        keTt = V(D_OFF + 33984, [128, 512], BF16)
        Sb2 = V(D_OFF + 32960, [128, 1024], BF16)
        qd_all = V(D_OFF + 35008, [128, NTM, 512], BF16)
        sc_all = V(D_OFF + 43200, [128, NTM, 512], BF16)
        ke_all = V(D_OFF + 51392, [128, NTA, 512], BF16)
        dec_all = V(D_OFF + 60608, [128, NTA + 1, 4], F32)
        poolw = V(D_OFF + 60768, [128, 4, 2, 256], BF16)
        wglr = V(D_OFF + 64864, [128, 16, 16], BF16)

        def sbt(par):
            o = D_OFF + par * 14336
            return dict(sq=V(o, [128, 1024]), rstd=V(o + 4096, [128, 512]), A=V(o + 6144, [128, 1024]),
                        osb=V(o + 10240, [128, 1024]))

        eo = E_OFF
        ident = V(eo, [128, 128], BF16); eo += 256
        mask4 = V(eo, [128, 512], BF16); eo += 1024
        ones = V(eo, [128, 128]); eo += 512
        gw2a = V(eo, [17, 512], BF16); eo += 1024
        gnw = V(eo, [128, 2]); eo += 8
        psc = V(eo, [128, 8]); eo += 32
        pscw = V(eo, [128, 8]); eo += 32
        flags = V(eo, [128, 2]); eo += 8
        st = V(eo, [128, 64]); eo += 256
        rs = V(eo, [128, 64]); eo += 256
        Sb = V(eo, [128, 1024], BF16); eo += 2048
        uTtail = V(eo, [128, 16, 16], BF16); eo += 512
        uTh = V(eo, [128, 16, 128], BF16); eo += 4096
        assert eo <= E_OFF + E_SIZE, eo - E_OFF
        assert D_OFF + 64864 + 512 <= E_OFF

        s_bounce = nc.dram_tensor("s_bounce", [128, 1024], F32)
        s_gath = nc.dram_tensor("s_gath", [256, 1024], F32)

        w_in_v = w_in.rearrange("(kc p) n -> p kc n", p=128)
        w_out_v = w_out.rearrange("(kc p) n -> p kc n", p=128)
        w1_v = w1.rearrange("(kc p) n -> p kc n", p=128)
        w2_v = w2.rearrange("(kc p) n -> p kc n", p=128)

        def sp_load(dst, src, key):
            return S.dma("sp", lambda e: e.dma_start(out=dst, in_=src), "c_" + str(key), writes=[key])

        def copy_op(i, dst, src, reads, writes):
            if i % 2 == 0:
                return S.op("act", lambda e: e.copy(out=dst, in_=src), reads=reads, writes=writes)
            return S.op("dve", lambda e: e.tensor_copy(out=dst, in_=src), reads=reads, writes=writes)

        def mm_group(mms, reads, writes):
            def fn(e):
                last = None
                for (o, l, r, a, b) in mms:
                    last = e.matmul(o, l, r, start=a, stop=b)
                return last
            return S.op("pe", fn, reads=reads, writes=writes)

        def tr_group(trs, reads, writes):
            def fn(e):
                last = None
                for (o, i) in trs:
                    last = e.transpose(out=o, in_=i, identity=ident)
                return last
            return S.op("pe", fn, reads=reads + ["ident"], writes=writes)

        try:
            sp_load(nwb1, nw1.partition_broadcast(128), "nwb1")
            S.op("dve", lambda e: e.memset(ones, 1.0), writes=["ones"])
            S.op("dve", lambda e: e.memset(st, 0.0), writes=["st"])
            ring_n = [0]

            def ring_load(src_ap, ncols=512):
                if ring_n[0] == 0:
                    S.wait("pool", x_toks[3])
                s = ring_n[0] % 2
                ring_n[0] += 1
                dst = ring[s] if ncols == 512 else ring[s][:, :, 0:ncols]
                S.dma("pool", lambda e: e.dma_start(out=dst, in_=src_ap), f"ring{s}", writes=[("ring", s)])
                return s

            stat_col = [0]
            x_toks = {}

            def norm_a(src, src_keys, nwb, nwb_key, stg, stg_key, apply=True):
                c = stat_col[0]
                stat_col[0] += 1
                S.op("act", lambda e: e.activation(out=stg, in_=src, func=AF.Square, scale=float(D ** -0.5),
                                                   accum_out=st[:, c:c + 1]),
                     reads=src_keys + ["st"], writes=[stg_key, ("st", c)])
                S.op("act", lambda e: e.activation(out=rs[:, c:c + 1], in_=st[:, c:c + 1], func=AF.Ln, bias=EPS),
                     reads=[("st", c)], writes=[("rs", c)])
                S.op("act", lambda e: e.activation(out=rs[:, c:c + 1], in_=rs[:, c:c + 1], func=AF.Exp, scale=-0.5),
                     reads=[("rs", c)], writes=[("rs", c)])
                if apply:
                    S.op("dve", lambda e: e.scalar_tensor_tensor(out=stg, in0=src, scalar=rs[:, c:c + 1], in1=nwb,
                                                                 op0=ALU.mult, op1=ALU.mult),
                         reads=src_keys + [("rs", c), nwb_key], writes=[stg_key])
                return c

            def norm_b(stg, stg_key, dstT, dcol, dkeys, bset):
                for half in range(2):
                    b = bset[half]
                    tr_group([(bankbf(b)[:, i * 128:(i + 1) * 128], stg[:, (half * 8 + i) * 128:(half * 8 + i + 1) * 128])
                              for i in range(8)], reads=[stg_key], writes=[("ps", b)])
                    copy_op(half, dstT[:, half * 8:half * 8 + 8, dcol:dcol + 128],
                            bankbf(b).rearrange("p (a b) -> p a b", b=128),
                            reads=[("ps", b)], writes=[dkeys[half]])

            def p0_a(t):
                sl = t % 2
                xl = t % NXS
                x_toks[t] = S.dma("sp", lambda e: e.dma_start(out=xs[xl], in_=xin[t * 128:(t + 1) * 128, :]),
                                  f"xs{xl}", writes=[("xs", xl)])
                norm_a(xs[xl], [("xs", xl)], nwb1, "nwb1", us[sl], ("us", sl))

            def p0_b(t):
                sl = t % 2
                bset = (0, 1) if t % 2 == 0 else (2, 3)
                if t == 0:
                    norm_b(us[sl], ("us", sl), uTh, 0, [("uTh", 0), ("uTh", 1)], bset)
                else:
                    norm_b(us[sl], ("us", sl), uT, (t - 1) * 128, [("uT", t - 1, 0), ("uT", t - 1, 1)], bset)

            XS_KEYS = [("xs", i) for i in range(NXS)] + [("us", 0), ("us", 1)]
            UTH_KEYS = [("uTh", 0), ("uTh", 1)]
            UT_KEYS = [("uT", j, h) for j in range(NTM) for h in range(2)]
            p0_a(0)
            sp_load(ident, c_ident, "ident")
            S.dma("pool", lambda e: e.dma_start(out=gw2a, in_=gw2a_d), "gw2a", writes=["gw2a"])
            S.dma("pool", lambda e: e.dma_start(out=wglr, in_=w_in_v[:, :, 3072:3088]), "wglr", writes=["wglr"])
            for t in range(NTA):
                if t + 1 < NTA:
                    p0_a(t + 1)
                p0_b(t)
            sp_load(mask4, c_mask4, "mask4")
            sp_load(gnw, gnw_d, "gnw")
            sp_load(psc, psc_d, "psc")
            sp_load(flags, flags_d, "flags")
            for g in range(4):
                S.op("dve", lambda e, g=g: e.tensor_scalar(out=pscw[:, 2 * g:2 * g + 2], in0=psc[:, 2 * g:2 * g + 2],
                                                           scalar1=1.0 / POOL_W[g], scalar2=None, op0=ALU.mult),
                     reads=["psc"], writes=[("pscw", g)])
            S.op("dve", lambda e: e.tensor_copy(out=uTtail, in_=uTh[:, :, 112:128]), reads=UTH_KEYS, writes=["uTtail"])
            checkpoint("p0", {"uT": (uT, UT_KEYS, [128, 16, TM], BF16)})

            chunk_ctr = [0]

            def next_bset():
                chunk_ctr[0] += 1
                return (0, 1, 2) if chunk_ctr[0] % 2 else (3, 4, 5)

            tm_ctr = [0]

            def next_tm_bank():
                tm_ctr[0] += 1
                return 3 + tm_ctr[0] % 5

            ev = [0]

            def fm_mms(w_ap_fn, bs_, with_head, tail=False):
                mms = []
                for kc in range(16):
                    for tb in range(2):
                        mms.append((bank(bs_[tb]), w_ap_fn(kc), uT[:, kc, tb * 512:(tb + 1) * 512], kc == 0, kc == 15))
                    if with_head:
                        mms.append((bank(bs_[2], 128), w_ap_fn(kc), uTh[:, kc, :], kc == 0, kc == 15))
                    if tail:
                        mms.append((bank(bs_[2], 16), w_ap_fn(kc), uTtail[:, kc, :], kc == 0, kc == 15))
                return mms

            def gates(T, K, glr_ap, glr_key, bg, dec_ap, dkey, want_eG, head):
                mm_group([(bank(bg)[:, h * 128:(h + 1) * 128], gw2a[0:17, h * 128:(h + 1) * 128], glr_ap, True, True)
                          for h in range(4)], reads=["gw2a", glr_key], writes=[("ps", bg)])
                S.op("act", lambda e: e.activation(out=T["e"], in_=bank(bg), func=AF.Exp, scale=-1.0),
                     reads=[("ps", bg)], writes=[K + "e"])
                S.op("act", lambda e: e.activation(out=T["e"], in_=T["e"], func=AF.Ln, bias=1.0),
                     reads=[K + "e"], writes=[K + "e"])
                if head:
                    S.op("dve", lambda e: e.tensor_scalar(out=T["e"], in0=T["e"], scalar1=flags[:, 0:1], scalar2=None,
                                                          op0=ALU.mult), reads=[K + "e", "flags"], writes=[K + "e"])
                for h in range(4):
                    S.op("dve", lambda e, h=h: e.tensor_tensor_scan(out=T["L"][:, h * 128:(h + 1) * 128],
                                                                     data0=T["e"][:, h * 128:(h + 1) * 128],
                                                                     data1=T["e"][:, h * 128:(h + 1) * 128],
                                                                     initial=0.0, op0=ALU.add, op1=ALU.bypass),
                         reads=[K + "e"], writes=[K + "L%d" % h])
                Lk = [K + "L%d" % h for h in range(4)]
                S.op("act", lambda e: e.activation(out=T["eN"], in_=T["L"], func=AF.Exp, scale=1.0 / 16),
                     reads=Lk, writes=[K + "eN"])
                S.op("act", lambda e: e.activation(out=dec_ap, in_=T["L"].rearrange("p (h t) -> p h t", t=128)[:, :, 127],
                                                   func=AF.Exp, scale=-1.0 / 16),
                     reads=Lk, writes=[dkey])
                if want_eG:
                    S.op("act", lambda e: e.activation(out=T["eG"], in_=T["L"], func=AF.Exp, scale=-1.0 / 16),
                         reads=Lk, writes=["ga_eG"])

            def ke_transpose(src, src_keys, bk, dst, dkey):
                tr_group([(bankbf(bk)[:, h * 128:(h + 1) * 128], src[:, h * 128:(h + 1) * 128]) for h in range(4)],
                         reads=src_keys, writes=[("ps", bk)])
                S.op("act", lambda e: e.copy(out=dst, in_=bankbf(bk)[:, 0:512]), reads=[("ps", bk)], writes=[dkey])

            def su_mm(t, bd):
                mm_group([(PS[:, bd * 512 + h * 256:bd * 512 + (h + 1) * 256], ke_all[:, t, h * 128:(h + 1) * 128],
                           vv[:, t, h * 256:(h + 1) * 256], True, True) for h in range(4)],
                         reads=[("ke", t), ("vv", t, 0), ("vv", t, 1)], writes=[("ps", bd), ("ps", bd + 1)])

            def su_stt(t, bd):
                for h in range(4):
                    S.op("dve", lambda e, h=h: e.scalar_tensor_tensor(
                        out=Sf[:, h * 256:(h + 1) * 256], in0=Sf[:, h * 256:(h + 1) * 256], scalar=dec_all[:, t, h:h + 1],
                        in1=PS[:, bd * 512 + h * 256:bd * 512 + (h + 1) * 256], op0=ALU.mult, op1=ALU.add),
                        reads=["Sf", ("dec", t), ("ps", bd), ("ps", bd + 1)], writes=["Sf"])

            def state_update(t, bd):
                su_mm(t, bd)
                su_stt(t, bd)

            GAKEYS = ["ki_t", "keTt", "ga_eG", "ga_ki32", "Sf", ("dec", NTA)] + \
                     [(n, t) for n in ("qd", "sc", "ke", "dec") for t in range(NTA)]
            for par in range(2):
                GAKEYS += ["ga%d_" % par + n for n in ["e", "L0", "L1", "L2", "L3", "eN"]]
            S.fence(XS_KEYS + ["nwb1"], GAKEYS)
            def stage_a1(t, bg):
                par = t % 2
                T = ga(par)
                K = "ga%d_" % par
                kcols = slice(t * 128, (t + 1) * 128)
                head = (t == 0)
                gates(T, K, glrT[0:17, kcols], "glrT", bg, dec_all[:, t, :], ("dec", t), not head, head)
                eN3 = T["eN"].rearrange("p (h t) -> p h t", t=128)
                if not head:
                    j = t - 1
                    qcols = slice(j * 128, (j + 1) * 128)
                    eG3 = T["eG"].rearrange("p (h t) -> p h t", t=128)
                    S.op("dve", lambda e: e.scalar_tensor_tensor(
                        out=qd_all[:, j, :].rearrange("p (h t) -> p h t", t=128), in0=qT[:, :, qcols],
                        scalar=float(128 ** -0.5), in1=eG3, op0=ALU.mult, op1=ALU.mult),
                        reads=[("qT", m, j // 4) for m in range(4)] + ["ga_eG"], writes=[("qd", t)])
                S.op("dve", lambda e: e.tensor_tensor(
                    out=T["ki32"].rearrange("p (h t) -> p h t", t=128), in0=kT[:, :, kcols], in1=eN3, op=ALU.mult),
                    reads=[("kT", m) for m in range(4)] + [K + "eN"], writes=["ga_ki32"])
                dsc, dsk = dec_all[:, t, :], ("dec", t)
                if head:
                    S.op("dve", lambda e: e.tensor_scalar(out=dec_all[:, NTA, :], in0=dec_all[:, 0, :], scalar1=flags[:, 0:1],
                                                          scalar2=None, op0=ALU.mult),
                         reads=[("dec", 0), "flags"], writes=[("dec", NTA)])
                    dsc, dsk = dec_all[:, NTA, :], ("dec", NTA)
                else:
                    S.op("act", lambda e: e.copy(out=ki_t, in_=T["ki32"]), reads=["ga_ki32"], writes=["ki_t"])
                for h in range(4):
                    S.op("dve", lambda e, h=h: e.tensor_scalar(
                        out=keTt[:, h * 128:(h + 1) * 128], in0=T["ki32"][:, h * 128:(h + 1) * 128],
                        scalar1=dsc[:, h:h + 1], scalar2=None, op0=ALU.mult),
                        reads=["ga_ki32", dsk], writes=["keTt"])

            def stage_a2(t, bk, bsx):
                head = (t == 0)
                if not head:
                    j = t - 1
                    mm_group([(bank(bsx)[:, h * 128:(h + 1) * 128], ki_t[:, h * 128:(h + 1) * 128],
                               qd_all[:, j, h * 128:(h + 1) * 128], True, True) for h in range(4)],
                             reads=["ki_t", ("qd", t)], writes=[("ps", bsx)])
                    S.op("dve", lambda e: e.tensor_tensor(out=sc_all[:, j, :], in0=bank(bsx), in1=mask4, op=ALU.mult),
                         reads=[("ps", bsx), "mask4"], writes=[("sc", t)])
                ke_transpose(keTt, ["keTt"], bk, ke_all[:, t, :], ("ke", t))

            sa_state = {"next": 0, "pend": None}

            def sa_hook(bg, bsx, bk=None):
                if sa_state["pend"] is not None:
                    stage_a2(sa_state["pend"], bg if bk is None else bk, bsx)
                    sa_state["pend"] = None
                if sa_state["next"] < NTA:
                    stage_a1(sa_state["next"], bg)
                    sa_state["pend"] = sa_state["next"]
                    sa_state["next"] += 1

            S.op("dve", lambda e: e.memset(glrT_full, 1.0), writes=["glrT"])
            bs_ = next_bset()
            mms = []
            for kc in range(16):
                for tb in range(2):
                    mms.append((PS[0:16, bs_[tb] * 512:(bs_[tb] + 1) * 512], wglr[:, kc, :],
                                uT[:, kc, tb * 512:(tb + 1) * 512], kc == 0, kc == 15))
                mms.append((PS[0:16, bs_[2] * 512:bs_[2] * 512 + 128], wglr[:, kc, :], uTh[:, kc, :], kc == 0, kc == 15))
            mm_group(mms, reads=["wglr"] + UT_KEYS + UTH_KEYS, writes=[("ps", b) for b in bs_])
            for tb in range(2):
                ev[0] += 1
                copy_op(ev[0], glrT[0:16, 128 + tb * 512:128 + (tb + 1) * 512], PS[0:16, bs_[tb] * 512:(bs_[tb] + 1) * 512],
                        reads=[("ps", bs_[tb])], writes=["glrT"])
            ev[0] += 1
            copy_op(ev[0], glrT[0:16, 0:128], PS[0:16, bs_[2] * 512:bs_[2] * 512 + 128], reads=[("ps", bs_[2])], writes=["glrT"])

            s = ring_load(w_in_v[:, :, 512:1024])
            for m in range(4):
                bs_ = next_bset()
                mm_group(fm_mms(lambda kc, m=m, s=s: ring[s][:, kc, m * 128:(m + 1) * 128], bs_, True),
                         reads=[("ring", s)] + UT_KEYS + UTH_KEYS, writes=[("ps", b) for b in bs_])
                for tb in range(2):
                    S.op("act" if tb == 0 else "dve",
                         (lambda e, m=m, tb=tb, b=bs_[tb]: e.copy(out=kT[:, m, 128 + tb * 512:128 + (tb + 1) * 512], in_=bank(b)))
                         if tb == 0 else
                         (lambda e, m=m, tb=tb, b=bs_[tb]: e.tensor_copy(out=kT[:, m, 128 + tb * 512:128 + (tb + 1) * 512], in_=bank(b))),
                         reads=[("ps", bs_[tb])], writes=[("kT", m)])
                S.op("act", lambda e, m=m, b=bs_[2]: e.copy(out=kT[:, m, 0:128], in_=bank(b, 128)),
                     reads=[("ps", bs_[2])], writes=[("kT", m)])

            def fm_block(col0, evac, hook=None, slot=None):
                s = ring_load(w_in_v[:, :, col0:col0 + 512]) if slot is None else slot
                for m in range(4):
                    bs_ = next_bset()
                    mm_group(fm_mms(lambda kc, m=m, s=s: ring[s][:, kc, m * 128:(m + 1) * 128], bs_, False),
                             reads=[("ring", s)] + UT_KEYS, writes=[("ps", bs_[0]), ("ps", bs_[1])])
                    evac(m, bs_)
                    if hook is not None:
                        hook(m)

            def ev_copy(dst, name):
                def f(m, bs_):
                    for tb in range(2):
                        ev[0] += 1
                        copy_op(ev[0], dst[:, m, tb * 512:(tb + 1) * 512], bank(bs_[tb]),
                                reads=[("ps", bs_[tb])], writes=[(name, m, tb)])
                return f

            fm_block(0, ev_copy(qT, "qT"))
            for g in range(4):
                S.dma("pool", lambda e, g=g: e.dma_start(out=poolw[:, g], in_=pool_w[g].rearrange("(cc p) d -> p cc d", p=128)),
                      "poolw%d" % g, writes=[("poolw", g)])
            MAINB2 = [("siluT", c, tb) for c in range(8) for tb in range(2)] + [("yT", c) for c in range(8)]
            PUT_KEYS = [("put", i) for i in range(3)]
            for pb in range(2):
                s = ring_load(w_in_v[:, :, 3088 + pb * 512:3088 + (pb + 1) * 512])
                for m in range(4):
                    c = pb * 4 + m
                    g = c // 2
                    w = POOL_W[g]
                    bs_ = next_bset()
                    mm_group(fm_mms(lambda kc, m=m, s=s: ring[s][:, kc, m * 128:(m + 1) * 128], bs_, False, tail=True),
                             reads=[("ring", s), "uTtail"] + UT_KEYS, writes=[("ps", b) for b in bs_])
                    S.op("act", lambda e, b=bs_[2]: e.copy(out=put[0][:, 0:16], in_=bank(b, 16)),
                         reads=[("ps", bs_[2])], writes=[("put", 0)])
                    S.op("act", lambda e, b=bs_[0]: e.copy(out=put[0][:, 16:528], in_=bank(b)),
                         reads=[("ps", bs_[0])], writes=[("put", 0)])
                    S.op("act", lambda e, b=bs_[1]: e.copy(out=put[0][:, 528:1040], in_=bank(b)),
                         reads=[("ps", bs_[1])], writes=[("put", 0)])
                    if w == 2:
                        S.op("dve", lambda e, c=c: e.tensor_tensor(out=yT[:, c, :], in0=put[0][:, 15:1039], in1=put[0][:, 16:1040],
                                                                   op=ALU.subtract),
                             reads=[("put", 0)], writes=[("yT", c)])
                    else:
                        src, cur_i, sh, lo = put[0], 1, 1, 1
                        srckey = ("put", 0)
                        while sh < w:
                            dst = put[cur_i]
                            S.op("dve", lambda e, src=src, dst=dst, sh=sh, lo=lo: e.tensor_tensor(
                                out=dst[:, lo:1040], in0=src[:, lo:1040], in1=src[:, lo - sh:1040 - sh], op=ALU.add),
                                reads=[srckey], writes=[("put", cur_i)])
                            src, srckey = dst, ("put", cur_i)
                            cur_i = 3 - cur_i
                            sh *= 2
                            lo = 2 * sh - 1
                        S.op("dve", lambda e, c=c, src=src, w=w: e.scalar_tensor_tensor(
                            out=yT[:, c, :], in0=put[0][:, 16:1040], scalar=-float(w), in1=src[:, 16:1040],
                            op0=ALU.mult, op1=ALU.add),
                            reads=[("put", 0), srckey], writes=[("yT", c)])
                    if c % 2 == 1:
                        sa_hook(6, 7)

            for half in range(2):
                s = ring_load(w_in_v[:, :, 1024 + half * 512:1024 + (half + 1) * 512])
                for t in range(NTA):
                    b = next_tm_bank()
                    lh = (lambda kc: uTh[:, kc, :]) if t == 0 else (lambda kc, t=t: uT[:, kc, (t - 1) * 128:t * 128])
                    rk = UTH_KEYS if t == 0 else [("uT", t - 1, 0), ("uT", t - 1, 1)]
                    mm_group([(bank(b), lh(kc), ring[s][:, kc, :], kc == 0, kc == 15) for kc in range(16)],
                             reads=[("ring", s)] + rk, writes=[("ps", b)])
                    ev[0] += 1
                    copy_op(ev[0], vv[:, t, half * 512:(half + 1) * 512], bank(b),
                            reads=[("ps", b)], writes=[("vv", t, half)])
                    if t in (2, 5, 8):
                        sa_hook(0, 1, 2)
                    if half == 1 and t >= 2:
                        if t == 2:
                            S.op("dve", lambda e: e.memset(Sf, 0.0), writes=["Sf"])
                        state_update(t - 2, 0)

            while sa_state["pend"] is not None or sa_state["next"] < NTA:
                sa_hook(0, 1, 2)
            r_slots = [ring_load(w_in_v[:, :, 2048 + rb * 512:2048 + (rb + 1) * 512]) for rb in range(2)]
            state_update(NTA - 2, 0)
            state_update(NTA - 1, 0)
            S.dma("sp", lambda e: e.dma_start(out=s_bounce.ap(), in_=Sf), "sbo", reads=["Sf"], writes=["s_bounce"])
            S._deps("pool", ["s_bounce"], ["s_gath"])
            S._sem("cc")
            S.cnt["cc"] += 1
            cc_tok = ("cc", S.cnt["cc"])
            S.prog["pool"].append(("op", lambda e: e.collective_compute(
                "AllGather", ALU.bypass, replica_groups=[[0, 1], [2, 3], [4, 5], [6, 7]],
                ins=[s_bounce.ap()], outs=[s_gath.ap()]), "cc", 1))
            S._commit(cc_tok, ["s_bounce"], ["s_gath"])
            S.dma("sp", lambda e: e.dma_start(out=Sf, in_=s_gath.ap()[0:128, :]), "sgi", reads=["s_gath"], writes=["Sf"])
            S.op("dve", lambda e: e.tensor_scalar(out=Sf, in0=Sf, scalar1=flags[:, 1:2], scalar2=None, op0=ALU.mult),
                 reads=["Sf", "flags"], writes=["Sf"])
            state_update(0, 6)
            S.op("dve", lambda e: e.tensor_copy(out=Sb, in_=Sf), reads=["Sf"], writes=["Sb0"])

            for rb in range(2):
                def ev_silu(m, bs_, rb=rb):
                    for tb in range(2):
                        S.op("act", lambda e, m=m, tb=tb, b=bs_[tb]: e.activation(
                            out=siluT[:, rb * 4 + m, tb * 512:(tb + 1) * 512], in_=bank(b), func=AF.Silu),
                            reads=[("ps", bs_[tb])], writes=[("siluT", rb * 4 + m, tb)])
                fm_block(2048 + rb * 512, ev_silu, slot=r_slots[rb])

            checkpoint("p3", {"qT": (qT, [("qT", m, tb) for m in range(4) for tb in range(2)], [128, 4, TM], BF16),
                              "siluT": (siluT, [("siluT", c, tb) for c in range(8) for tb in range(2)], [128, 8, TM], BF16),
                              "yT": (yT, [("yT", c) for c in range(8)], [128, 8, TM], BF16)})

            MIX_KEYS = [("mix", c, tb) for c in range(16) for tb in range(2)] + [("mixg", j) for j in range(NTM)]
            S.fence(UT_KEYS, MIX_KEYS)
            mixedT = uT
            for g in range(4):
                for dc in range(2):
                    bs_ = (0, 1) if (g * 2 + dc) % 2 == 0 else (2, 3)
                    c = g * 2 + dc
                    for tb in range(2):
                        mm_group([(bank(bs_[tb]), poolw[:, g, cc, dc * 128:(dc + 1) * 128],
                                   yT[:, g * 2 + cc, tb * 512:(tb + 1) * 512], cc == 0, cc == 1) for cc in range(2)],
                                 reads=[("poolw", g), ("yT", g * 2), ("yT", g * 2 + 1)], writes=[("ps", bs_[tb])])
                        S.op("dve", lambda e, c=c, tb=tb, b=bs_[tb]: e.tensor_scalar(
                            out=mixedT[:, 8 + c, tb * 512:(tb + 1) * 512], in0=bank(b), scalar1=pscw[:, c:c + 1],
                            scalar2=None, op0=ALU.mult),
                            reads=[("ps", bs_[tb]), ("pscw", g)], writes=[("mix", 8 + c, tb)])

            SBKEYS = []
            for par in range(2):
                SBKEYS += ["sb%d_" % par + n for n in ["sq0", "sq1", "rstd", "A0", "A1", "osb0", "osb1"]]
            S.fence(PUT_KEYS + UTH_KEYS + [k_ for k_ in GAKEYS if k_ != "Sf" and not (isinstance(k_, tuple) and k_[0] in ("qd", "sc", "ke", "dec", "poolw"))] + XS_KEYS, SBKEYS)

            Sbs = [Sb, Sb2]
            S.fence(["ki_t", "keTt"], ["Sb1"])

            def p4_main(j):
                t = j + 1
                par = j % 2
                bo = 0 if par == 0 else 2
                Scur, Snxt = Sbs[par], Sbs[1 - par]
                if j < NTM - 1:
                    su_mm(t, 4)
                mms = []
                for h in range(4):
                    for vc in range(2):
                        o_ap = PS[:, bo * 512 + (h * 2 + vc) * 128:bo * 512 + (h * 2 + vc + 1) * 128]
                        mms.append((o_ap, vv[:, t, h * 256 + vc * 128:h * 256 + (vc + 1) * 128],
                                    sc_all[:, j, h * 128:(h + 1) * 128], True, False))
                        mms.append((o_ap, Scur[:, h * 256 + vc * 128:h * 256 + (vc + 1) * 128],
                                    qd_all[:, j, h * 128:(h + 1) * 128], False, True))
                mm_group(mms, reads=[("sc", t), ("qd", t), "Sb%d" % par, ("vv", t, 0), ("vv", t, 1)],
                         writes=[("ps", bo), ("ps", bo + 1)])
                if j < NTM - 1:
                    su_stt(t, 4)
                    S.op("dve", lambda e: e.tensor_copy(out=Snxt, in_=Sf), reads=["Sf"], writes=["Sb%d" % (1 - par)])
                T = sbt(par)
                K = "sb%d_" % par
                for hb in range(2):
                    S.op("act", lambda e, hb=hb: e.activation(out=T["sq"][:, hb * 512:(hb + 1) * 512],
                                                               in_=bank(bo + hb), func=AF.Square),
                         reads=[("ps", bo + hb)], writes=[K + "sq%d" % hb])
                    S.op("act", lambda e, hb=hb: e.copy(out=T["osb"][:, hb * 512:(hb + 1) * 512], in_=bank(bo + hb)),
                         reads=[("ps", bo + hb)], writes=[K + "osb%d" % hb])

            def p4_post(j):
                par = j % 2
                bg = 6 + par
                T = sbt(par)
                K = "sb%d_" % par
                mms = []
                for h in range(4):
                    for vc in range(2):
                        mms.append((bank(bg)[:, h * 128:(h + 1) * 128], ones,
                                    T["sq"][:, (h * 2 + vc) * 128:(h * 2 + vc + 1) * 128], vc == 0, vc == 1))
                mm_group(mms, reads=["ones", K + "sq0", K + "sq1"], writes=[("ps", bg)])
                S.op("act", lambda e: e.activation(out=T["rstd"], in_=bank(bg), func=AF.Ln, scale=1.0 / 256, bias=EPS),
                     reads=[("ps", bg)], writes=[K + "rstd"])
                S.op("act", lambda e: e.activation(out=T["rstd"], in_=T["rstd"], func=AF.Exp, scale=-0.5),
                     reads=[K + "rstd"], writes=[K + "rstd"])

            def p4_fin(j):
                par = j % 2
                T = sbt(par)
                K = "sb%d_" % par
                cols = slice(j * 128, (j + 1) * 128)
                r3 = T["rstd"].rearrange("p (h t) -> p h t", t=128)
                A4 = T["A"].rearrange("p (h v t) -> p h v t", v=2, t=128)
                o4 = T["osb"].rearrange("p (h v t) -> p h v t", v=2, t=128)
                silu4 = siluT[:, :, cols].rearrange("p (h v) t -> p h v t", v=2)
                mix4 = mixedT[:, 0:8, cols].rearrange("p (h v) t -> p h v t", v=2)
                for vc in range(2):
                    S.op("dve", lambda e, vc=vc: e.tensor_tensor(
                        out=A4[:, :, vc, :], in0=silu4[:, :, vc, :], in1=r3, op=ALU.mult),
                        reads=[("siluT", c, j // 4) for c in range(8)] + [K + "rstd"], writes=[K + "A%d" % vc])
                for vc in range(2):
                    S.op("dve", lambda e, vc=vc: e.scalar_tensor_tensor(
                        out=mix4[:, :, vc, :], in0=o4[:, :, vc, :], scalar=gnw[:, vc:vc + 1], in1=A4[:, :, vc, :],
                        op0=ALU.mult, op1=ALU.mult),
                        reads=[K + "osb0", K + "osb1", "gnw", K + "A%d" % vc], writes=[("mixg", j)])

            for j in range(NTM):
                if j >= 2:
                    p4_fin(j - 2)
                p4_main(j)
                if j >= 1:
                    p4_post(j - 1)
            p4_fin(NTM - 2)
            p4_post(NTM - 1)
            p4_fin(NTM - 1)

            checkpoint("p4", {"mixedT": (uT, MIX_KEYS, [128, 16, TM], BF16)})
            H_KEYS = [("h", j, fc) for j in range(NTM) for fc in range(4)]
            S.fence(GAKEYS + SBKEYS + PUT_KEYS + UTH_KEYS + [("poolw", g) for g in range(4)] + ["wglr", "nwb1", "Sb1"] + XS_KEYS, H_KEYS)
            for j in range(NTM):
                S.dma("sp", lambda e, j=j: e.dma_start(out=hh[:, j, :], in_=xin[128 + j * 128:128 + (j + 1) * 128, :]),
                      f"hld{j}", writes=[("h", j, fc) for fc in range(4)])
            N2_KEYS = [("n2T", j, h) for j in range(NTM) for h in range(2)]
            MAINB1 = [("qT", m, tb) for m in range(4) for tb in range(2)] + [("kT", m) for m in range(4)] + \
                     [("vv", t, h) for t in range(NTA) for h in range(2)] + ["glrT"]
            B_OLD = MAINB1 + MAINB2
            B_NEW = [("us2", 0), ("us2", 1), ("us2", 2), "nwb2", ("w2r", 0), ("w2r", 1), ("rtmp", 0), ("rtmp", 1)] + \
                    [("aT", sl, m, tb) for sl in range(2) for m in range(4) for tb in range(2)]
            S.fence(B_OLD, B_NEW)
            n2T = uT
            sp_load(nwb2, nw2.partition_broadcast(128), "nwb2")

            p6_c = {}

            def p6_a(j):
                sl = j % 3
                p6_c[j] = norm_a(hh[:, j, :], [("h", j, fc) for fc in range(4)], nwb2, "nwb2", us2[sl], ("us2", sl),
                                 apply=False)

            def p6_apply(j):
                sl = j % 3
                c = p6_c[j]
                S.op("dve", lambda e: e.scalar_tensor_tensor(out=us2[sl], in0=hh[:, j, :], scalar=rs[:, c:c + 1], in1=nwb2,
                                                             op0=ALU.mult, op1=ALU.mult),
                     reads=[("h", j, fc) for fc in range(4)] + [("rs", c), "nwb2"], writes=[("us2", sl)])

            def p6_b(j):
                sl = j % 3
                bset = (0, 1) if j % 2 == 0 else (2, 3)
                S.fence([("mixr", j), ("mixg", j)], [("n2T", j, 0), ("n2T", j, 1)])
                norm_b(us2[sl], ("us2", sl), n2T, j * 128, [("n2T", j, 0), ("n2T", j, 1)], bset)

            rb = [0]

            def next_rot():
                rb[0] += 1
                return 4 + rb[0] % 4

            for cb in range(4):
                s = ring_load(w_out_v[:, :, cb * 512:(cb + 1) * 512])
                for j in range(NTM):
                    b = next_rot()
                    mm_group([(bank(b), mixedT[:, kc, j * 128:(j + 1) * 128], ring[s][:, kc, :], kc == 0, kc == 15)
                              for kc in range(16)],
                             reads=[("ring", s), ("mixg", j), ("mixr", j)] + [("mix", c, j // 4) for c in range(8, 16)],
                             writes=[("ps", b)])
                    S.op("dve", lambda e, j=j, cb=cb, b=b: e.tensor_tensor(
                        out=hh[:, j, cb * 512:(cb + 1) * 512], in0=bank(b), in1=hh[:, j, cb * 512:(cb + 1) * 512], op=ALU.add),
                        reads=[("ps", b), ("h", j, cb)], writes=[("h", j, cb)])
                    if cb == 3:
                        p6_a(j)
                        if j >= 1:
                            p6_apply(j - 1)
                        if j >= 2:
                            p6_b(j - 2)
            p6_apply(NTM - 1)

            checkpoint("p6", {"n2T": (uT, N2_KEYS, [128, 16, TM], BF16)})
            NG = DFF // 512

            def mlp_p1(g):
                s = ring_load(w1_v[:, :, g * 512:(g + 1) * 512])
                a = aT[g % 2]
                for m in range(4):
                    bs_ = (0, 1) if (g * 4 + m) % 2 == 0 else (2, 3)
                    mms = []
                    for kc in range(16):
                        for tb in range(2):
                            mms.append((bank(bs_[tb]), ring[s][:, kc, m * 128:(m + 1) * 128],
                                        n2T[:, kc, tb * 512:(tb + 1) * 512], kc == 0, kc == 15))
                    mm_group(mms, reads=[("ring", s)] + N2_KEYS, writes=[("ps", bs_[0]), ("ps", bs_[1])])
                    for tb in range(2):
                        S.op("act", lambda e, tb=tb, b=bs_[tb]: e.activation(out=rtmp[tb], in_=bank(b), func=AF.Relu),
                             reads=[("ps", bs_[tb])], writes=[("rtmp", tb)])
                        S.op("act", lambda e, tb=tb, m=m, a=a: e.activation(out=a[:, m, tb * 512:(tb + 1) * 512],
                                                                             in_=rtmp[tb], func=AF.Square),
                             reads=[("rtmp", tb)], writes=[("aT", g % 2, m, tb)])

            def mlp_p1_first():
                s = ring_load(w1_v[:, :, 0:512])
                a = aT[0]
                for tb in range(2):
                    keys = [("n2T", j, h) for j in range(tb * 4, tb * 4 + 4) for h in range(2)]
                    if tb == 1:
                        p6_b(NTM - 2)
                        p6_b(NTM - 1)
                    for m in range(4):
                        b = 4 + m if tb == 0 else m
                        mm_group([(bank(b), ring[s][:, kc, m * 128:(m + 1) * 128], n2T[:, kc, tb * 512:(tb + 1) * 512],
                                   kc == 0, kc == 15) for kc in range(16)],
                                 reads=[("ring", s)] + keys, writes=[("ps", b)])
                        S.op("act", lambda e, tb=tb, b=b: e.activation(out=rtmp[tb], in_=bank(b), func=AF.Relu),
                             reads=[("ps", b)], writes=[("rtmp", tb)])
                        S.op("act", lambda e, tb=tb, m=m: e.activation(out=a[:, m, tb * 512:(tb + 1) * 512],
                                                                        in_=rtmp[tb], func=AF.Square),
                             reads=[("rtmp", tb)], writes=[("aT", 0, m, tb)])

            out_toks = []

            fin_c = {}

            def final_a(j):
                sl = j % 2
                hk = [("h", j, fc) for fc in range(4)]
                fin_c[j] = norm_a(hh[:, j, :], hk, None, None, us2[sl], ("us2", sl), apply=False)

            def final_b(j):
                sl = j % 2
                hk = [("h", j, fc) for fc in range(4)]
                c = fin_c[j]
                S.op("dve", lambda e: e.scalar_tensor_tensor(
                    out=ostage[sl], in0=hh[:, j, :], scalar=rs[:, c:c + 1], in1=nwb2, op0=ALU.mult, op1=ALU.mult),
                    reads=hk + [("rs", c), "nwb2"], writes=[("ost", sl)])
                out_toks.append(S.dma("sp", lambda e: e.dma_start(out=out[j * 128:(j + 1) * 128, :], in_=ostage[sl]),
                                      f"ost{sl}", reads=[("ost", sl)]))

            w2n = [0]

            def mlp_p2(g, last=False):
                s2 = w2n[0] % 2
                w2n[0] += 1
                S.dma("pool", lambda e, g=g, s2=s2: e.dma_start(out=w2r[s2], in_=w2_v[:, g * 4:(g + 1) * 4, :]),
                      f"w2r{s2}", writes=[("w2r", s2)])
                a = aT[g % 2]
                for j in range(NTM):
                    for fc in range(4):
                        b = 4 + (j * 4 + fc) % 4
                        mm_group([(bank(b), a[:, kc, j * 128:(j + 1) * 128], w2r[s2][:, kc, fc * 512:(fc + 1) * 512],
                                   kc == 0, kc == 3) for kc in range(4)],
                                 reads=[("w2r", s2)] + [("aT", g % 2, kc, j // 4) for kc in range(4)], writes=[("ps", b)])
                        S.op("dve", lambda e, j=j, fc=fc, b=b: e.tensor_tensor(
                            out=hh[:, j, fc * 512:(fc + 1) * 512], in0=bank(b), in1=hh[:, j, fc * 512:(fc + 1) * 512], op=ALU.add),
                            reads=[("ps", b), ("h", j, fc)], writes=[("h", j, fc)])
                    if last:
                        final_a(j)
                        if j >= 1:
                            final_b(j - 1)
                if last:
                    final_b(NTM - 1)

            def w2_load(g):
                s2 = w2n[0] % 2
                w2n[0] += 1
                S.dma("pool", lambda e: e.dma_start(out=w2r[s2], in_=w2_v[:, g * 4:(g + 1) * 4, :]),
                      f"w2r{s2}", writes=[("w2r", s2)])
                return s2

            def mlp_p2_last_pair(ga_, gb_, sa_, sb_):
                srcs = [(aT[ga_ % 2], w2r[sa_], ga_ % 2, sa_), (aT[gb_ % 2], w2r[sb_], gb_ % 2, sb_)]
                for j in range(NTM):
                    for fc in range(4):
                        b = 4 + (j * 4 + fc) % 4
                        mms, rd = [], []
                        for gi, (a, w, ap_, ws_) in enumerate(srcs):
                            for kc in range(4):
                                mms.append((bank(b), a[:, kc, j * 128:(j + 1) * 128], w[:, kc, fc * 512:(fc + 1) * 512],
                                            gi == 0 and kc == 0, gi == 1 and kc == 3))
                            rd += [("w2r", ws_)] + [("aT", ap_, kc, j // 4) for kc in range(4)]
                        mm_group(mms, reads=rd, writes=[("ps", b)])
                        S.op("dve", lambda e, j=j, fc=fc, b=b: e.tensor_tensor(
                            out=hh[:, j, fc * 512:(fc + 1) * 512], in0=bank(b), in1=hh[:, j, fc * 512:(fc + 1) * 512], op=ALU.add),
                            reads=[("ps", b), ("h", j, fc)], writes=[("h", j, fc)])
                    final_a(j)
                    if j >= 1:
                        final_b(j - 1)
                final_b(NTM - 1)

            S.fence([("us2", 2)], [("aT", 0, m, tb) for m in range(4) for tb in range(2)])
            mlp_p1_first()
            for g in range(NG):
                if g + 1 < NG:
                    mlp_p1(g + 1)
                if g < NG - 2:
                    mlp_p2(g)
                elif g == NG - 2:
                    s_pen = w2_load(g)
                else:
                    s_last = w2_load(g)
                    S.fence([("ring", 0), ("ring", 1)], [("ost", 0), ("ost", 1)])
                    sp_load(nwb2, nwf.partition_broadcast(128), "nwb2")
                    mlp_p2_last_pair(NG - 2, NG - 1, s_pen, s_last)
            checkpoint("p7", {"h2": (hh, H_KEYS, [128, NTM, D], F32)})
        except _Stop:
            pass
        for k_ in list(S.cnt):
            if not k_.startswith('E_'):
                S.wait('sp', (k_, S.cnt[k_]))
        S.emit()
    return nc


_CACHE = {}


def make_in_maps(x, meta_tokens, norm1_w, w_in, gate_w2, gate_b, gla_norm_w, pool_w, pool_scale,
                 w_out, norm2_w, mlp_w1, mlp_w2, final_norm_w, cores=None):
    f = lambda a: np.ascontiguousarray(np.asarray(a, dtype=np.float32))
    x = f(x); meta = f(meta_tokens)
    B = x.shape[0]
    cores = list(range(2 * B)) if cores is None else cores
    shared = {
        "w_in": f(w_in)[0], "w_out": f(w_out)[0], "w1": f(mlp_w1)[0], "w2": f(mlp_w2)[0],
        "pool_w": f(pool_w)[0],
        "gw2a": np.ascontiguousarray(np.concatenate([f(gate_w2)[0], f(gate_b)[0][None, :]], axis=0)),
        "gnw": np.ascontiguousarray(f(gla_norm_w)[0].reshape(2, 128).T),
        "psc": np.ascontiguousarray(f(pool_scale)[0].reshape(8, 128).T),
        "nw1": f(norm1_w)[0], "nw2": f(norm2_w)[0], "nwf": f(final_norm_w),
        "c_ident": np.eye(128, dtype=np.float32).astype(ml_dtypes.bfloat16),
        "c_mask4": np.ascontiguousarray(np.tile(np.triu(np.ones((128, 128), np.float32)), (1, 4))).astype(ml_dtypes.bfloat16),
    }
    in_maps = []
    for c in cores:
        b, half = divmod(c, 2)
        xin = np.zeros((TA, D), np.float32)
        fl = np.zeros((128, 2), np.float32)
        if half == 0:
            xin[112:128] = meta
            xin[128:] = x[b, 0:1024]
            fl[:, 0] = 1.0
        else:
            xin[112:128] = x[b, 1008:1024]
            xin[128:] = x[b, 1024:2048]
            fl[:, 1] = 1.0
        m = dict(shared)
        m["xin"] = xin
        m["flags"] = fl
        in_maps.append(m)
    return in_maps


def kernel(x, meta_tokens, norm1_w, w_in, gate_w2, gate_b, gla_norm_w, pool_w, pool_scale,
           w_out, norm2_w, mlp_w1, mlp_w2, final_norm_w):
    B = np.asarray(x).shape[0]
    n_cores = 2 * B
    in_maps = make_in_maps(x, meta_tokens, norm1_w, w_in, gate_w2, gate_b, gla_norm_w, pool_w, pool_scale,
                           w_out, norm2_w, mlp_w1, mlp_w2, final_norm_w)
    if "nc" not in _CACHE:
        _CACHE["nc"] = build_program()
    res = run_bass_kernel_spmd(_CACHE["nc"], in_maps, core_ids=list(range(n_cores)))
    outp = np.empty((B, 2048, D), np.float32)
    for c in range(n_cores):
        b, half = divmod(c, 2)
        outp[b, half * 1024:(half + 1) * 1024] = np.asarray(res.results[c]["out"], dtype=np.float32)
    return outp
```
